# Optimizing a Trainium2 kernel written in Bass

```python
import math
import jax
import jax.numpy as jnp
from jax import lax
import numpy as np

D_MODEL = 2048
BATCH = 2
SEQ = 4096
DEPTH = 4

CTX_LEN = 256
GRID_W = 64
N_EVEN = (DEPTH + 1) // 2
N_ODD = DEPTH // 2
N_MOD = 9
D_FF = 5632
EPS = 1e-6

RET_HEADS = 8
RET_DK = 128
RET_DV = 128
RET_W = RET_HEADS * RET_DK
RET_CHUNK = 64
ROPE_BASE = 10000.0
ROPE_PAIRS = (16, 24, 24)

SSD_HEADS = 16
SSD_HEADDIM = 64
SSD_INNER = SSD_HEADS * SSD_HEADDIM
SSD_GROUPS = 2
SSD_HPG = SSD_HEADS // SSD_GROUPS
SSD_STATE = 128
SSD_CONV = 3
SSD_CHUNK = 64
XBC_W = SSD_INNER + 2 * SSD_GROUPS * SSD_STATE

SC_WIDTH = 3

MIX_W = RET_W + SSD_INNER
IN_SPLITS = (RET_W, 2 * RET_W, 3 * RET_W, 4 * RET_W, 4 * RET_W + SSD_INNER, 4 * RET_W + SSD_INNER + XBC_W)
IN_W = IN_SPLITS[-1] + 2 * SSD_HEADS

kernel_name = 'hybrid_retention_ssd_shortconv_dit'


def rmsnorm(x, g):
    xf = x.astype(jnp.float32)
    y = xf * lax.rsqrt(jnp.mean(xf * xf, axis=-1, keepdims=True) + EPS)
    return (y * g.astype(jnp.float32)).astype(x.dtype)


def head_layernorm(o):
    o = o.astype(jnp.float32)
    d = o - jnp.mean(o, axis=-1, keepdims=True)
    return d * lax.rsqrt(jnp.mean(d * d, axis=-1, keepdims=True) + EPS)


def ada_mod(cond, w, b):
    m = jax.nn.silu(cond) @ w + b
    return m.reshape(cond.shape[0], N_MOD, D_MODEL)


def modulated_norm(x, g, mod, i):
    return rmsnorm(x, g) * (1.0 + mod[:, 3 * i + 1, None]) + mod[:, 3 * i, None]


def swiglu(h, w_up, w_down):
    gate, up = jnp.split(h @ w_up, 2, axis=-1)
    return (jax.nn.silu(gate) * up) @ w_down


def ffn_half(x, g, mod, i, w_up, w_down):
    return x + 0.5 * mod[:, 3 * i + 2, None] * swiglu(modulated_norm(x, g, mod, i), w_up, w_down)


def centred_dwconv(x, w):
    k = w.shape[0]
    p = k // 2
    n_pos = x.shape[1]
    xp = jnp.pad(x, ((0, 0), (p, p), (0, 0)))
    return sum(xp[:, i:i + n_pos] * w[i] for i in range(k))


def row_dwconv(x, w, rows):
    bn, n_pos, ch = x.shape
    return centred_dwconv(x.reshape(bn * rows, GRID_W, ch), w).reshape(bn, n_pos, ch)


def rotary_tables(pos):
    angs = []
    for axis, n in enumerate(ROPE_PAIRS):
        inv = ROPE_BASE ** (-jnp.arange(n, dtype=jnp.float32) / n)
        angs.append(pos[:, axis:axis + 1] * inv)
    ang = jnp.concatenate(angs, axis=-1)
    return jnp.cos(ang), jnp.sin(ang)


def apply_rotary(x, cos, sin):
    x1, x2 = jnp.split(x, 2, axis=-1)
    cos = cos[None, :, None, :].astype(x.dtype)
    sin = sin[None, :, None, :].astype(x.dtype)
    return jnp.concatenate([x1 * cos - x2 * sin, x1 * sin + x2 * cos], axis=-1)


def to_chunks(t, chunk):
    bn, n_pos = t.shape[:2]
    return jnp.moveaxis(t.reshape((bn, n_pos // chunk, chunk) + t.shape[2:]), 1, 0)


def from_chunks(t):
    t = jnp.moveaxis(t, 0, 1)
    return t.reshape((t.shape[0], t.shape[1] * t.shape[2]) + t.shape[3:])


def retention_scan(q, k, v, log_gamma, s0):
    cs = RET_CHUNK
    pos = jnp.arange(cs, dtype=jnp.float32)
    lg = log_gamma.astype(jnp.float32)
    diff = pos[:, None] - pos[None, :]
    intra = jnp.exp(jnp.where(diff >= 0, lg[:, None, None] * diff, -jnp.inf))
    q_dec = jnp.exp(lg[None, :] * (pos[:, None] + 1.0))
    k_dec = jnp.exp(lg[None, :] * (cs - 1.0 - pos[:, None]))
    c_dec = jnp.exp(lg * cs)

    def step(s, inp):
        qi, ki, vi = inp
        scores = jnp.einsum('bqhd,bkhd->bhqk', qi, ki) * intra
        o = jnp.einsum('bhqk,bkhv->bqhv', scores, vi)
        o = o + jnp.einsum('bqhd,bhdv->bqhv', qi * q_dec[:, :, None], s)
        s = s * c_dec[:, None, None] + jnp.einsum('bkhd,bkhv->bhdv', ki * k_dec[:, :, None], vi)
        return s, o

    seqs = tuple(to_chunks(t.astype(jnp.float32), cs) for t in (q, k, v))
    s, o = lax.scan(step, s0, seqs)
    return from_chunks(o), s


def ssd_scan(x, bm, cm, dt, a, s0):
    cs = SSD_CHUNK
    mask = jnp.tril(jnp.ones((cs, cs), dtype=bool))[None, :, :, None, None]

    def step(s, inp):
        xi, bi, ci, dti = inp
        cum = jnp.cumsum(dti * a, axis=1)
        seg = cum[:, :, None] - cum[:, None]
        decay = jnp.exp(jnp.where(mask, seg, -jnp.inf))
        cb = jnp.einsum('bqgn,bkgn->bqkg', ci, bi)
        w = cb[..., None] * decay * dti[:, None]
        y = jnp.einsum('bqkge,bkgep->bqgep', w, xi)
        y = y + jnp.einsum('bqgn,bgepn->bqgep', ci, s) * jnp.exp(cum)[..., None]
        to_end = jnp.exp(cum[:, -1:] - cum) * dti
        s = s * jnp.exp(cum[:, -1])[..., None, None] + jnp.einsum('bkgn,bkgep->bgepn', bi, xi * to_end[..., None])
        return s, y

    seqs = tuple(to_chunks(t.astype(jnp.float32), cs) for t in (x, bm, cm, dt))
    s, y = lax.scan(step, s0, seqs)
    return from_chunks(y), s


def bidir_scan(scan_fn, seqs_fwd, seqs_bwd, p_fwd, p_bwd, s0_fwd, s0_bwd):
    y_f, s_f = scan_fn(*seqs_fwd, p_fwd, s0_fwd)
    y_b, s_b = scan_fn(*[jnp.flip(t, 1) for t in seqs_bwd], p_bwd, s0_bwd)
    return y_f + jnp.flip(y_b, 1), s_f, s_b


def prep_branch(p, conv_fn, cos, sin, dt_bias):
    bn, n_pos, _ = p.shape
    q, k, v, g, z, xbc, dt = jnp.split(p, list(IN_SPLITS), axis=-1)
    q = apply_rotary(q.reshape(bn, n_pos, RET_HEADS, RET_DK), cos, sin)
    k = apply_rotary(k.reshape(bn, n_pos, RET_HEADS, RET_DK), cos, sin) * (RET_DK ** -0.5)
    v = v.reshape(bn, n_pos, RET_HEADS, RET_DV)
    xbc = jax.nn.silu(conv_fn(xbc))
    xs, bm, cm = jnp.split(xbc, [SSD_INNER, SSD_INNER + SSD_GROUPS * SSD_STATE], axis=-1)
    xs = xs.reshape(bn, n_pos, SSD_GROUPS, SSD_HPG, SSD_HEADDIM)
    bm = bm.reshape(bn, n_pos, SSD_GROUPS, SSD_STATE)
    cm = cm.reshape(bn, n_pos, SSD_GROUPS, SSD_STATE)
    dt = jax.nn.softplus(dt.reshape(bn, n_pos, 2, SSD_GROUPS, SSD_HPG).astype(jnp.float32)
                         + dt_bias.reshape(2, SSD_GROUPS, SSD_HPG).astype(jnp.float32))
    return (q, k, v), g, (xs, bm, cm, dt[:, :, 0], dt[:, :, 1]), z


def merge_heads(o_ret, g, y_ssd, xs, z, ssd_d, ret_gn_g, ssd_norm_g, w_out):
    bn, n_pos = g.shape[:2]
    o = head_layernorm(o_ret).reshape(bn, n_pos, RET_W) * ret_gn_g * jax.nn.silu(g)
    y = (y_ssd + ssd_d.reshape(SSD_GROUPS, SSD_HPG, 1) * xs).reshape(bn, n_pos, SSD_INNER)
    y = rmsnorm(y * jax.nn.silu(z), ssd_norm_g)
    return jnp.concatenate([o, y], axis=-1) @ w_out


def retention_ssd_mixer(hc, hl, rows, rot_c, rot_l, w_in, conv_w, conv_b, ret_log_decay, ret_gn_g,
                        ssd_a_log, ssd_dt_bias, ssd_d, ssd_norm_g, w_out, ctx_out):
    bn = hl.shape[0]
    a = -jnp.exp(ssd_a_log.astype(jnp.float32)).reshape(2, SSD_GROUPS, SSD_HPG)
    ret_c, g_c, ssd_c, z_c = prep_branch(hc @ w_in, lambda t: centred_dwconv(t, conv_w) + conv_b,
                                         rot_c[0], rot_c[1], ssd_dt_bias)
    ret_l, g_l, ssd_l, z_l = prep_branch(hl @ w_in, lambda t: row_dwconv(t, conv_w, rows) + conv_b,
                                         rot_l[0], rot_l[1], ssd_dt_bias)
    s_ret0 = jnp.zeros((bn, RET_HEADS, RET_DK, RET_DV), jnp.float32)
    s_ssd0 = jnp.zeros((bn, SSD_GROUPS, SSD_HPG, SSD_HEADDIM, SSD_STATE), jnp.float32)
    o_c, rs_f, rs_b = bidir_scan(retention_scan, ret_c, ret_c, ret_log_decay[0], ret_log_decay[1], s_ret0, s_ret0)
    y_c, ss_f, ss_b = bidir_scan(ssd_scan, (ssd_c[0], ssd_c[1], ssd_c[2], ssd_c[3]),
                                 (ssd_c[0], ssd_c[1], ssd_c[2], ssd_c[4]), a[0], a[1], s_ssd0, s_ssd0)
    o_l, _, _ = bidir_scan(retention_scan, ret_l, ret_l, ret_log_decay[0], ret_log_decay[1], rs_f, rs_b)
    y_l, _, _ = bidir_scan(ssd_scan, (ssd_l[0], ssd_l[1], ssd_l[2], ssd_l[3]),
                           (ssd_l[0], ssd_l[1], ssd_l[2], ssd_l[4]), a[0], a[1], ss_f, ss_b)
    out_l = merge_heads(o_l, g_l, y_l, ssd_l[0], z_l, ssd_d, ret_gn_g, ssd_norm_g, w_out)
    if not ctx_out:
        return None, out_l
    out_c = merge_heads(o_c, g_c, y_c, ssd_c[0], z_c, ssd_d, ret_gn_g, ssd_norm_g, w_out)
    return out_c, out_l


def short_conv_mixer(h, conv_fn, w_in, w_out):
    b_gate, c_gate, u = jnp.split(h @ w_in, 3, axis=-1)
    return (b_gate * conv_fn(c_gate * u)) @ w_out


def setup_inputs(seed: int = 0) -> dict:
    key = jax.random.key(seed)
    ks = jax.random.split(key, 24)
    f32 = jnp.float32

    def nrm(k, shape, fan_in, scale=1.0):
        return jax.random.normal(k, shape, f32) * (scale * fan_in ** -0.5)

    def gain(k, shape):
        return 1.0 + 0.05 * jax.random.normal(k, shape, f32)

    x = jax.random.normal(ks[0], (BATCH, SEQ, D_MODEL), f32)
    c = jax.random.normal(ks[1], (BATCH, D_MODEL), f32)
    ctx = jax.random.normal(ks[2], (BATCH, CTX_LEN, D_MODEL), f32)
    c_ctx = jax.random.normal(ks[3], (D_MODEL,), f32)
    ada_w = nrm(ks[4], (DEPTH, D_MODEL, N_MOD * D_MODEL), D_MODEL, 0.5)
    ada_b = 0.02 * jax.random.normal(ks[5], (DEPTH, N_MOD * D_MODEL), f32)
    norm_g = gain(ks[6], (DEPTH, 3, D_MODEL))
    final_g = gain(ks[7], (D_MODEL,))
    ffn_up = nrm(ks[8], (DEPTH, 2, D_MODEL, 2 * D_FF), D_MODEL)
    ffn_down = nrm(ks[9], (DEPTH, 2, D_FF, D_MODEL), D_FF)
    mix_w_in = nrm(ks[10], (N_EVEN, D_MODEL, IN_W), D_MODEL)
    mix_conv_w = nrm(ks[11], (N_EVEN, SSD_CONV, XBC_W), SSD_CONV)
    mix_conv_b = 0.02 * jax.random.normal(ks[12], (N_EVEN, XBC_W), f32)
    u = jax.random.uniform(ks[13], (N_EVEN, 2, RET_HEADS), f32, 0.0, 0.5)
    ret_log_decay = jnp.log1p(-jnp.exp2(-5.0 - jnp.arange(RET_HEADS, dtype=f32) - u))
    ret_gn_g = gain(ks[14], (N_EVEN, RET_W))
    ssd_a_log = jnp.log(jax.random.uniform(ks[15], (N_EVEN, 2, SSD_HEADS), f32, 1.0, 16.0))
    dt0 = jnp.exp(jax.random.uniform(ks[16], (N_EVEN, 2, SSD_HEADS), f32, math.log(1e-3), math.log(1e-1)))
    ssd_dt_bias = dt0 + jnp.log(-jnp.expm1(-dt0))
    ssd_d = gain(ks[17], (N_EVEN, SSD_HEADS))
    ssd_norm_g = gain(ks[18], (N_EVEN, SSD_INNER))
    mix_w_out = nrm(ks[19], (N_EVEN, MIX_W, D_MODEL), MIX_W)
    sc_w_in = nrm(ks[20], (N_ODD, D_MODEL, 3 * D_MODEL), D_MODEL)
    sc_conv_w = nrm(ks[21], (N_ODD, SC_WIDTH, D_MODEL), SC_WIDTH)
    sc_w_out = nrm(ks[22], (N_ODD, D_MODEL, D_MODEL), D_MODEL)
    return {'x': x, 'c': c, 'ctx': ctx, 'c_ctx': c_ctx, 'ada_w': ada_w, 'ada_b': ada_b,
            'norm_g': norm_g, 'final_g': final_g, 'ffn_up': ffn_up, 'ffn_down': ffn_down,
            'mix_w_in': mix_w_in, 'mix_conv_w': mix_conv_w, 'mix_conv_b': mix_conv_b,
            'ret_log_decay': ret_log_decay, 'ret_gn_g': ret_gn_g, 'ssd_a_log': ssd_a_log,
            'ssd_dt_bias': ssd_dt_bias, 'ssd_d': ssd_d, 'ssd_norm_g': ssd_norm_g,
            'mix_w_out': mix_w_out, 'sc_w_in': sc_w_in, 'sc_conv_w': sc_conv_w, 'sc_w_out': sc_w_out}


def reference(x, c, ctx, c_ctx, ada_w, ada_b, norm_g, final_g, ffn_up, ffn_down, mix_w_in, mix_conv_w,
              mix_conv_b, ret_log_decay, ret_gn_g, ssd_a_log, ssd_dt_bias, ssd_d, ssd_norm_g, mix_w_out,
              sc_w_in, sc_conv_w, sc_w_out):
    n_lat = x.shape[1]
    rows = n_lat // GRID_W
    n_ctx = ctx.shape[1]
    t = jnp.arange(n_lat, dtype=jnp.int32)
    pos_l = jnp.stack([jnp.full((n_lat,), n_ctx, jnp.int32), t // GRID_W, t % GRID_W], axis=-1).astype(jnp.float32)
    tc = jnp.arange(n_ctx, dtype=jnp.int32)
    pos_c = jnp.stack([tc, jnp.zeros_like(tc), jnp.zeros_like(tc)], axis=-1).astype(jnp.float32)
    rot_l = rotary_tables(pos_l)
    rot_c = rotary_tables(pos_c)

    last_even = DEPTH - 1 - (DEPTH - 1) % 2
    xl, xc = x, ctx
    hc, yc, mod_c = None, None, None
    for l in range(DEPTH):
        even = l % 2 == 0
        m = l // 2
        ctx_in = l <= last_even
        ctx_out = l < last_even
        g = norm_g[l]
        mod_l = ada_mod(c, ada_w[l], ada_b[l])
        xl = ffn_half(xl, g[0], mod_l, 0, ffn_up[l, 0], ffn_down[l, 0])
        hl = modulated_norm(xl, g[1], mod_l, 1)
        if ctx_in:
            mod_c = ada_mod(c_ctx[None], ada_w[l], ada_b[l])
            xc = ffn_half(xc, g[0], mod_c, 0, ffn_up[l, 0], ffn_down[l, 0])
            hc = modulated_norm(xc, g[1], mod_c, 1)
        if even:
            yc, yl = retention_ssd_mixer(hc, hl, rows, rot_c, rot_l, mix_w_in[m], mix_conv_w[m], mix_conv_b[m],
                                         ret_log_decay[m], ret_gn_g[m], ssd_a_log[m], ssd_dt_bias[m], ssd_d[m],
                                         ssd_norm_g[m], mix_w_out[m], ctx_out)
        else:
            conv_w = sc_conv_w[m]
            yl = short_conv_mixer(hl, lambda u: row_dwconv(u, conv_w, rows), sc_w_in[m], sc_w_out[m])
            if ctx_out:
                yc = short_conv_mixer(hc, lambda u: centred_dwconv(u, conv_w), sc_w_in[m], sc_w_out[m])
        xl = xl + mod_l[:, 5, None] * yl
        xl = ffn_half(xl, g[2], mod_l, 2, ffn_up[l, 1], ffn_down[l, 1])
        if ctx_out:
            xc = xc + mod_c[:, 5, None] * yc
            xc = ffn_half(xc, g[2], mod_c, 2, ffn_up[l, 1], ffn_down[l, 1])
    return rmsnorm(xl, final_g)
```

```python
import numpy as np
import concourse.bass as bass
import concourse.mybir as mybir
from concourse.bass_utils import run_bass_kernel_spmd

F32 = mybir.dt.float32
BF16 = mybir.dt.bfloat16
AF = mybir.ActivationFunctionType
ALU = mybir.AluOpType
AX = mybir.AxisListType

D = 2048
KC = D // 128
EPS = 1e-6
N_MOD = 9


class TK:
    NS = {'sp': 16, 'pool': 8, 'act': 8}

    def __init__(self, nc):
        self.nc = nc
        self.names = ['pe', 'act', 'dve', 'pool', 'sp']
        self.q = {k: [] for k in self.names}
        self.cnt = {k: 0 for k in self.names}
        self.dcnt = {k: 0 for k in self.NS}
        self.waited = {k: {} for k in self.names}
        self.lastw = {}
        self.reads = {}
        self.semh = {}
        self.engs = {'pe': nc.tensor, 'act': nc.scalar, 'dve': nc.vector, 'pool': nc.gpsimd, 'sp': nc.sync}

    def semkeys(self):
        ks = list(self.names)
        for qn, n in self.NS.items():
            ks += [(qn, i) for i in range(n)]
        return ks

    def _deps(self, eng, reads, writes):
        deps = {}

        def add(ev):
            if ev is None:
                return
            k, v = ev
            if k == 'pe' and eng == 'pe':
                return
            if deps.get(k, 0) < v:
                deps[k] = v
        for r in reads:
            add(self.lastw.get(r))
        for w in writes:
            add(self.lastw.get(w))
            for k, v in self.reads.get(w, {}).items():
                add((k, v))
        out = []
        wd = self.waited[eng]
        for k, v in deps.items():
            if wd.get(k, 0) < v:
                wd[k] = v
                out.append((k, v))
        return out

    def _record(self, ev, reads, writes):
        for w in writes:
            self.lastw[w] = ev
            self.reads[w] = {}
        for r in reads:
            if r in writes:
                continue
            d = self.reads.setdefault(r, {})
            if d.get(ev[0], 0) < ev[1]:
                d[ev[0]] = ev[1]

    def op(self, eng, fn, reads=(), writes=()):
        waits = self._deps(eng, reads, writes)
        self.cnt[eng] += 1
        ev = (eng, self.cnt[eng])
        self._record(ev, reads, writes)
        semh = self.semh

        e = self.engs[eng]
        for k, v in waits:
            e.wait_ge(semh[k], v)
        fn(e).then_inc(semh[eng], 1)
        return ev

    def dma(self, qn, out, in_, reads=(), writes=(), **kw):
        waits = self._deps(qn, reads, writes)
        i = self.dcnt[qn]
        self.dcnt[qn] += 1
        ns = self.NS[qn]
        key = (qn, i % ns)
        val = 16 * (i // ns + 1)
        if i >= ns and self.waited[qn].get(key, 0) < val - 16:
            self.waited[qn][key] = val - 16
            waits.append((key, val - 16))
        ev = (key, val)
        self._record(ev, reads, writes)
        semh = self.semh

        e = self.engs[qn]
        for k, v in waits:
            e.wait_ge(semh[k], v)
        e.dma_start(out=out, in_=in_, **kw).then_inc(semh[key], 16)
        return ev

    def wait_all(self, eng):
        evs = {}
        for ev in self.lastw.values():
            if evs.get(ev[0], 0) < ev[1]:
                evs[ev[0]] = ev[1]
        semh = self.semh
        lst = list(evs.items())

        e = self.engs[eng]
        wd = self.waited[eng]
        for k, v in lst:
            if wd.get(k, 0) < v:
                wd[k] = v
                e.wait_ge(semh[k], v)

    def barrier(self):
        for eng in self.names:
            self.wait_all(eng)


RET_H = 8
SSD_H = 16
XBC_W = 1536
BIG = 30000.0
import contextlib
import math


def build(cfg):
    DEPTH, SEQ, CTX, DFF = cfg['depth'], cfg['seq'], cfg['ctx'], cfg['dff']
    do_mix = cfg.get('mix', True)
    debug = cfg.get('debug', False)
    T = SEQ + CTX
    NT = T // 128
    NCT = CTX // 128
    NH = DFF // 128
    NHA = max(NH, 32)
    NDB = D // 256
    N_ODD = max(DEPTH // 2, 1)
    N_EVEN = (DEPTH + 1) // 2
    nc = bass.Bass("TRN2", target_bir_lowering=False)
    tk = TK(nc)

    def dram_in(name, shape, dt=F32):
        return nc.dram_tensor(name, list(shape), dt, kind="ExternalInput").ap()

    def dram(name, shape, dt=F32, dbg=False):
        if dbg and debug:
            return nc.dram_tensor(name, list(shape), dt, kind="ExternalOutput").ap()
        return nc.dram_tensor(name, list(shape), dt).ap()

    xin = dram_in("xin", [T, D])
    cT = dram_in("cT", [128, KC, 2])
    ada_w = dram_in("ada_w", [DEPTH, D, N_MOD * D])
    ada_b = dram_in("ada_b", [DEPTH, N_MOD * D])
    norm_g = dram_in("norm_g", [DEPTH, 3, D])
    final_g = dram_in("final_g", [1, D])
    wup_in = dram_in("wup", [DEPTH, 2, NH, 128, KC * 256])
    wdn_in = dram_in("wdn", [DEPTH, 2, NDB, 128, NH * 256])
    ident_in = dram_in("ident_in", [128, 128])
    iota_in = dram_in("iota_in", [128, 128])
    scw_cu_in = dram_in("scw_cu", [N_ODD, KC, 128, KC * 256])
    scw_b_in = dram_in("scw_b", [N_ODD, KC, 128, KC * 128])
    scwo_in = dram_in("scwo", [N_ODD, NDB, 128, KC * 256])
    sccw_in = dram_in("sccw", [N_ODD, 128, KC, 3])
    wfm_in = dram_in("wfm", [N_EVEN, 14, 128, KC * 256])
    wtm_in = dram_in("wtm", [N_EVEN, 6, 128, KC * 512])
    wdt_in = dram_in("wdt", [N_EVEN, 128, KC * 32])
    mwo_in = dram_in("mwo", [N_EVEN, NDB, 128, KC * 256])
    mcw_in = dram_in("mcw", [N_EVEN, 128, 12, 3])
    mcb_in = dram_in("mcb", [N_EVEN, 128, 12])
    rld_in = dram_in("rld", [N_EVEN, 1, 16])
    gng_in = dram_in("gng", [N_EVEN, 1, 1024])
    alog_in = dram_in("alog", [N_EVEN, 1, 32])
    dtb_in = dram_in("dtb", [N_EVEN, 1, 32])
    ssdd_in = dram_in("ssdd", [N_EVEN, 1, 16])
    sng_in = dram_in("sng", [N_EVEN, 1, 1024])
    pax_in = dram_in("pax", [128, T])
    inv_in = dram_in("inv", [128, 1])
    out = nc.dram_tensor("out", [SEQ, D], F32, kind="ExternalOutput").ap()

    X = dram("X", [T, D])
    MOD = dram("MOD", [2, DEPTH * N_MOD * D])
    wup_b = [[dram(f"wup_b{l}_{i}", [NH, 128, KC * 256], BF16) for i in range(2)] for l in range(DEPTH)]
    wdn_b = [[dram(f"wdn_b{l}_{i}", [NDB, 128, NH * 256], BF16) for i in range(2)] for l in range(DEPTH)]
    scw_cu_b = dram("scw_cu_b", [N_ODD, KC, 128, KC * 256], BF16)
    scw_b_b = dram("scw_b_b", [N_ODD, KC, 128, KC * 128], BF16)
    scwo_b = dram("scwo_b", [N_ODD, NDB, 128, KC * 256], BF16)
    wfm_b = dram("wfm_b", [N_EVEN, 14, 128, KC * 256], BF16)
    wtm_b = dram("wtm_b", [N_EVEN, 6, 128, KC * 512], BF16)
    wdt_b = dram("wdt_b", [N_EVEN, 128, KC * 32], BF16)
    mwo_b = dram("mwo_b", [N_EVEN, NDB, 128, KC * 256], BF16)
    ROPE = dram("ROPE", [4, 128, T], dbg=True)
    QT = dram("QT", [RET_H, 128, T], dbg=True)
    KT = dram("KT", [RET_H, 128, T], dbg=True)
    Vd = dram("Vd", [T, 1024], dbg=True)
    SGd = dram("SGd", [T, 1024], dbg=True)
    SZd = dram("SZd", [T, 1024], dbg=True)
    XSd = dram("XSd", [T, 1024], dbg=True)
    BTOK = dram("BTOK", [T, 256], dbg=True)
    BTd = dram("BTd", [2, 128, T], dbg=True)
    CTd = dram("CTd", [2, 128, T], dbg=True)
    DTd = dram("DTd", [T, 32], dbg=True)
    DTAd = dram("DTAd", [T, 32], dbg=True)
    SBst = dram("SBst", [NT, 128, RET_H * 128])
    HBst = dram("HBst", [NT, 128, 1024])
    MTd = dram("MTd", [KC, 128, T], BF16)
    MIXd = dram("MIXd", [T, D], dbg=True)

    uid = [0]
    B = {}

    def mk_scope():
        es = contextlib.ExitStack()

        def sb(name, shape, dt=F32):
            uid[0] += 1
            t = es.enter_context(nc.sbuf_tensor(f"{name}_{uid[0]}", list(shape), dt))
            B[name] = t
            return t
        return es, sb

    TB = 4
    NTOK = TB * 128
    with contextlib.ExitStack() as es0:
        for k in tk.semkeys():
            nm = k if isinstance(k, str) else f"{k[0]}{k[1]}"
            tk.semh[k] = es0.enter_context(nc.semaphore("s_" + nm))

        def sb0(name, shape, dt=F32):
            return es0.enter_context(nc.sbuf_tensor(name, list(shape), dt))

        ident = sb0("ident", [128, 128], BF16)
        identf = sb0("identf", [128, 128], F32)
        GS = sb0("GS", [128, D])
        SH = sb0("SH", [128, D])
        GM = sb0("GM", [128, D])
        SCR = sb0("SCR", [128, D])
        HB = sb0("HB", [128, D], BF16)
        ST = sb0("ST", [128, 8])
        CW = sb0("CW", [128, KC, 3])
        cs = sb0("cs", [128, KC, 2])
        csg = sb0("csg", [128, KC, 2])
        ADB = sb0("ADB", [2, 512])
        ADO = sb0("ADO", [2, 512])
        IOT = sb0("IOT", [128, 128])
        PS = [es0.enter_context(nc.psum_tensor(f"ps{i}", [128, 512], F32)) for i in range(8)]
        PST = [PS[6][:].bitcast(BF16), PS[7][:].bitcast(BF16)]

        def scope_a():
            es, sb = mk_scope()
            sb("XT", [128, TB, D])
            sb("HT", [128, KC, NTOK], BF16)
            sb("AT", [128, NHA, NTOK], BF16)
            for i in range(2):
                sb(f"WUP{i}", [128, KC, 256], BF16)
                sb(f"WDN{i}", [128, NHA, 256], BF16)
                sb(f"SG{i}", [128, NTOK])
            return es

        def end_scope(es):
            tk.barrier()
            es.close()

        tk.dma('sp', identf[:], ident_in, writes=['identf'])
        tk.op('dve', lambda e: e.tensor_copy(out=ident[:], in_=identf[:]), reads=['identf'], writes=['ident'])
        tk.dma('sp', IOT[:], iota_in, writes=['IOT'])
        tk.dma('sp', X, xin, writes=['X'])
        cast_jobs = []
        GU = 11

        def add_ffn_casts(l, i):
            for h0 in range(0, NH, GU):
                h1 = min(NH, h0 + GU)
                cast_jobs.append((('wupb', l, i, h0), wup_b[l][i][h0:h1], wup_in[l, i, h0:h1]))
            for d0 in range(0, NDB, 2):
                cast_jobs.append((('wdnb', l, i, d0), wdn_b[l][i][d0:d0 + 2], wdn_in[l, i, d0:d0 + 2]))
        for l in range(DEPTH):
            add_ffn_casts(l, 0)
            m = l // 2
            if l % 2 == 1:
                cast_jobs.append((('scw', m, 0), scw_cu_b[m], scw_cu_in[m]))
                cast_jobs.append((('scw', m, 1), scw_b_b[m], scw_b_in[m]))
                cast_jobs.append((('scw', m, 2), scwo_b[m], scwo_in[m]))
            elif do_mix is True:
                cast_jobs.append((('mw', m, 0), wfm_b[m, 0:7], wfm_in[m, 0:7]))
                cast_jobs.append((('mw', m, 1), wfm_b[m, 7:14], wfm_in[m, 7:14]))
                cast_jobs.append((('mw', m, 2), wtm_b[m, 0:3], wtm_in[m, 0:3]))
                cast_jobs.append((('mw', m, 3), wtm_b[m, 3:6], wtm_in[m, 3:6]))
                cast_jobs.append((('mw', m, 4), wdt_b[m], wdt_in[m]))
                cast_jobs.append((('mw', m, 5), mwo_b[m], mwo_in[m]))
            add_ffn_casts(l, 1)
        cast_pos = [0]
        cast_done = set()

        def cast_tick(n):
            for _ in range(n):
                if cast_pos[0] < len(cast_jobs):
                    key, o_, i_ = cast_jobs[cast_pos[0]]
                    cast_pos[0] += 1
                    tk.dma('pool', o_, i_, writes=[key])
                    cast_done.add(key)

        def need(keys):
            for k in keys:
                while k not in cast_done:
                    cast_tick(1)
            return list(keys)

        def wdeps(kind, l, i, idx):
            if kind == 'up':
                return need([('wupb', l, i, (idx // GU) * GU)])
            return need([('wdnb', l, i, (idx // 2) * 2)])

        cast_tick(8)

        tk.dma('sp', cs[:], cT, writes=['cs'])
        tk.op('act', lambda e: e.activation(out=csg[:], in_=cs[:], func=AF.Silu), reads=['cs'], writes=['csg'])
        ncb = N_MOD * D // 512

        def ada_gen(layers, AW, pi):
            it = 0
            pn = f'ps{pi}'
            for l in layers:
                for cb in range(ncb):
                    tk.dma('act', ADB[:], ada_b[l:l + 1, cb * 512:(cb + 1) * 512].partition_broadcast(2), writes=['ADB'])
                    for kq in range(4):
                        buf, bname = AW[it % 2]
                        it += 1
                        bv = buf.rearrange("p (k c) -> p k c", k=4)
                        tk.dma('sp', bv, ada_w[l, kq * 512:(kq + 1) * 512, cb * 512:(cb + 1) * 512].rearrange(
                            "(k p) c -> p k c", p=128), writes=[bname])
                        for k in range(4):
                            kc = kq * 4 + k
                            tk.op('pe', lambda e, bv=bv, k=k, kc=kc: e.matmul(
                                PS[pi][0:2, :], lhsT=csg[:, kc, :], rhs=bv[:, k, :], start=(kc == 0), stop=(kc == KC - 1)),
                                reads=['csg', bname], writes=[pn])
                    tk.op('dve', lambda e: e.tensor_tensor(out=ADO[:], in0=PS[pi][0:2, :], in1=ADB[:], op=ALU.add),
                          reads=[pn, 'ADB'], writes=['ADO'])
                    off = l * N_MOD * D + cb * 512
                    tk.dma('act', MOD[:, off:off + 512], ADO[:], reads=['ADO'], writes=['MOD'])
                    yield

        defer_ada = (do_mix is True)
        for _ in ada_gen([0] if defer_ada else list(range(DEPTH)), [(SCR[:], 'SCR'), (GM[:], 'GM')], 0):
            pass

        def modrow(l, j, row):
            off = (l * N_MOD + j) * D
            return MOD[row:row + 1, off:off + D].partition_broadcast(128)

        def load_consts(l, gi, mi, row, gate_scale):
            tk.dma('act', SCR[:], norm_g[l, gi:gi + 1, :].partition_broadcast(128), writes=['SCR'])
            tk.dma('act', GS[:], modrow(l, 3 * mi + 1, row), reads=['MOD'], writes=['GS'])
            tk.dma('act', SH[:], modrow(l, 3 * mi, row), reads=['MOD'], writes=['SH'])
            tk.dma('act', GM[:], modrow(l, 3 * mi + 2, row), reads=['MOD'], writes=['GM'])
            tk.op('dve', lambda e: e.scalar_tensor_tensor(out=GS[:], in0=GS[:], scalar=1.0, in1=SCR[:],
                                                         op0=ALU.add, op1=ALU.mult),
                  reads=['GS', 'SCR'], writes=['GS'])
            if gate_scale != 1.0:
                tk.op('pool', lambda e: e.tensor_scalar(out=GM[:], in0=GM[:], scalar1=gate_scale, scalar2=None,
                                                       op0=ALU.mult), reads=['GM'], writes=['GM'])

        def rstd_from_sumsq(n):
            tk.op('dve', lambda e: e.tensor_scalar(out=ST[:, 1:2], in0=ST[:, 0:1], scalar1=1.0 / n, scalar2=EPS,
                                                  op0=ALU.mult, op1=ALU.add), reads=['ST'], writes=['ST'])
            tk.op('act', lambda e: e.activation(out=ST[:, 3:4], in_=ST[:, 1:2], func=AF.Sqrt), reads=['ST'], writes=['ST'])
            tk.op('dve', lambda e: e.reciprocal(out=ST[:, 2:3], in_=ST[:, 3:4]), reads=['ST'], writes=['ST'])

        def load_x(j, t0):
            tk.dma('sp', B['XT'][:, j, :], X[t0:t0 + 128, :], reads=['X'], writes=[('XT', j)])

        def norm_tile(j, t0, modulate=True):
            XT = B['XT']
            xj = ('XT', j)
            load_x(j, t0)
            tk.op('pool', lambda e: e.memset(ST[:, 0:1], 0.0), writes=['ST'])
            tk.op('act', lambda e: e.activation(out=SCR[:], in_=XT[:, j, :], func=AF.Square, accum_out=ST[:, 0:1]),
                  reads=[xj, 'ST'], writes=['SCR', 'ST'])
            rstd_from_sumsq(D)
            tk.op('dve', lambda e: e.scalar_tensor_tensor(out=SCR[:], in0=XT[:, j, :], scalar=ST[:, 2:3], in1=GS[:],
                                                         op0=ALU.mult, op1=ALU.mult),
                  reads=[xj, 'ST', 'GS'], writes=['SCR'])
            if modulate:
                tk.op('dve', lambda e: e.tensor_tensor(out=HB[:], in0=SCR[:], in1=SH[:], op=ALU.add),
                      reads=['SCR', 'SH'], writes=['HB'])

        def transpose_hb(dst, j, dname):
            for k4 in range(KC // 4):
                pi = k4 % 2
                pname = f'ps{6 + pi}'
                for k in range(4):
                    kc = k4 * 4 + k
                    tk.op('pe', lambda e, kc=kc, k=k, pi=pi: e.transpose(
                        out=PST[pi][:, k * 128:(k + 1) * 128], in_=HB[:, kc * 128:(kc + 1) * 128], identity=ident[:]),
                        reads=['HB', 'ident'], writes=[pname])
                tk.op('act', lambda e, k4=k4, pi=pi: e.copy(
                    out=dst[:, k4 * 4:(k4 + 1) * 4, j * 128:(j + 1) * 128],
                    in_=PST[pi][:, 0:512].rearrange("p (k t) -> p k t", k=4)),
                    reads=[pname], writes=[dname])

        def norm_block(t0, nt):
            for j in range(nt):
                norm_tile(j, t0 + j * 128)
                transpose_hb(B['HT'], j, 'HT')

        wctr = {'up': 0, 'dn': 0}

        def wk(base, s):
            return [f'{base}{s}', f'{base}{s}a', f'{base}{s}b']

        def load_wup(src, deps):
            s = (wctr['up'] // 2) % 2
            wctr['up'] += 2
            tk.dma('sp', B[f'WUP{s}'][:].rearrange("p k c -> p (k c)"), src, reads=need(deps), writes=wk('WUP', s))
            return B[f'WUP{s}'], wk('WUP', s)

        def fm_mm(pi, W, wn, c0, ntok, m=128):
            HT = B['HT']
            for kc in range(KC):
                tk.op('pe', lambda e, kc=kc: e.matmul(
                    PS[pi][0:m, 0:ntok], lhsT=W[:, kc, c0:c0 + m], rhs=HT[:, kc, 0:ntok], start=(kc == 0), stop=(kc == KC - 1)),
                    reads=(wn if isinstance(wn, list) else [wn]) + ['HT'], writes=[f'ps{pi}'])

        def up_ffn(l, ii, ntok):
            AT, HT = B['AT'], B['HT']
            HK = KC // 2
            for hc in range(NH):
                a = 2 * (hc % 2)
                deps = wdeps('up', l, ii, hc)
                for half in range(2):
                    r = wctr['up'] % 4
                    wctr['up'] += 1
                    sl, part = r // 2, r % 2
                    key = f"WUP{sl}{'ab'[part]}"
                    W = B[f'WUP{sl}'][:, part * HK:(part + 1) * HK, :]
                    tk.dma('sp', W, wup_b[l][ii][hc][:, half * HK * 256:(half + 1) * HK * 256].rearrange("p (k c) -> p k c", c=256),
                           reads=deps, writes=[key])
                    for (pi, c0) in ((a, 0), (a + 1, 128)):
                        for k in range(HK):
                            kc = half * HK + k
                            tk.op('pe', lambda e, k=k, kc=kc, pi=pi, c0=c0, W=W: e.matmul(
                                PS[pi][:, 0:ntok], lhsT=W[:, k, c0:c0 + 128], rhs=HT[:, kc, 0:ntok], start=(kc == 0), stop=(kc == KC - 1)),
                                reads=[key, 'HT'], writes=[f'ps{pi}'])
                sg, sgn = B[f'SG{hc % 2}'], f'SG{hc % 2}'
                tk.op('act', lambda e, sg=sg, a=a: e.activation(out=sg[:, 0:ntok], in_=PS[a][:, 0:ntok], func=AF.Silu),
                      reads=[f'ps{a}'], writes=[sgn])
                tk.op('dve', lambda e, sg=sg, a=a, hc=hc: e.tensor_tensor(
                    out=AT[:, hc, 0:ntok], in0=sg[:, 0:ntok], in1=PS[a + 1][:, 0:ntok], op=ALU.mult),
                    reads=[sgn, f'ps{a + 1}'], writes=['AT'])

        def conv3(R, Tt, ntok, rowlen, cw, ci, cwn):
            R3 = R.rearrange("p (r w) -> p r w", w=rowlen)
            T3 = Tt.rearrange("p (r w) -> p r w", w=rowlen)
            tk.op('dve', lambda e: e.tensor_scalar(out=R, in0=Tt, scalar1=cw[:, ci, 1:2], scalar2=None, op0=ALU.mult),
                  reads=['SCR', cwn], writes=['SCR'])
            tk.op('dve', lambda e: e.scalar_tensor_tensor(
                out=R3[:, :, 1:rowlen], in0=T3[:, :, 0:rowlen - 1], scalar=cw[:, ci, 0:1], in1=R3[:, :, 1:rowlen],
                op0=ALU.mult, op1=ALU.add), reads=['SCR', cwn], writes=['SCR'])
            tk.op('dve', lambda e: e.scalar_tensor_tensor(
                out=R3[:, :, 0:rowlen - 1], in0=T3[:, :, 1:rowlen], scalar=cw[:, ci, 2:3], in1=R3[:, :, 0:rowlen - 1],
                op0=ALU.mult, op1=ALU.add), reads=['SCR', cwn], writes=['SCR'])

        def up_sconv(m, ntok, rowlen):
            AT = B['AT']
            R = SCR[:, 0:ntok]
            Tt = SCR[:, 512:512 + ntok]
            for fc in range(KC):
                tk.dma('sp', B['WUP0'][:].rearrange("p k c -> p (k c)"), scw_cu_b[m, fc], reads=need([('scw', m, 0)]), writes=wk('WUP', 0))
                tk.dma('sp', B['WUP1'][:, :, 0:128], scw_b_b[m, fc].rearrange("p (k c) -> p k c", c=128),
                       reads=need([('scw', m, 1)]), writes=wk('WUP', 1))
                fm_mm(0, B['WUP0'], wk('WUP', 0), 0, ntok)
                fm_mm(1, B['WUP0'], wk('WUP', 0), 128, ntok)
                fm_mm(2, B['WUP1'], wk('WUP', 1), 0, ntok)
                sg = B['SG0']
                tk.op('act', lambda e: e.copy(out=sg[:, 0:ntok], in_=PS[0][:, 0:ntok]), reads=['ps0'], writes=['SG0'])
                tk.op('dve', lambda e: e.tensor_tensor(out=Tt, in0=sg[:, 0:ntok], in1=PS[1][:, 0:ntok], op=ALU.mult),
                      reads=['SG0', 'ps1'], writes=['SCR'])
                conv3(R, Tt, ntok, rowlen, CW, fc, 'CW')
                tk.op('dve', lambda e, fc=fc: e.tensor_tensor(out=AT[:, fc, 0:ntok], in0=R, in1=PS[2][:, 0:ntok], op=ALU.mult),
                      reads=['SCR', 'ps2'], writes=['AT'])

        def down_proj(wsrc, nh, depfn, nt):
            AT, XT = B['AT'], B['XT']
            hn = nh // 2
            for db in range(NDB):
                deps = depfn(db)
                pbase = 4 * (db % 2)
                for half in range(2):
                    r = wctr['dn'] % 4
                    wctr['dn'] += 1
                    sl, part = r // 2, r % 2
                    key = f"WDN{sl}{'ab'[part]}"
                    W = B[f'WDN{sl}'][:, part * (NHA // 2):part * (NHA // 2) + hn, :]
                    tk.dma('sp', W, wsrc(db)[:, half * hn * 256:(half + 1) * hn * 256].rearrange("p (h c) -> p h c", c=256),
                           reads=deps, writes=[key])
                    for j in range(nt):
                        pb = pbase + j
                        for h in range(hn):
                            hc = half * hn + h
                            tk.op('pe', lambda e, W=W, h=h, hc=hc, j=j, pb=pb: e.matmul(
                                PS[pb][:, 0:256], lhsT=AT[:, hc, j * 128:(j + 1) * 128], rhs=W[:, h, :],
                                start=(hc == 0), stop=(hc == nh - 1)),
                                reads=[key, 'AT'], writes=[f'ps{pb}'])
                for j in range(nt):
                    pb = pbase + j
                    pn = f'ps{pb}'
                    xj = ('XT', j)
                    sl_ = slice(db * 256, (db + 1) * 256)
                    tk.op('dve', lambda e, pb=pb, sl_=sl_: e.tensor_tensor(
                        out=SCR[:, sl_], in0=PS[pb][:, 0:256], in1=GM[:, sl_], op=ALU.mult),
                        reads=[pn, 'GM'], writes=['SCR'])
                    tk.op('pool', lambda e, j=j, sl_=sl_: e.tensor_tensor(
                        out=XT[:, j, sl_], in0=XT[:, j, sl_], in1=SCR[:, sl_], op=ALU.add),
                        reads=['SCR', xj], writes=[xj])

        def store_block(t0, nt):
            for j in range(nt):
                tk.dma('pool', X[t0 + j * 128:t0 + (j + 1) * 128, :], B['XT'][:, j, :], reads=[('XT', j)], writes=['X'])

        def blocks(with_ctx):
            bl = []
            if with_ctx:
                for t in range(0, NCT, TB):
                    bl.append((t * 128, min(TB, NCT - t), 1))
            for t in range(NCT, NT, TB):
                bl.append((t * 128, min(TB, NT - t), 0))
            return bl

        def ffn_phase(l, i, with_ctx):
            mi = 0 if i == 0 else 2
            ii = 0 if i == 0 else 1
            currow = None
            for (t0, nt, row) in blocks(with_ctx):
                if row != currow:
                    load_consts(l, mi, mi, row, 0.5)
                    currow = row
                norm_block(t0, nt)
                up_ffn(l, ii, nt * 128)
                down_proj(lambda db: wdn_b[l][ii][db], NH, lambda db: wdeps('dn', l, ii, db), nt)
                cast_tick(3)
                store_block(t0, nt)

        def sconv_phase(l, with_ctx):
            m = l // 2
            tk.dma('act', CW[:], sccw_in[m], writes=['CW'])
            currow = None
            for (t0, nt, row) in blocks(with_ctx):
                if row != currow:
                    load_consts(l, 1, 1, row, 1.0)
                    currow = row
                norm_block(t0, nt)
                up_sconv(m, nt * 128, 64 if row == 0 else nt * 128)
                down_proj(lambda db: scwo_b[m, db], KC, lambda db: need([('scw', m, 2)]), nt)
                cast_tick(3)
                store_block(t0, nt)

        def rope_tables():
            es, sb = mk_scope()
            PAX = sb("PAX", [128, 512])
            ANG = sb("ANG", [128, 512])
            TMP = sb("TMP", [128, 512])
            RES = sb("RES", [128, 4, 512])
            INV = sb("INV", [128, 1])
            YI = sb("YI", [128, 512], mybir.dt.int32)
            YF = sb("YF", [128, 512])
            tk.dma('sp', INV[:], inv_in, writes=['INV'])
            for c0 in range(0, T, 512):
                n = min(512, T - c0)
                tk.dma('sp', PAX[:, 0:n], pax_in[:, c0:c0 + n], writes=['PAX'])
                tk.op('dve', lambda e: e.tensor_scalar(out=ANG[:, 0:n], in0=PAX[:, 0:n], scalar1=INV[:, 0:1], scalar2=None,
                                                      op0=ALU.mult), reads=['PAX', 'INV'], writes=['ANG'])
                for (ri, shift) in ((0, 0.25), (1, 0.0)):
                    tk.op('dve', lambda e, shift=shift: e.tensor_scalar(
                        out=TMP[:, 0:n], in0=ANG[:, 0:n], scalar1=1.0 / (2 * math.pi), scalar2=shift, op0=ALU.mult, op1=ALU.add),
                        reads=['ANG'], writes=['TMP'])
                    tk.op('dve', lambda e: e.tensor_copy(out=YI[:, 0:n], in_=TMP[:, 0:n]), reads=['TMP'], writes=['YI'])
                    tk.op('dve', lambda e: e.tensor_copy(out=YF[:, 0:n], in_=YI[:, 0:n]), reads=['YI'], writes=['YF'])
                    tk.op('dve', lambda e: e.tensor_tensor(out=TMP[:, 0:n], in0=TMP[:, 0:n], in1=YF[:, 0:n], op=ALU.subtract),
                          reads=['TMP', 'YF'], writes=['TMP'])
                    tk.op('dve', lambda e: e.tensor_scalar(out=YF[:, 0:n], in0=TMP[:, 0:n], scalar1=0.5, scalar2=None, op0=ALU.is_gt),
                          reads=['TMP'], writes=['YF'])
                    tk.op('dve', lambda e: e.tensor_tensor(out=TMP[:, 0:n], in0=TMP[:, 0:n], in1=YF[:, 0:n], op=ALU.subtract),
                          reads=['TMP', 'YF'], writes=['TMP'])
                    tk.op('act', lambda e, ri=ri: e.activation(out=RES[:, ri, 0:n], in_=TMP[:, 0:n], func=AF.Sin,
                                                               scale=2 * math.pi - 1e-5),
                          reads=['TMP'], writes=['RES'])
                    tk.op('pool', lambda e, ri=ri: e.tensor_scalar(
                        out=RES[:, 2 + ri, 0:n], in0=RES[:, ri, 0:n], scalar1=128 ** -0.5, scalar2=None, op0=ALU.mult),
                        reads=['RES'], writes=['RES'])
                tk.dma('pool', ROPE[:, :, c0:c0 + n].rearrange("r p t -> p r t"), RES[:, :, 0:n], reads=['RES'], writes=['ROPE'])
            end_scope(es)

        def e1_phase(l):
            m = l // 2
            MCW = B['MCW']
            for (t0, nt, row) in blocks(True):
                ntok = nt * 128
                rowlen = 64 if row == 0 else ntok
                load_consts(l, 1, 1, row, 1.0)
                norm_block(t0, nt)
                WS = B['AT'][:].rearrange("p h t -> p (h t)").bitcast(F32)
                tabs = [WS[:, i * 512:i * 512 + ntok] for i in range(4)]
                for i in range(4):
                    tk.dma('sp', tabs[i], ROPE[i, :, t0:t0 + ntok], reads=['ROPE'], writes=['AT'])
                SG0, SG1 = B['SG0'], B['SG1']
                for slot in range(8):
                    W, wn = load_wup(wfm_b[m, slot], [('mw', m, 0), ('mw', m, 1)])
                    fm_mm(0, W, wn, 0, ntok)
                    fm_mm(1, W, wn, 128, ntok)
                    cos, sin = (tabs[0], tabs[1]) if slot < 4 else (tabs[2], tabs[3])
                    for (oi, ta, tb, op) in ((0, cos, sin, ALU.subtract), (1, sin, cos, ALU.add)):
                        tk.op('dve', lambda e, ta=ta: e.tensor_tensor(out=SG0[:, 0:ntok], in0=PS[0][:, 0:ntok], in1=ta, op=ALU.mult),
                              reads=['ps0', 'AT'], writes=['SG0'])
                        tk.op('dve', lambda e, tb=tb: e.tensor_tensor(out=SG1[:, 0:ntok], in0=PS[1][:, 0:ntok], in1=tb, op=ALU.mult),
                              reads=['ps1', 'AT'], writes=['SG1'])
                        tk.op('pool', lambda e, oi=oi, op=op: e.tensor_tensor(
                            out=SCR[:, oi * 512:oi * 512 + ntok], in0=SG0[:, 0:ntok], in1=SG1[:, 0:ntok], op=op),
                            reads=['SG0', 'SG1'], writes=['SCR'])
                    dst = QT if slot < 4 else KT
                    h0 = 2 * (slot % 4)
                    for hh in range(2):
                        for half in range(2):
                            tk.dma('act', dst[h0 + hh, half * 64:(half + 1) * 64, t0:t0 + ntok],
                                   SCR[hh * 64:(hh + 1) * 64, half * 512:half * 512 + ntok], reads=['SCR'], writes=['QK'])
                for slot in range(6):
                    W, wn = load_wup(wfm_b[m, 8 + slot], [('mw', m, 1)])
                    for half in range(2):
                        ch = 2 * slot + half
                        fm_mm(half, W, wn, half * 128, ntok)
                        Tt = SCR[:, 512:512 + ntok]
                        R = SCR[:, 0:ntok]
                        A = SCR[:, 1024:1024 + ntok]
                        tk.op('act', lambda e, half=half: e.copy(out=Tt, in_=PS[half][:, 0:ntok]), reads=[f'ps{half}'], writes=['SCR'])
                        conv3(R, Tt, ntok, rowlen, MCW, ch, 'MCW')
                        tk.op('act', lambda e, ch=ch: e.activation(out=A, in_=R, func=AF.Silu, bias=B['MCB'][:, ch:ch + 1], scale=1.0),
                              reads=['SCR', 'MCB'], writes=['SCR'])
                        if ch >= 8:
                            g = (ch - 8) % 2
                            dst = BTd if ch < 10 else CTd
                            tk.dma('act', dst[g, :, t0:t0 + ntok], A, reads=['SCR'], writes=['BC'])
                        if ch < 10:
                            for j in range(nt):
                                tk.op('pe', lambda e, j=j: e.transpose(out=PS[2][:, j * 128:(j + 1) * 128], in_=A[:, j * 128:(j + 1) * 128],
                                                                       identity=identf[:]), reads=['SCR', 'identf'], writes=['ps2'])
                            tk.op('act', lambda e: e.copy(out=SG0[:, 0:ntok], in_=PS[2][:, 0:ntok]), reads=['ps2'], writes=['SG0'])
                            for j in range(nt):
                                r0 = t0 + j * 128
                                if ch < 8:
                                    tk.dma('act', XSd[r0:r0 + 128, ch * 128:(ch + 1) * 128], SG0[:, j * 128:(j + 1) * 128], reads=['SG0'], writes=['XS'])
                                else:
                                    tk.dma('act', BTOK[r0:r0 + 128, (ch - 8) * 128:(ch - 7) * 128], SG0[:, j * 128:(j + 1) * 128], reads=['SG0'], writes=['XS'])
                HT = B['HT']
                for blk in range(6):
                    s = (wctr['dn'] // 2) % 2
                    wctr['dn'] += 2
                    wn = f'WDN{s}'
                    W = B[wn][:].rearrange("p h c -> p (h c)")[:, 0:KC * 512].rearrange("p (k c) -> p k c", c=512)
                    tk.dma('sp', W, wtm_b[m, blk].rearrange("p (k c) -> p k c", c=512), reads=need([('mw', m, 2), ('mw', m, 3)]), writes=wk('WDN', s))
                    dst = (Vd, SGd, SZd)[blk // 2]
                    for j in range(nt):
                        pb = 4 + j % 2
                        for kc in range(KC):
                            tk.op('pe', lambda e, kc=kc, j=j, pb=pb, W=W: e.matmul(
                                PS[pb][:, :], lhsT=HT[:, kc, j * 128:(j + 1) * 128], rhs=W[:, kc, :], start=(kc == 0), stop=(kc == KC - 1)),
                                reads=wk('WDN', s) + ['HT'], writes=[f'ps{pb}'])
                        sg, sgn = B[f'SG{j % 2}'], f'SG{j % 2}'
                        if blk < 2:
                            tk.op('act', lambda e, sg=sg, pb=pb: e.copy(out=sg[:], in_=PS[pb][:, :]), reads=[f'ps{pb}'], writes=[sgn])
                        else:
                            tk.op('act', lambda e, sg=sg, pb=pb: e.activation(out=sg[:], in_=PS[pb][:, :], func=AF.Silu), reads=[f'ps{pb}'], writes=[sgn])
                        r0 = t0 + j * 128
                        tk.dma('pool', dst[r0:r0 + 128, (blk % 2) * 512:(blk % 2 + 1) * 512], sg[:], reads=[sgn], writes=['TM'])
                W = B['WUP0'][:].rearrange("p k c -> p (k c)")[:, 0:KC * 32].rearrange("p (k c) -> p k c", c=32)
                tk.dma('sp', W, wdt_b[m].rearrange("p (k c) -> p k c", c=32), reads=need([('mw', m, 4)]), writes=wk('WUP', 0))
                for j in range(nt):
                    for kc in range(KC):
                        tk.op('pe', lambda e, kc=kc, j=j: e.matmul(
                            PS[3][:, 0:32], lhsT=HT[:, kc, j * 128:(j + 1) * 128], rhs=W[:, kc, :], start=(kc == 0), stop=(kc == KC - 1)),
                            reads=wk('WUP', 0) + ['HT'], writes=['ps3'])
                    DTt = B['DTT']
                    tk.op('dve', lambda e: e.tensor_tensor(out=DTt[:, 0:32], in0=PS[3][:, 0:32], in1=B['DTB'][:], op=ALU.add),
                          reads=['ps3', 'DTB'], writes=['DTT'])
                    tk.op('act', lambda e: e.activation(out=DTt[:, 32:64], in_=DTt[:, 0:32], func=AF.Exp), reads=['DTT'], writes=['DTT'])
                    tk.op('act', lambda e: e.activation(out=DTt[:, 64:96], in_=DTt[:, 32:64], func=AF.Ln, bias=B['ONE1'][:, 0:1], scale=1.0),
                          reads=['DTT', 'ONE1'], writes=['DTT'])
                    tk.op('dve', lambda e: e.tensor_tensor(out=DTt[:, 96:128], in0=DTt[:, 64:96], in1=B['A32'][:], op=ALU.mult),
                          reads=['DTT', 'A32'], writes=['DTT'])
                    r0 = t0 + j * 128
                    tk.dma('pool', DTd[r0:r0 + 128, :], DTt[:, 64:96], reads=['DTT'], writes=['DT'])
                    tk.dma('pool', DTAd[r0:r0 + 128, :], DTt[:, 96:128], reads=['DTT'], writes=['DT'])

        def mixer_consts(m, sb):
            MCW = sb("MCW", [128, 12, 3])
            MCB = sb("MCB", [128, 12])
            DTB = sb("DTB", [128, 32])
            A32 = sb("A32", [128, 32])
            ONE1 = sb("ONE1", [128, 1])
            DTT = sb("DTT", [128, 128])
            tk.dma('act', MCW[:], mcw_in[m], writes=['MCW'])
            tk.dma('act', MCB[:], mcb_in[m], writes=['MCB'])
            tk.dma('act', DTB[:], dtb_in[m].partition_broadcast(128), writes=['DTB'])
            tk.dma('act', A32[:], alog_in[m].partition_broadcast(128), writes=['A32'])
            tk.op('act', lambda e: e.activation(out=A32[:], in_=A32[:], func=AF.Exp), reads=['A32'], writes=['A32'])
            tk.op('dve', lambda e: e.tensor_scalar(out=A32[:], in0=A32[:], scalar1=-1.0, scalar2=None, op0=ALU.mult),
                  reads=['A32'], writes=['A32'])
            tk.op('pool', lambda e: e.memset(ONE1[:], 1.0), writes=['ONE1'])

        def scan_phases(l):
            m = l // 2
            es, sb = mk_scope()
            TRIF = sb("TRIF", [128, 128]); TRIB = sb("TRIB", [128, 128])
            MF = sb("MF", [128, 128]); MB = sb("MB", [128, 128]); ONES = sb("ONES", [128, 128])
            LG = sb("LG", [128, 16]); CD = sb("CD", [128, 16]); KD = sb("KD", [128, 16])
            DC = sb("DC", [128, RET_H, 128]); DQF = sb("DQF", [128, RET_H, 128]); DQB = sb("DQB", [128, RET_H, 128])
            GNG = sb("GNG", [128, 1024]); SNG = sb("SNG", [128, 1024]); DSK = sb("DSK", [128, 16])
            TMPA = sb("TMPA", [128, 128]); TMPB = sb("TMPB", [128, 128]); PCOL = sb("PCOL", [128, 2])
            ONE1 = sb("ONE1", [128, 1])
            tk.op('pool', lambda e: e.memset(ONE1[:], 1.0), writes=['ONE1'])
            tk.op('pool', lambda e: e.memset(ONES[:], 1.0), writes=['ONES'])
            tk.op('dve', lambda e: e.tensor_scalar(out=TRIF[:], in0=IOT[:], scalar1=0.0, scalar2=None, op0=ALU.is_ge), reads=['IOT'], writes=['TRIF'])
            tk.op('dve', lambda e: e.tensor_scalar(out=TRIB[:], in0=IOT[:], scalar1=0.0, scalar2=None, op0=ALU.is_le), reads=['IOT'], writes=['TRIB'])
            tk.op('dve', lambda e: e.tensor_scalar(out=MF[:], in0=IOT[:], scalar1=0.0, scalar2=-BIG, op0=ALU.is_lt, op1=ALU.mult), reads=['IOT'], writes=['MF'])
            tk.op('dve', lambda e: e.tensor_scalar(out=MB[:], in0=IOT[:], scalar1=0.0, scalar2=-BIG, op0=ALU.is_gt, op1=ALU.mult), reads=['IOT'], writes=['MB'])
            tk.dma('act', LG[:], rld_in[m].partition_broadcast(128), writes=['LG'])
            tk.dma('act', GNG[:], gng_in[m].partition_broadcast(128), writes=['GNG'])
            tk.dma('act', SNG[:], sng_in[m].partition_broadcast(128), writes=['SNG'])
            tk.dma('act', DSK[:], ssdd_in[m].partition_broadcast(128), writes=['DSK'])
            tk.op('act', lambda e: e.activation(out=CD[:], in_=LG[:], func=AF.Exp, scale=128.0), reads=['LG'], writes=['CD'])
            tk.op('dve', lambda e: e.tensor_copy(out=PCOL[:, 0:1], in_=IOT[:, 127:128]), reads=['IOT'], writes=['PCOL'])
            tk.op('dve', lambda e: e.tensor_scalar(out=PCOL[:, 1:2], in0=IOT[:, 0:1], scalar1=-1.0, scalar2=None, op0=ALU.mult), reads=['IOT'], writes=['PCOL'])
            tk.op('dve', lambda e: e.tensor_scalar(out=KD[:, 0:8], in0=LG[:, 0:8], scalar1=PCOL[:, 0:1], scalar2=None, op0=ALU.mult), reads=['LG', 'PCOL'], writes=['KD'])
            tk.op('dve', lambda e: e.tensor_scalar(out=KD[:, 8:16], in0=LG[:, 8:16], scalar1=PCOL[:, 1:2], scalar2=None, op0=ALU.mult), reads=['LG', 'PCOL'], writes=['KD'])
            tk.op('act', lambda e: e.activation(out=KD[:], in_=KD[:], func=AF.Exp), reads=['KD'], writes=['KD'])
            TPOS = sb("TPOS", [128, 128])
            tk.op('dve', lambda e: e.tensor_scalar(out=TPOS[:], in0=IOT[:], scalar1=PCOL[:, 1:2], scalar2=None, op0=ALU.add), reads=['IOT', 'PCOL'], writes=['TPOS'])
            for h in range(RET_H):
                tk.op('dve', lambda e, h=h: e.scalar_tensor_tensor(out=TMPA[:], in0=IOT[:], scalar=LG[:, h:h + 1], in1=MF[:], op0=ALU.mult, op1=ALU.add),
                      reads=['IOT', 'LG', 'MF'], writes=['TMPA'])
                tk.op('act', lambda e: e.activation(out=TMPA[:], in_=TMPA[:], func=AF.Exp), reads=['TMPA'], writes=['TMPA'])
                tk.op('dve', lambda e, h=h: e.tensor_scalar(out=TMPB[:], in0=IOT[:], scalar1=LG[:, 8 + h:9 + h], scalar2=-1.0, op0=ALU.mult, op1=ALU.mult),
                      reads=['IOT', 'LG'], writes=['TMPB'])
                tk.op('dve', lambda e: e.tensor_tensor(out=TMPB[:], in0=TMPB[:], in1=MB[:], op=ALU.add), reads=['TMPB', 'MB'], writes=['TMPB'])
                tk.op('act', lambda e: e.activation(out=TMPB[:], in_=TMPB[:], func=AF.Exp), reads=['TMPB'], writes=['TMPB'])
                tk.op('dve', lambda e, h=h: e.tensor_tensor(out=DC[:, h, :], in0=TMPA[:], in1=TMPB[:], op=ALU.add), reads=['TMPA', 'TMPB'], writes=['DC'])
                tk.op('dve', lambda e, h=h: e.tensor_scalar(out=TMPA[:], in0=TPOS[:], scalar1=1.0, scalar2=LG[:, h:h + 1], op0=ALU.add, op1=ALU.mult),
                      reads=['TPOS', 'LG'], writes=['TMPA'])
                tk.op('act', lambda e, h=h: e.activation(out=DQF[:, h, :], in_=TMPA[:], func=AF.Exp), reads=['TMPA'], writes=['DQF'])
                tk.op('dve', lambda e, h=h: e.tensor_scalar(out=TMPB[:], in0=TPOS[:], scalar1=-128.0, scalar2=LG[:, 8 + h:9 + h], op0=ALU.add, op1=ALU.mult),
                      reads=['TPOS', 'LG'], writes=['TMPB'])
                tk.op('act', lambda e, h=h: e.activation(out=DQB[:, h, :], in_=TMPB[:], func=AF.Exp, scale=-1.0), reads=['TMPB'], writes=['DQB'])
            CH = []
            for par in range(2):
                d = {}
                for nm, shp in (("KTc", [128, RET_H, 128]), ("QTc", [128, RET_H, 128]), ("Vc", [128, 1024]), ("XSc", [128, 1024]),
                                ("SGc", [128, 1024]), ("SZc", [128, 1024]), ("BTc", [128, 2, 128]), ("CTc", [128, 2, 128]),
                                ("BKc", [128, 256]), ("DTc", [128, 32]), ("DTAc", [128, 32]),
                                ("SBi", [128, RET_H, 128]), ("HBi", [128, 2, 512])):
                    d[nm] = sb(f"{nm}{par}", shp)
                d['par'] = par
                CH.append(d)
            SF = sb("SF", [128, RET_H, 128]); HF = sb("HF", [128, 2, 512])
            KXs = [sb(f"KX{i}", [128, 128]) for i in range(2)]; PTs = [sb(f"PT{i}", [128, 128]) for i in range(2)]
            QFs = [sb(f"QF{i}", [128, 128]) for i in range(2)]; QBs = [sb(f"QB{i}", [128, 128]) for i in range(2)]
            ada_it = ada_gen([x for x in (l + 1, l + 2) if x < DEPTH] if defer_ada else [], [(GS[:], 'GS'), (SH[:], 'SH')], 3)

            def ada_tick(n=1):
                for _ in range(n):
                    next(ada_it, None)
            O8 = sb("O8", [128, RET_H, 128]); D8 = sb("D8", [128, RET_H, 128]); S8 = sb("S8", [128, 32])
            CUM = sb("CUM", [128, 32]); TOT = sb("TOT", [128, 32]); ECUM = sb("ECUM", [128, 32]); ETOT = sb("ETOT", [128, 32]); WG = sb("WG", [128, 32])
            ZH = [GM[:].rearrange("p (a t) -> p a t", t=128), SCR[:].rearrange("p (a t) -> p a t", t=128)]
            ZK = ['GM', 'SCR']
            CBs = sb("CBs", [128, 128])
            EFs = [sb(f"EF{i}", [128, 128]) for i in range(2)]; EBs = [sb(f"EB{i}", [128, 128]) for i in range(2)]
            WTs = [sb(f"WT{i}", [128, 128]) for i in range(2)]
            XWs = [sb(f"XW{i}", [128, 512]) for i in range(2)]
            YG = sb("YG", [128, 1024]); YT = sb("YT", [128, 512])
            MB16 = sb("MB16", [128, D], BF16); MTc = sb("MTc", [128, KC, 128], BF16)

            def bc8(ap8):
                return ap8.unsqueeze(2).to_broadcast([128, 8, 64])

            def interleave(gens, width):
                gens = iter(gens)
                active = []
                while True:
                    while len(active) < width:
                        g = next(gens, None)
                        if g is None:
                            break
                        active.append(g)
                    if not active:
                        break
                    for g in list(active):
                        try:
                            next(g)
                        except StopIteration:
                            active.remove(g)

            def k_(C, nm):
                return f"{nm}{C['par']}"

            def load_chunk(C, c, full, sbsrc=None):
                r0 = c * 128
                tk.dma('sp', C['KTc'][:], KT[:, :, r0:r0 + 128].rearrange("h d t -> d h t"), reads=['QK'], writes=[k_(C, 'KTc')])
                tk.dma('sp', C['Vc'][:], Vd[r0:r0 + 128, :], reads=['TM'], writes=[k_(C, 'Vc')])
                tk.dma('sp', C['XSc'][:], XSd[r0:r0 + 128, :], reads=['XS'], writes=[k_(C, 'XSc')])
                tk.dma('sp', C['BKc'][:], BTOK[r0:r0 + 128, :], reads=['XS'], writes=[k_(C, 'BKc')])
                tk.dma('sp', C['DTc'][:], DTd[r0:r0 + 128, :], reads=['DT'], writes=[k_(C, 'DTc')])
                tk.dma('sp', C['DTAc'][:], DTAd[r0:r0 + 128, :], reads=['DT'], writes=[k_(C, 'DTAc')])
                if full:
                    tk.dma('sp', C['QTc'][:], QT[:, :, r0:r0 + 128].rearrange("h d t -> d h t"), reads=['QK'], writes=[k_(C, 'QTc')])
                    tk.dma('sp', C['SGc'][:], SGd[r0:r0 + 128, :], reads=['TM'], writes=[k_(C, 'SGc')])
                    tk.dma('sp', C['SZc'][:], SZd[r0:r0 + 128, :], reads=['TM'], writes=[k_(C, 'SZc')])
                    tk.dma('sp', C['BTc'][:], BTd[:, :, r0:r0 + 128].rearrange("g n t -> n g t"), reads=['BC'], writes=[k_(C, 'BTc')])
                    tk.dma('sp', C['CTc'][:], CTd[:, :, r0:r0 + 128].rearrange("g n t -> n g t"), reads=['BC'], writes=[k_(C, 'CTc')])
                    tk.dma('sp', C['SBi'][:].rearrange("p h d -> p (h d)"), SBst[c], reads=['SBst'], writes=[(k_(C, 'SBi'), h) for h in range(RET_H)])
                    tk.dma('sp', C['HBi'][:].rearrange("p g d -> p (g d)"), HBst[c], reads=['HBst'], writes=[k_(C, 'HBi')])

            def ret_state_update(C, S, sname, h, kdcol, cdcol):
                par = h % 2
                pa, pb = (0, 2) if par == 0 else (4, 7)
                KX, kxn = KXs[par], f'KX{par}'
                KTc, Vc = C['KTc'], C['Vc']
                tk.op('pe', lambda e: e.transpose(out=PS[pa][:, 0:128], in_=KTc[:, h, :], identity=identf[:]), reads=[k_(C, 'KTc'), 'identf'], writes=[f'ps{pa}'])
                yield
                tk.op('dve', lambda e: e.tensor_scalar(out=KX[:], in0=PS[pa][:, 0:128], scalar1=KD[:, kdcol:kdcol + 1], scalar2=None, op0=ALU.mult),
                      reads=[f'ps{pa}', 'KD'], writes=[kxn])
                yield
                tk.op('pe', lambda e: e.matmul(PS[pb][:, 0:128], lhsT=KX[:], rhs=Vc[:, h * 128:(h + 1) * 128], start=True, stop=True),
                      reads=[kxn, k_(C, 'Vc')], writes=[f'ps{pb}'])
                yield
                tk.op('dve', lambda e: e.scalar_tensor_tensor(out=S[:, h, :], in0=S[:, h, :], scalar=CD[:, cdcol:cdcol + 1], in1=PS[pb][:, 0:128],
                                                             op0=ALU.mult, op1=ALU.add), reads=[(sname, h), 'CD', f'ps{pb}'], writes=[(sname, h)])
                yield

            def ssd_cols(C, dirs):
                DTAc, DTc = C['DTAc'], C['DTc']
                for d in dirs:
                    tri = TRIF if d == 0 else TRIB
                    tk.op('pe', lambda e, d=d, tri=tri: e.matmul(PS[3][:, d * 16:(d + 1) * 16], lhsT=tri[:], rhs=DTAc[:, d * 16:(d + 1) * 16], start=True, stop=True),
                          reads=['TRIF', 'TRIB', k_(C, 'DTAc')], writes=['ps3'])
                    tk.op('pe', lambda e, d=d: e.matmul(PS[3][:, 32 + d * 16:32 + (d + 1) * 16], lhsT=ONES[:], rhs=DTAc[:, d * 16:(d + 1) * 16], start=True, stop=True),
                          reads=['ONES', k_(C, 'DTAc')], writes=['ps3'])
                lo, hi = min(dirs) * 16, (max(dirs) + 1) * 16
                tk.op('act', lambda e: e.copy(out=CUM[:, lo:hi], in_=PS[3][:, lo:hi]), reads=['ps3'], writes=['CUM'])
                tk.op('act', lambda e: e.copy(out=TOT[:, lo:hi], in_=PS[3][:, 32 + lo:32 + hi]), reads=['ps3'], writes=['TOT'])
                tk.op('act', lambda e: e.activation(out=ECUM[:, lo:hi], in_=CUM[:, lo:hi], func=AF.Exp), reads=['CUM'], writes=['ECUM'])
                tk.op('act', lambda e: e.activation(out=ETOT[:, lo:hi], in_=TOT[:, lo:hi], func=AF.Exp), reads=['TOT'], writes=['ETOT'])
                tk.op('dve', lambda e: e.tensor_tensor(out=WG[:, lo:hi], in0=TOT[:, lo:hi], in1=CUM[:, lo:hi], op=ALU.subtract), reads=['TOT', 'CUM'], writes=['WG'])
                tk.op('act', lambda e: e.activation(out=WG[:, lo:hi], in_=WG[:, lo:hi], func=AF.Exp), reads=['WG'], writes=['WG'])
                tk.op('dve', lambda e: e.tensor_tensor(out=WG[:, lo:hi], in0=WG[:, lo:hi], in1=DTc[:, lo:hi], op=ALU.mult), reads=['WG', k_(C, 'DTc')], writes=['WG'])

            def ssd_state_update(C, H, hname, g, d):
                col = d * 16 + g * 8
                XW, xwn = XWs[g], f'XW{g}'
                pb = 5 if g == 0 else 6
                XSc, BKc = C['XSc'], C['BKc']
                xs3 = XSc[:, g * 512:(g + 1) * 512].rearrange("p (e q) -> p e q", q=64)
                xw3 = XW[:].rearrange("p (e q) -> p e q", q=64)
                tk.op('dve', lambda e: e.tensor_tensor(out=xw3, in0=xs3, in1=bc8(WG[:, col:col + 8]), op=ALU.mult), reads=[k_(C, 'XSc'), 'WG'], writes=[xwn])
                yield
                tk.op('pe', lambda e: e.matmul(PS[pb][:, :], lhsT=BKc[:, g * 128:(g + 1) * 128], rhs=XW[:], start=True, stop=True),
                      reads=[k_(C, 'BKc'), xwn], writes=[f'ps{pb}'])
                yield
                h3 = H[:, g, :].rearrange("p (e q) -> p e q", q=64)
                tk.op('dve', lambda e: e.tensor_tensor(out=h3, in0=h3, in1=bc8(ETOT[:, col:col + 8]), op=ALU.mult), reads=[(hname, g), 'ETOT'], writes=[(hname, g)])
                yield
                tk.op('dve', lambda e: e.tensor_tensor(out=H[:, g, :], in0=H[:, g, :], in1=PS[pb][:, :], op=ALU.add), reads=[(hname, g), f'ps{pb}'], writes=[(hname, g)])
                yield

            if stop == 'sc_const':
                end_scope(es)
                return
            SBs = sb("SBs", [128, RET_H, 128]); HBs = sb("HBs", [128, 2, 512])
            tk.op('pool', lambda e: e.memset(SBs[:], 0.0), writes=[('SBs', h) for h in range(RET_H)])
            tk.op('pool', lambda e: e.memset(HBs[:], 0.0), writes=[('HBs', g) for g in range(2)])
            tk.op('pool', lambda e: e.memset(SF[:], 0.0), writes=[('SF', h) for h in range(RET_H)])
            tk.op('pool', lambda e: e.memset(HF[:], 0.0), writes=[('HF', g) for g in range(2)])
            bchain = list(range(NCT - 1, -1, -1)) + list(range(NT - 1, NCT - 1, -1))
            load_chunk(CH[0], bchain[0], False)
            for ci, c in enumerate(bchain):
                C = CH[ci % 2]
                if ci + 1 < len(bchain):
                    load_chunk(CH[(ci + 1) % 2], bchain[ci + 1], False)
                tk.dma('pool', SBst[c], SBs[:].rearrange("p h d -> p (h d)"), reads=[('SBs', h) for h in range(RET_H)], writes=['SBst'])
                tk.dma('pool', HBst[c], HBs[:].rearrange("p g d -> p (g d)"), reads=[('HBs', g) for g in range(2)], writes=['HBst'])
                ada_tick()
                ssd_cols(C, [1])
                interleave([ssd_state_update(C, HBs, 'HBs', g, 1) for g in range(2)], 2)
                interleave([ret_state_update(C, SBs, 'SBs', h, 8 + h, 8 + h) for h in range(RET_H)], 2)
            if stop == 'e2':
                end_scope(es)
                return

            def ret_head(C, h):
                par = h % 2
                pa, pb = (0, 1) if par == 0 else (4, 5)
                PT, QF, QB = PTs[par], QFs[par], QBs[par]
                ptn, qfn, qbn = f'PT{par}', f'QF{par}', f'QB{par}'
                KTc, QTc, Vc, SBin = C['KTc'], C['QTc'], C['Vc'], C['SBi']
                tk.op('pe', lambda e: e.matmul(PS[pa][:, 0:128], lhsT=KTc[:, h, :], rhs=QTc[:, h, :], start=True, stop=True),
                      reads=[k_(C, 'KTc'), k_(C, 'QTc')], writes=[f'ps{pa}'])
                tk.op('pool', lambda e: e.tensor_tensor(out=QF[:], in0=QTc[:, h, :], in1=DQF[:, h, :], op=ALU.mult), reads=[k_(C, 'QTc'), 'DQF'], writes=[qfn])
                yield
                tk.op('dve', lambda e: e.tensor_tensor(out=PT[:], in0=PS[pa][:, 0:128], in1=DC[:, h, :], op=ALU.mult), reads=[f'ps{pa}', 'DC'], writes=[ptn])
                tk.op('pool', lambda e: e.tensor_tensor(out=QB[:], in0=QTc[:, h, :], in1=DQB[:, h, :], op=ALU.mult), reads=[k_(C, 'QTc'), 'DQB'], writes=[qbn])
                yield
                tk.op('pe', lambda e: e.matmul(PS[pb][:, 0:128], lhsT=PT[:], rhs=Vc[:, h * 128:(h + 1) * 128], start=True, stop=False),
                      reads=[ptn, k_(C, 'Vc')], writes=[f'ps{pb}'])
                tk.op('pe', lambda e: e.matmul(PS[pb][:, 0:128], lhsT=QF[:], rhs=SF[:, h, :], start=False, stop=False),
                      reads=[qfn, ('SF', h)], writes=[f'ps{pb}'])
                tk.op('pe', lambda e: e.matmul(PS[pb][:, 0:128], lhsT=QB[:], rhs=SBin[:, h, :], start=False, stop=True),
                      reads=[qbn, (k_(C, 'SBi'), h)], writes=[f'ps{pb}'])
                yield
                tk.op('act', lambda e: e.copy(out=O8[:, h, :], in_=PS[pb][:, 0:128]), reads=[f'ps{pb}'], writes=[('O8', h)])
                yield
                yield from ret_state_update(C, SF, 'SF', h, h, h)

            def ssd_head(C, g, e_):
                idx = g * 8 + e_
                par = idx % 2
                o = par * 256
                EF, EB, WT = EFs[par], EBs[par], WTs[par]
                efn, ebn, wtn = f'EF{par}', f'EB{par}', f'WT{par}'
                DTc, XSc = C['DTc'], C['XSc']
                if par == 0:
                    q = idx // 2
                    zq = ZH[q // 4][:, 4 * (q % 4):4 * (q % 4) + 4, :]
                    tk.op('pe', lambda e: e.matmul(PS[4][:, :], lhsT=ONES[:], rhs=zq.rearrange("p a t -> p (a t)"),
                                                   start=True, stop=True), reads=['ONES', ZK[q // 4]], writes=['ps4'])
                yield
                tk.op('dve', lambda e: e.scalar_tensor_tensor(out=EF[:], in0=PS[4][:, o:o + 128], scalar=CUM[:, idx:idx + 1], in1=MF[:],
                                                             op0=ALU.subtract, op1=ALU.add), reads=['ps4', 'CUM', 'MF'], writes=[efn])
                tk.op('dve', lambda e: e.scalar_tensor_tensor(out=EB[:], in0=PS[4][:, o + 128:o + 256], scalar=CUM[:, 16 + idx:17 + idx], in1=MB[:],
                                                             op0=ALU.subtract, op1=ALU.add), reads=['ps4', 'CUM', 'MB'], writes=[ebn])
                yield
                tk.op('act', lambda e: e.activation(out=EF[:], in_=EF[:], func=AF.Exp), reads=[efn], writes=[efn])
                tk.op('act', lambda e: e.activation(out=EB[:], in_=EB[:], func=AF.Exp), reads=[ebn], writes=[ebn])
                yield
                tk.op('pool', lambda e: e.tensor_scalar(out=EF[:], in0=EF[:], scalar1=DTc[:, idx:idx + 1], scalar2=None, op0=ALU.mult),
                      reads=[efn, k_(C, 'DTc')], writes=[efn])
                yield
                tk.op('dve', lambda e: e.scalar_tensor_tensor(out=EB[:], in0=EB[:], scalar=DTc[:, 16 + idx:17 + idx], in1=EF[:],
                                                             op0=ALU.mult, op1=ALU.add), reads=[ebn, k_(C, 'DTc'), efn], writes=[ebn])
                yield
                tk.op('pool', lambda e: e.tensor_tensor(out=WT[:], in0=EB[:], in1=CBs[:], op=ALU.mult), reads=[ebn, 'CBs'], writes=[wtn])
                yield
                tk.op('pe', lambda e: e.matmul(PS[6][:, e_ * 64:(e_ + 1) * 64], lhsT=WT[:], rhs=XSc[:, idx * 64:(idx + 1) * 64], start=True, stop=True),
                      reads=[wtn, k_(C, 'XSc')], writes=['ps6'])
                yield

            load_chunk(CH[0], 0, True)
            for c in range(NT):
                r0 = c * 128
                C = CH[c % 2]
                if c + 1 < NT:
                    load_chunk(CH[(c + 1) % 2], c + 1, True)
                ada_tick()
                KTc, QTc, Vc, XSc, SGc, SZc, BTc, CTc, DTc, DTAc, HBin = (C[n] for n in ('KTc', 'QTc', 'Vc', 'XSc', 'SGc', 'SZc', 'BTc', 'CTc', 'DTc', 'DTAc', 'HBi'))
                ssd_cols(C, [0, 1])
                for zh in range(2):
                    z4 = ZH[zh].rearrange("p (i d) t -> p i d t", d=2)
                    for d in range(2):
                        tri = TRIF if d == 0 else TRIB
                        c0 = d * 16 + zh * 8
                        tk.op('dve', lambda e, d=d, tri=tri, z4=z4, c0=c0: e.tensor_tensor(
                            out=z4[:, :, d, :], in0=tri[:].unsqueeze(1).to_broadcast([128, 8, 128]),
                            in1=DTAc[:, c0:c0 + 8].unsqueeze(2).to_broadcast([128, 8, 128]), op=ALU.mult),
                            reads=['TRIF', 'TRIB', k_(C, 'DTAc')], writes=[ZK[zh]])
                interleave([ret_head(C, h) for h in range(RET_H)], 2)
                o8k = [('O8', h) for h in range(RET_H)]
                tk.op('dve', lambda e: e.reduce_sum(out=S8[:, 0:8], in_=O8[:], axis=AX.X), reads=o8k, writes=['S8'])
                tk.op('dve', lambda e: e.tensor_scalar(out=S8[:, 0:8], in0=S8[:, 0:8], scalar1=1.0 / 128, scalar2=None, op0=ALU.mult), reads=['S8'], writes=['S8'])
                tk.op('dve', lambda e: e.tensor_tensor(out=D8[:], in0=O8[:], in1=S8[:, 0:8].unsqueeze(2).to_broadcast([128, 8, 128]), op=ALU.subtract),
                      reads=o8k + ['S8'], writes=['D8'])
                tk.op('pool', lambda e: e.tensor_tensor(out=O8[:], in0=D8[:], in1=D8[:], op=ALU.mult), reads=['D8'], writes=o8k)
                tk.op('dve', lambda e: e.reduce_sum(out=S8[:, 8:16], in_=O8[:], axis=AX.X), reads=o8k, writes=['S8'])
                tk.op('dve', lambda e: e.tensor_scalar(out=S8[:, 8:16], in0=S8[:, 8:16], scalar1=1.0 / 128, scalar2=EPS, op0=ALU.mult, op1=ALU.add), reads=['S8'], writes=['S8'])
                tk.op('act', lambda e: e.activation(out=S8[:, 16:24], in_=S8[:, 8:16], func=AF.Sqrt), reads=['S8'], writes=['S8'])
                tk.op('dve', lambda e: e.reciprocal(out=S8[:, 24:32], in_=S8[:, 16:24]), reads=['S8'], writes=['S8'])
                tk.op('dve', lambda e: e.tensor_tensor(out=D8[:], in0=D8[:], in1=S8[:, 24:32].unsqueeze(2).to_broadcast([128, 8, 128]), op=ALU.mult),
                      reads=['D8', 'S8'], writes=['D8'])
                d8f = D8[:].rearrange("p h d -> p (h d)")
                tk.op('pool', lambda e: e.tensor_tensor(out=d8f, in0=d8f, in1=GNG[:], op=ALU.mult), reads=['D8', 'GNG'], writes=['D8'])
                tk.op('dve', lambda e: e.tensor_tensor(out=MB16[:, 0:1024], in0=d8f, in1=SGc[:], op=ALU.mult), reads=['D8', k_(C, 'SGc')], writes=[('MB16', 0)])
                if stop == 'e3ret':
                    continue
                for g in range(2):
                    tk.op('pe', lambda e, g=g: e.matmul(PS[5][:, 0:128], lhsT=BTc[:, g, :], rhs=CTc[:, g, :], start=True, stop=True),
                          reads=[k_(C, 'BTc'), k_(C, 'CTc')], writes=['ps5'])
                    tk.op('act', lambda e: e.copy(out=CBs[:], in_=PS[5][:, 0:128]), reads=['ps5'], writes=['CBs'])
                    interleave([ssd_head(C, g, e_) for e_ in range(8)], 2)
                    yg = YG[:, g * 512:(g + 1) * 512]
                    yt3 = YT[:].rearrange("p (e q) -> p e q", q=64)
                    tk.op('act', lambda e, yg=yg: e.copy(out=yg, in_=PS[6][:, :]), reads=['ps6'], writes=[('YG', g)])
                    for d, H, hn in ((0, HF, ('HF', g)), (1, HBin, k_(C, 'HBi'))):
                        col = d * 16 + g * 8
                        tk.op('pe', lambda e, g=g, H=H: e.matmul(PS[7][:, :], lhsT=CTc[:, g, :], rhs=H[:, g, :], start=True, stop=True),
                              reads=[k_(C, 'CTc'), hn], writes=['ps7'])
                        tk.op('dve', lambda e, col=col: e.tensor_tensor(out=yt3, in0=PS[7][:, :].rearrange("p (e q) -> p e q", q=64), in1=bc8(ECUM[:, col:col + 8]), op=ALU.mult),
                              reads=['ps7', 'ECUM'], writes=['YT'])
                        tk.op('pool', lambda e, yg=yg: e.tensor_tensor(out=yg, in0=yg, in1=YT[:], op=ALU.add), reads=[('YG', g), 'YT'], writes=[('YG', g)])
                    for _ in ssd_state_update(C, HF, 'HF', g, 0):
                        pass
                ygk = [('YG', 0), ('YG', 1)]
                xs3a = XSc[:].rearrange("p (i q) -> p i q", q=64)
                tk.op('dve', lambda e: e.tensor_tensor(out=xs3a, in0=xs3a, in1=DSK[:].unsqueeze(2).to_broadcast([128, 16, 64]), op=ALU.mult),
                      reads=[k_(C, 'XSc'), 'DSK'], writes=[k_(C, 'XSc')])
                tk.op('pool', lambda e: e.tensor_tensor(out=YG[:], in0=YG[:], in1=XSc[:], op=ALU.add), reads=ygk + [k_(C, 'XSc')], writes=ygk)
                tk.op('dve', lambda e: e.tensor_tensor(out=YG[:], in0=YG[:], in1=SZc[:], op=ALU.mult), reads=ygk + [k_(C, 'SZc')], writes=ygk)
                tk.op('pool', lambda e: e.memset(ST[:, 0:1], 0.0), writes=['ST'])
                tk.op('act', lambda e: e.activation(out=XSc[:], in_=YG[:], func=AF.Square, accum_out=ST[:, 0:1]), reads=ygk + ['ST'], writes=[k_(C, 'XSc'), 'ST'])
                rstd_from_sumsq(1024)
                tk.op('dve', lambda e: e.scalar_tensor_tensor(out=MB16[:, 1024:2048], in0=YG[:], scalar=ST[:, 2:3], in1=SNG[:], op0=ALU.mult, op1=ALU.mult),
                      reads=ygk + ['ST', 'SNG'], writes=[('MB16', 1)])
                if debug:
                    tk.op('dve', lambda e: e.tensor_copy(out=SCR[:], in_=MB16[:]), reads=[('MB16', 0), ('MB16', 1)], writes=['SCR'])
                    tk.dma('pool', MIXd[r0:r0 + 128, :], SCR[:], reads=['SCR'], writes=['MIXd'])
                for k4 in range(KC // 4):
                    pi = k4 % 2
                    pname = f'ps{6 + pi}'
                    for k in range(4):
                        kc = k4 * 4 + k
                        tk.op('pe', lambda e, kc=kc, k=k, pi=pi: e.transpose(
                            out=PST[pi][:, k * 128:(k + 1) * 128], in_=MB16[:, kc * 128:(kc + 1) * 128], identity=ident[:]),
                            reads=[('MB16', kc // 8), 'ident'], writes=[pname])
                    tk.op('act', lambda e, k4=k4, pi=pi: e.copy(
                        out=MTc[:, k4 * 4:(k4 + 1) * 4, :], in_=PST[pi][:, 0:512].rearrange("p (k t) -> p k t", k=4)),
                        reads=[pname], writes=['MTc'])
                tk.dma('pool', MTd[:, :, r0:r0 + 128].rearrange("k p t -> p k t"), MTc[:], reads=['MTc'], writes=['MTd'])
            for _ in ada_it:
                pass
            end_scope(es)

        def e4_phase(l, with_ctx):
            m = l // 2
            currow = None
            for (t0, nt, row) in blocks(with_ctx):
                ntok = nt * 128
                if row != currow:
                    load_consts(l, 1, 1, row, 1.0)
                    currow = row
                for j in range(nt):
                    load_x(j, t0 + j * 128)
                tk.dma('sp', B['AT'][:, 0:KC, 0:ntok], MTd[:, :, t0:t0 + ntok].rearrange("k p t -> p k t"), reads=['MTd'], writes=['AT'])
                down_proj(lambda db: mwo_b[m, db], KC, lambda db: need([('mw', m, 5)]), nt)
                cast_tick(3)
                store_block(t0, nt)

        stop = cfg.get('stop')
        if do_mix is True:
            rope_tables()
        last_even = DEPTH - 1 - (DEPTH - 1) % 2
        for l in range(DEPTH if stop != 'rope' else 0):
            ctx_in = l <= last_even
            ctx_out = l < last_even
            esA = scope_a()
            ffn_phase(l, 0, ctx_in)
            if l % 2 == 1 and do_mix:
                sconv_phase(l, ctx_out)
            elif l % 2 == 0 and do_mix is True:
                esC, sbc = mk_scope()
                mixer_consts(l // 2, sbc)
                if stop != 'ffn':
                    e1_phase(l)
                end_scope(esC)
                end_scope(esA)
                if stop in ('e1', 'ffn'):
                    return nc
                scan_phases(l)
                if stop in ('scan', 'sc_const', 'e2', 'e3ret', 'e3z', 'e3g', 'e3post'):
                    return nc
                esA = scope_a()
                e4_phase(l, ctx_out)
            ffn_phase(l, 1, ctx_out)
            end_scope(esA)

        esA = scope_a()
        tk.dma('act', GS[:], final_g.partition_broadcast(128), writes=['GS'])
        for t in range(NCT, NT):
            j = t % TB
            norm_tile(j, t * 128, modulate=False)
            tk.dma('pool', out[(t - NCT) * 128:(t - NCT + 1) * 128, :], SCR[:], reads=['SCR'], writes=['out'])
        tk.barrier()
        esA.close()
    return nc


def prep_inputs(cfg, b, x, c, ctx, c_ctx, ada_w, ada_b, norm_g, final_g, ffn_up, ffn_down, **kw):
    DEPTH, DFF, SEQ, CTX = cfg['depth'], cfg['dff'], cfg['seq'], cfg['ctx']
    NH = DFF // 128
    T = SEQ + CTX
    f32 = np.float32
    xin = np.concatenate([ctx[b], x[b]], axis=0)
    cc = np.stack([c[b], c_ctx], axis=0)
    cT = np.ascontiguousarray(cc.reshape(2, KC, 128).transpose(2, 1, 0))
    up = ffn_up[:DEPTH].reshape(DEPTH, 2, KC, 128, 2, NH, 128)
    wup = np.ascontiguousarray(up.transpose(0, 1, 5, 3, 2, 4, 6)).reshape(DEPTH, 2, NH, 128, KC * 256)
    dn = ffn_down[:DEPTH].reshape(DEPTH, 2, NH, 128, D // 256, 256)
    wdn = np.ascontiguousarray(dn.transpose(0, 1, 4, 3, 2, 5)).reshape(DEPTH, 2, D // 256, 128, NH * 256)
    N_ODD = max(DEPTH // 2, 1)
    N_EVEN = (DEPTH + 1) // 2
    sc_w_in, sc_conv_w, sc_w_out = kw['sc_w_in'], kw['sc_conv_w'], kw['sc_w_out']
    wi = sc_w_in[:N_ODD].reshape(N_ODD, KC, 128, 3, KC, 128)
    scw_cu = np.ascontiguousarray(wi[:, :, :, 1:3].transpose(0, 4, 2, 1, 3, 5)).reshape(N_ODD, KC, 128, KC * 256)
    scw_b = np.ascontiguousarray(wi[:, :, :, 0].transpose(0, 3, 2, 1, 4)).reshape(N_ODD, KC, 128, KC * 128)
    wo = sc_w_out[:N_ODD].reshape(N_ODD, KC, 128, D // 256, 256)
    scwo = np.ascontiguousarray(wo.transpose(0, 3, 2, 1, 4)).reshape(N_ODD, D // 256, 128, KC * 256)
    sccw = np.ascontiguousarray(sc_conv_w[:N_ODD].reshape(N_ODD, 3, KC, 128).transpose(0, 3, 2, 1))
    mw = kw['mix_w_in'][:N_EVEN]

    def fm_slot(cols):
        w = mw[:, :, cols].reshape(N_EVEN, KC, 128, 256)
        return w.transpose(0, 2, 1, 3).reshape(N_EVEN, 128, KC * 256)
    slots = []
    for base in (0, 1024):
        for pr in range(4):
            h0, h1 = 2 * pr, 2 * pr + 1
            cols = np.concatenate([base + h0 * 128 + np.arange(64), base + h1 * 128 + np.arange(64),
                                   base + h0 * 128 + 64 + np.arange(64), base + h1 * 128 + 64 + np.arange(64)])
            slots.append(fm_slot(cols))
    for s in range(6):
        slots.append(fm_slot(5120 + s * 256 + np.arange(256)))
    wfm = np.ascontiguousarray(np.stack(slots, axis=1))
    tms = []
    for blk in range(6):
        w = mw[:, :, 2048 + blk * 512:2048 + (blk + 1) * 512].reshape(N_EVEN, KC, 128, 512)
        tms.append(w.transpose(0, 2, 1, 3).reshape(N_EVEN, 128, KC * 512))
    wtm = np.ascontiguousarray(np.stack(tms, axis=1))
    wdt = np.ascontiguousarray(mw[:, :, 6656:6688].reshape(N_EVEN, KC, 128, 32).transpose(0, 2, 1, 3)).reshape(N_EVEN, 128, KC * 32)
    mo = kw['mix_w_out'][:N_EVEN].reshape(N_EVEN, KC, 128, D // 256, 256)
    mwo = np.ascontiguousarray(mo.transpose(0, 3, 2, 1, 4)).reshape(N_EVEN, D // 256, 128, KC * 256)
    mcw = np.ascontiguousarray(kw['mix_conv_w'][:N_EVEN].reshape(N_EVEN, 3, 12, 128).transpose(0, 3, 2, 1))
    mcb = np.ascontiguousarray(kw['mix_conv_b'][:N_EVEN].reshape(N_EVEN, 12, 128).transpose(0, 2, 1))
    tl = np.arange(SEQ)
    pos = np.zeros((T, 3), f32)
    pos[:CTX, 0] = np.arange(CTX)
    pos[CTX:, 0] = CTX
    pos[CTX:, 1] = tl // 64
    pos[CTX:, 2] = tl % 64
    axis = np.concatenate([np.zeros(16, int), np.ones(24, int), 2 * np.ones(24, int)])
    inv = np.concatenate([10000.0 ** (-np.arange(n, dtype=f32) / n) for n in (16, 24, 24)]).astype(f32)
    pax64 = pos[:, axis].T
    pax = np.ascontiguousarray(np.concatenate([pax64, pax64], axis=0)).astype(f32)
    inv128 = np.concatenate([inv, inv]).reshape(128, 1).astype(f32)
    iota = (np.arange(128)[None, :] - np.arange(128)[:, None]).astype(f32)
    return {
        "scw_cu": scw_cu, "scw_b": scw_b, "scwo": scwo, "sccw": sccw,
        "wfm": wfm, "wtm": wtm, "wdt": wdt, "mwo": mwo, "mcw": mcw, "mcb": mcb,
        "rld": np.ascontiguousarray(kw['ret_log_decay'][:N_EVEN].reshape(N_EVEN, 1, 16)),
        "gng": np.ascontiguousarray(kw['ret_gn_g'][:N_EVEN].reshape(N_EVEN, 1, 1024)),
        "alog": np.ascontiguousarray(kw['ssd_a_log'][:N_EVEN].reshape(N_EVEN, 1, 32)),
        "dtb": np.ascontiguousarray(kw['ssd_dt_bias'][:N_EVEN].reshape(N_EVEN, 1, 32)),
        "ssdd": np.ascontiguousarray(kw['ssd_d'][:N_EVEN].reshape(N_EVEN, 1, 16)),
        "sng": np.ascontiguousarray(kw['ssd_norm_g'][:N_EVEN].reshape(N_EVEN, 1, 1024)),
        "pax": pax, "inv": inv128, "iota_in": iota,
        "xin": np.ascontiguousarray(xin), "cT": cT,
        "ada_w": np.ascontiguousarray(ada_w[:DEPTH]), "ada_b": np.ascontiguousarray(ada_b[:DEPTH]),
        "norm_g": np.ascontiguousarray(norm_g[:DEPTH]), "final_g": np.ascontiguousarray(final_g.reshape(1, D)),
        "wup": wup, "wdn": wdn, "ident_in": np.eye(128, dtype=np.float32),
    }


def run(cfg, inputs, full=False):
    nc = build(cfg)
    nb = inputs['x'].shape[0]
    in_maps = [prep_inputs(cfg, b, **inputs) for b in range(nb)]
    res = run_bass_kernel_spmd(nc, in_maps, core_ids=list(range(nb)))
    if full:
        return res.results
    return np.stack([r["out"] for r in res.results], axis=0)


def kernel(**inputs):
    inputs = {k: np.asarray(v) for k, v in inputs.items()}
    cfg = dict(depth=4, seq=4096, ctx=256, dff=5632)
    return run(cfg, inputs).astype(np.float32)
```

```python
import numpy as np
import concourse.bass as bass
import concourse.mybir as mybir
from concourse.bass_utils import run_bass_kernel_spmd

F32 = mybir.dt.float32
BF16 = mybir.dt.bfloat16
AF = mybir.ActivationFunctionType
ALU = mybir.AluOpType
AX = mybir.AxisListType

D = 2048
KC = D // 128
EPS = 1e-6
N_MOD = 9


class TK:
    NS = {'sp': 16, 'pool': 8, 'act': 8}

    def __init__(self, nc):
        self.nc = nc
        self.names = ['pe', 'act', 'dve', 'pool', 'sp']
        self.q = {k: [] for k in self.names}
        self.cnt = {k: 0 for k in self.names}
        self.dcnt = {k: 0 for k in self.NS}
        self.waited = {k: {} for k in self.names}
        self.lastw = {}
        self.reads = {}
        self.semh = {}
        self.engs = {'pe': nc.tensor, 'act': nc.scalar, 'dve': nc.vector, 'pool': nc.gpsimd, 'sp': nc.sync}

    def semkeys(self):
        ks = list(self.names)
        for qn, n in self.NS.items():
            ks += [(qn, i) for i in range(n)]
        return ks

    def _deps(self, eng, reads, writes):
        deps = {}

        def add(ev):
            if ev is None:
                return
            k, v = ev
            if k == 'pe' and eng == 'pe':
                return
            if deps.get(k, 0) < v:
                deps[k] = v
        for r in reads:
            add(self.lastw.get(r))
        for w in writes:
            add(self.lastw.get(w))
            for k, v in self.reads.get(w, {}).items():
                add((k, v))
        out = []
        wd = self.waited[eng]
        for k, v in deps.items():
            if wd.get(k, 0) < v:
                wd[k] = v
                out.append((k, v))
        return out

    def _record(self, ev, reads, writes):
        for w in writes:
            self.lastw[w] = ev
            self.reads[w] = {}
        for r in reads:
            if r in writes:
                continue
            d = self.reads.setdefault(r, {})
            if d.get(ev[0], 0) < ev[1]:
                d[ev[0]] = ev[1]

    def op(self, eng, fn, reads=(), writes=()):
        waits = self._deps(eng, reads, writes)
        self.cnt[eng] += 1
        ev = (eng, self.cnt[eng])
        self._record(ev, reads, writes)
        semh = self.semh

        e = self.engs[eng]
        for k, v in waits:
            e.wait_ge(semh[k], v)
        fn(e).then_inc(semh[eng], 1)
        return ev

    def dma(self, qn, out, in_, reads=(), writes=(), **kw):
        waits = self._deps(qn, reads, writes)
        i = self.dcnt[qn]
        self.dcnt[qn] += 1
        ns = self.NS[qn]
        key = (qn, i % ns)
        val = 16 * (i // ns + 1)
        if i >= ns and self.waited[qn].get(key, 0) < val - 16:
            self.waited[qn][key] = val - 16
            waits.append((key, val - 16))
        ev = (key, val)
        self._record(ev, reads, writes)
        semh = self.semh

        e = self.engs[qn]
        for k, v in waits:
            e.wait_ge(semh[k], v)
        e.dma_start(out=out, in_=in_, **kw).then_inc(semh[key], 16)
        return ev

    def wait_all(self, eng):
        evs = {}
        for ev in self.lastw.values():
            if evs.get(ev[0], 0) < ev[1]:
                evs[ev[0]] = ev[1]
        semh = self.semh
        lst = list(evs.items())

        e = self.engs[eng]
        wd = self.waited[eng]
        for k, v in lst:
            if wd.get(k, 0) < v:
                wd[k] = v
                e.wait_ge(semh[k], v)

    def barrier(self):
        for eng in self.names:
            self.wait_all(eng)


RET_H = 8
SSD_H = 16
XBC_W = 1536
BIG = 30000.0
import contextlib
import math


def build(cfg):
    DEPTH, SEQ, CTX, DFF = cfg['depth'], cfg['seq'], cfg['ctx'], cfg['dff']
    do_mix = cfg.get('mix', True)
    debug = cfg.get('debug', False)
    T = SEQ + CTX
    NT = T // 128
    NCT = CTX // 128
    NH = DFF // 128
    NHA = max(NH, 32)
    NDB = D // 256
    N_ODD = max(DEPTH // 2, 1)
    N_EVEN = (DEPTH + 1) // 2
    nc = bass.Bass("TRN2", target_bir_lowering=False)
    tk = TK(nc)

    def dram_in(name, shape, dt=F32):
        return nc.dram_tensor(name, list(shape), dt, kind="ExternalInput").ap()

    def dram(name, shape, dt=F32, dbg=False):
        if dbg and debug:
            return nc.dram_tensor(name, list(shape), dt, kind="ExternalOutput").ap()
        return nc.dram_tensor(name, list(shape), dt).ap()

    xin = dram_in("xin", [T, D])
    cT = dram_in("cT", [128, KC, 2])
    ada_w = dram_in("ada_w", [DEPTH, D, N_MOD * D])
    ada_b = dram_in("ada_b", [DEPTH, N_MOD * D])
    norm_g = dram_in("norm_g", [DEPTH, 3, D])
    final_g = dram_in("final_g", [1, D])
    wup_in = dram_in("wup", [DEPTH, 2, NH, 128, KC * 256])
    wdn_in = dram_in("wdn", [DEPTH, 2, NDB, 128, NH * 256])
    ident_in = dram_in("ident_in", [128, 128])
    iota_in = dram_in("iota_in", [128, 128])
    scw_cu_in = dram_in("scw_cu", [N_ODD, KC, 128, KC * 256])
    scw_b_in = dram_in("scw_b", [N_ODD, KC, 128, KC * 128])
    scwo_in = dram_in("scwo", [N_ODD, NDB, 128, KC * 256])
    sccw_in = dram_in("sccw", [N_ODD, 128, KC, 3])
    wfm_in = dram_in("wfm", [N_EVEN, 14, 128, KC * 256])
    wtm_in = dram_in("wtm", [N_EVEN, 6, 128, KC * 512])
    wdt_in = dram_in("wdt", [N_EVEN, 128, KC * 32])
    mwo_in = dram_in("mwo", [N_EVEN, NDB, 128, KC * 256])
    mcw_in = dram_in("mcw", [N_EVEN, 128, 12, 3])
    mcb_in = dram_in("mcb", [N_EVEN, 128, 12])
    rld_in = dram_in("rld", [N_EVEN, 1, 16])
    gng_in = dram_in("gng", [N_EVEN, 1, 1024])
    alog_in = dram_in("alog", [N_EVEN, 1, 32])
    dtb_in = dram_in("dtb", [N_EVEN, 1, 32])
    ssdd_in = dram_in("ssdd", [N_EVEN, 1, 16])
    sng_in = dram_in("sng", [N_EVEN, 1, 1024])
    pax_in = dram_in("pax", [128, T])
    inv_in = dram_in("inv", [128, 1])
    out = nc.dram_tensor("out", [SEQ, D], F32, kind="ExternalOutput").ap()

    X = dram("X", [T, D])
    MOD = dram("MOD", [2, DEPTH * N_MOD * D])
    wup_b = [[dram(f"wup_b{l}_{i}", [NH, 128, KC * 256], BF16) for i in range(2)] for l in range(DEPTH)]
    wdn_b = [[dram(f"wdn_b{l}_{i}", [NDB, 128, NH * 256], BF16) for i in range(2)] for l in range(DEPTH)]
    scw_cu_b = dram("scw_cu_b", [N_ODD, KC, 128, KC * 256], BF16)
    scw_b_b = dram("scw_b_b", [N_ODD, KC, 128, KC * 128], BF16)
    scwo_b = dram("scwo_b", [N_ODD, NDB, 128, KC * 256], BF16)
    wfm_b = dram("wfm_b", [N_EVEN, 14, 128, KC * 256], BF16)
    wtm_b = dram("wtm_b", [N_EVEN, 6, 128, KC * 512], BF16)
    wdt_b = dram("wdt_b", [N_EVEN, 128, KC * 32], BF16)
    mwo_b = dram("mwo_b", [N_EVEN, NDB, 128, KC * 256], BF16)
    ROPE = dram("ROPE", [4, 128, T], dbg=True)
    QT = dram("QT", [RET_H, 128, T], dbg=True)
    KT = dram("KT", [RET_H, 128, T], dbg=True)
    Vd = dram("Vd", [T, 1024], dbg=True)
    SGd = dram("SGd", [T, 1024], dbg=True)
    SZd = dram("SZd", [T, 1024], dbg=True)
    XSd = dram("XSd", [T, 1024], dbg=True)
    BTOK = dram("BTOK", [T, 256], dbg=True)
    BTd = dram("BTd", [2, 128, T], dbg=True)
    CTd = dram("CTd", [2, 128, T], dbg=True)
    DTd = dram("DTd", [T, 32], dbg=True)
    DTAd = dram("DTAd", [T, 32], dbg=True)
    SBst = dram("SBst", [NT, 128, RET_H * 128])
    HBst = dram("HBst", [NT, 128, 1024])
    MTd = dram("MTd", [KC, 128, T], BF16)
    MIXd = dram("MIXd", [T, D], dbg=True)

    uid = [0]
    B = {}

    def mk_scope():
        es = contextlib.ExitStack()

        def sb(name, shape, dt=F32):
            uid[0] += 1
            t = es.enter_context(nc.sbuf_tensor(f"{name}_{uid[0]}", list(shape), dt))
            B[name] = t
            return t
        return es, sb

    TB = 4
    NTOK = TB * 128
    with contextlib.ExitStack() as es0:
        for k in tk.semkeys():
            nm = k if isinstance(k, str) else f"{k[0]}{k[1]}"
            tk.semh[k] = es0.enter_context(nc.semaphore("s_" + nm))

        def sb0(name, shape, dt=F32):
            return es0.enter_context(nc.sbuf_tensor(name, list(shape), dt))

        ident = sb0("ident", [128, 128], BF16)
        identf = sb0("identf", [128, 128], F32)
        GS = sb0("GS", [128, D])
        SH = sb0("SH", [128, D])
        GM = sb0("GM", [128, D])
        SCR = sb0("SCR", [128, D])
        HBL = [sb0("HB", [128, D], BF16), sb0("HB1", [128, D], BF16)]
        ST = sb0("ST", [128, 8])
        CW = sb0("CW", [128, KC, 3])
        cs = sb0("cs", [128, KC, 2])
        csg = sb0("csg", [128, KC, 2])
        ADB = sb0("ADB", [2, 512])
        ADO = sb0("ADO", [2, 512])
        IOT = sb0("IOT", [128, 128])
        PS = [es0.enter_context(nc.psum_tensor(f"ps{i}", [128, 512], F32)) for i in range(8)]
        PST = [PS[6][:].bitcast(BF16), PS[7][:].bitcast(BF16)]

        def scope_a():
            es, sb = mk_scope()
            sb("XT", [128, TB, D])
            sb("HT", [128, KC, NTOK], BF16)
            sb("AT", [128, NHA, NTOK], BF16)
            for i in range(2):
                sb(f"WUP{i}", [128, KC, 256], BF16)
                sb(f"WDN{i}", [128, NHA, 256], BF16)
                sb(f"SG{i}", [128, NTOK])
            return es

        def end_scope(es):
            tk.barrier()
            es.close()

        tk.dma('sp', identf[:], ident_in, writes=['identf'])
        tk.op('dve', lambda e: e.tensor_copy(out=ident[:], in_=identf[:]), reads=['identf'], writes=['ident'])
        tk.dma('sp', IOT[:], iota_in, writes=['IOT'])
        tk.dma('sp', X, xin, writes=['X'])
        cast_jobs = []
        GU = 11

        def add_ffn_casts(l, i):
            for h0 in range(0, NH, GU):
                h1 = min(NH, h0 + GU)
                cast_jobs.append((('wupb', l, i, h0), wup_b[l][i][h0:h1], wup_in[l, i, h0:h1]))
            for d0 in range(0, NDB, 2):
                cast_jobs.append((('wdnb', l, i, d0), wdn_b[l][i][d0:d0 + 2], wdn_in[l, i, d0:d0 + 2]))
        for l in range(DEPTH):
            add_ffn_casts(l, 0)
            m = l // 2
            if l % 2 == 1:
                cast_jobs.append((('scw', m, 0), scw_cu_b[m], scw_cu_in[m]))
                cast_jobs.append((('scw', m, 1), scw_b_b[m], scw_b_in[m]))
                cast_jobs.append((('scw', m, 2), scwo_b[m], scwo_in[m]))
            elif do_mix is True:
                cast_jobs.append((('mw', m, 0), wfm_b[m, 0:7], wfm_in[m, 0:7]))
                cast_jobs.append((('mw', m, 1), wfm_b[m, 7:14], wfm_in[m, 7:14]))
                cast_jobs.append((('mw', m, 2), wtm_b[m, 0:3], wtm_in[m, 0:3]))
                cast_jobs.append((('mw', m, 3), wtm_b[m, 3:6], wtm_in[m, 3:6]))
                cast_jobs.append((('mw', m, 4), wdt_b[m], wdt_in[m]))
                cast_jobs.append((('mw', m, 5), mwo_b[m], mwo_in[m]))
            add_ffn_casts(l, 1)
        cast_pos = [0]
        cast_done = set()

        def cast_tick(n):
            for _ in range(n):
                if cast_pos[0] < len(cast_jobs):
                    key, o_, i_ = cast_jobs[cast_pos[0]]
                    cast_pos[0] += 1
                    tk.dma('pool', o_, i_, writes=[key])
                    cast_done.add(key)

        def need(keys):
            for k in keys:
                while k not in cast_done:
                    cast_tick(1)
            return list(keys)

        def wdeps(kind, l, i, idx):
            if kind == 'up':
                return need([('wupb', l, i, (idx // GU) * GU)])
            return need([('wdnb', l, i, (idx // 2) * 2)])

        cast_tick(8)

        tk.dma('sp', cs[:], cT, writes=['cs'])
        tk.op('act', lambda e: e.activation(out=csg[:], in_=cs[:], func=AF.Silu), reads=['cs'], writes=['csg'])
        ncb = N_MOD * D // 512

        def ada_gen(layers, AW, pi):
            it = 0
            pn = f'ps{pi}'
            for l in layers:
                for cb in range(ncb):
                    tk.dma('act', ADB[:], ada_b[l:l + 1, cb * 512:(cb + 1) * 512].partition_broadcast(2), writes=['ADB'])
                    for kq in range(4):
                        buf, bname = AW[it % 2]
                        it += 1
                        bv = buf.rearrange("p (k c) -> p k c", k=4)
                        tk.dma('sp', bv, ada_w[l, kq * 512:(kq + 1) * 512, cb * 512:(cb + 1) * 512].rearrange(
                            "(k p) c -> p k c", p=128), writes=[bname])
                        for k in range(4):
                            kc = kq * 4 + k
                            tk.op('pe', lambda e, bv=bv, k=k, kc=kc: e.matmul(
                                PS[pi][0:2, :], lhsT=csg[:, kc, :], rhs=bv[:, k, :], start=(kc == 0), stop=(kc == KC - 1)),
                                reads=['csg', bname], writes=[pn])
                    tk.op('dve', lambda e: e.tensor_tensor(out=ADO[:], in0=PS[pi][0:2, :], in1=ADB[:], op=ALU.add),
                          reads=[pn, 'ADB'], writes=['ADO'])
                    off = l * N_MOD * D + cb * 512
                    tk.dma('act', MOD[:, off:off + 512], ADO[:], reads=['ADO'], writes=['MOD'])
                    yield

        defer_ada = (do_mix is True)
        for _ in ada_gen([0] if defer_ada else list(range(DEPTH)), [(SCR[:], 'SCR'), (GM[:], 'GM')], 0):
            pass

        def modrow(l, j, row):
            off = (l * N_MOD + j) * D
            return MOD[row:row + 1, off:off + D].partition_broadcast(128)

        def load_consts(l, gi, mi, row, gate_scale):
            tk.dma('act', SCR[:], norm_g[l, gi:gi + 1, :].partition_broadcast(128), writes=['SCR'])
            tk.dma('act', GS[:], modrow(l, 3 * mi + 1, row), reads=['MOD'], writes=['GS'])
            tk.dma('act', SH[:], modrow(l, 3 * mi, row), reads=['MOD'], writes=['SH'])
            tk.dma('act', GM[:], modrow(l, 3 * mi + 2, row), reads=['MOD'], writes=['GM'])
            tk.op('dve', lambda e: e.scalar_tensor_tensor(out=GS[:], in0=GS[:], scalar=1.0, in1=SCR[:],
                                                         op0=ALU.add, op1=ALU.mult),
                  reads=['GS', 'SCR'], writes=['GS'])
            if gate_scale != 1.0:
                tk.op('pool', lambda e: e.tensor_scalar(out=GM[:], in0=GM[:], scalar1=gate_scale, scalar2=None,
                                                       op0=ALU.mult), reads=['GM'], writes=['GM'])

        def rstd_from_sumsq(n):
            tk.op('dve', lambda e: e.tensor_scalar(out=ST[:, 1:2], in0=ST[:, 0:1], scalar1=1.0 / n, scalar2=EPS,
                                                  op0=ALU.mult, op1=ALU.add), reads=['ST'], writes=['ST'])
            tk.op('act', lambda e: e.activation(out=ST[:, 3:4], in_=ST[:, 1:2], func=AF.Sqrt), reads=['ST'], writes=['ST'])
            tk.op('dve', lambda e: e.reciprocal(out=ST[:, 2:3], in_=ST[:, 3:4]), reads=['ST'], writes=['ST'])

        def load_x(j, t0):
            tk.dma('sp', B['XT'][:, j, :], X[t0:t0 + 128, :], reads=['X'], writes=[('XT', j)])

        def norm_tile(j, t0, modulate=True):
            XT = B['XT']
            xj = ('XT', j)
            load_x(j, t0)
            tk.op('pool', lambda e: e.memset(ST[:, 0:1], 0.0), writes=['ST'])
            tk.op('act', lambda e: e.activation(out=SCR[:], in_=XT[:, j, :], func=AF.Square, accum_out=ST[:, 0:1]),
                  reads=[xj, 'ST'], writes=['SCR', 'ST'])
            rstd_from_sumsq(D)
            tk.op('dve', lambda e: e.scalar_tensor_tensor(out=SCR[:], in0=XT[:, j, :], scalar=ST[:, 2:3], in1=GS[:],
                                                         op0=ALU.mult, op1=ALU.mult),
                  reads=[xj, 'ST', 'GS'], writes=['SCR'])
            if modulate:
                HB = HBL[j % 2]
                tk.op('dve', lambda e: e.tensor_tensor(out=HB[:], in0=SCR[:], in1=SH[:], op=ALU.add),
                      reads=['SCR', 'SH'], writes=[f'HB{j % 2}'])

        def transpose_hb(dst, j, dname):
            for k4 in range(KC // 4):
                pi = k4 % 2
                pname = f'ps{6 + pi}'
                for k in range(4):
                    kc = k4 * 4 + k
                    tk.op('pe', lambda e, kc=kc, k=k, pi=pi: e.transpose(
                        out=PST[pi][:, k * 128:(k + 1) * 128], in_=HBL[j % 2][:, kc * 128:(kc + 1) * 128], identity=ident[:]),
                        reads=[f'HB{j % 2}', 'ident'], writes=[pname])
                tk.op('act', lambda e, k4=k4, pi=pi: e.copy(
                    out=dst[:, k4 * 4:(k4 + 1) * 4, j * 128:(j + 1) * 128],
                    in_=PST[pi][:, 0:512].rearrange("p (k t) -> p k t", k=4)),
                    reads=[pname], writes=[dname])

        def norm_block(t0, nt):
            for j in range(nt):
                norm_tile(j, t0 + j * 128)
                transpose_hb(B['HT'], j, 'HT')

        wctr = {'up': 0, 'dn': 0}

        def wk(base, s):
            return [f'{base}{s}', f'{base}{s}a', f'{base}{s}b']

        def load_wup(src, deps):
            s = (wctr['up'] // 2) % 2
            wctr['up'] += 2
            tk.dma('sp', B[f'WUP{s}'][:].rearrange("p k c -> p (k c)"), src, reads=need(deps), writes=wk('WUP', s))
            return B[f'WUP{s}'], wk('WUP', s)

        def fm_mm(pi, W, wn, c0, ntok, m=128):
            HT = B['HT']
            for kc in range(KC):
                tk.op('pe', lambda e, kc=kc: e.matmul(
                    PS[pi][0:m, 0:ntok], lhsT=W[:, kc, c0:c0 + m], rhs=HT[:, kc, 0:ntok], start=(kc == 0), stop=(kc == KC - 1)),
                    reads=(wn if isinstance(wn, list) else [wn]) + ['HT'], writes=[f'ps{pi}'])

        def up_ffn(l, ii, ntok):
            AT, HT = B['AT'], B['HT']
            HK = KC // 2
            for hc in range(NH):
                a = 2 * (hc % 2)
                deps = wdeps('up', l, ii, hc)
                for half in range(2):
                    r = wctr['up'] % 4
                    wctr['up'] += 1
                    sl, part = r // 2, r % 2
                    key = f"WUP{sl}{'ab'[part]}"
                    W = B[f'WUP{sl}'][:, part * HK:(part + 1) * HK, :]
                    tk.dma('sp', W, wup_b[l][ii][hc][:, half * HK * 256:(half + 1) * HK * 256].rearrange("p (k c) -> p k c", c=256),
                           reads=deps, writes=[key])
                    for (pi, c0) in ((a, 0), (a + 1, 128)):
                        for k in range(HK):
                            kc = half * HK + k
                            tk.op('pe', lambda e, k=k, kc=kc, pi=pi, c0=c0, W=W: e.matmul(
                                PS[pi][:, 0:ntok], lhsT=W[:, k, c0:c0 + 128], rhs=HT[:, kc, 0:ntok], start=(kc == 0), stop=(kc == KC - 1)),
                                reads=[key, 'HT'], writes=[f'ps{pi}'])
                sg, sgn = B[f'SG{hc % 2}'], f'SG{hc % 2}'
                tk.op('act', lambda e, sg=sg, a=a: e.activation(out=sg[:, 0:ntok], in_=PS[a][:, 0:ntok], func=AF.Silu),
                      reads=[f'ps{a}'], writes=[sgn])
                tk.op('dve', lambda e, sg=sg, a=a, hc=hc: e.tensor_tensor(
                    out=AT[:, hc, 0:ntok], in0=sg[:, 0:ntok], in1=PS[a + 1][:, 0:ntok], op=ALU.mult),
                    reads=[sgn, f'ps{a + 1}'], writes=['AT'])

        def conv3(R, Tt, ntok, rowlen, cw, ci, cwn):
            R3 = R.rearrange("p (r w) -> p r w", w=rowlen)
            T3 = Tt.rearrange("p (r w) -> p r w", w=rowlen)
            tk.op('dve', lambda e: e.tensor_scalar(out=R, in0=Tt, scalar1=cw[:, ci, 1:2], scalar2=None, op0=ALU.mult),
                  reads=['SCR', cwn], writes=['SCR'])
            tk.op('dve', lambda e: e.scalar_tensor_tensor(
                out=R3[:, :, 1:rowlen], in0=T3[:, :, 0:rowlen - 1], scalar=cw[:, ci, 0:1], in1=R3[:, :, 1:rowlen],
                op0=ALU.mult, op1=ALU.add), reads=['SCR', cwn], writes=['SCR'])
            tk.op('dve', lambda e: e.scalar_tensor_tensor(
                out=R3[:, :, 0:rowlen - 1], in0=T3[:, :, 1:rowlen], scalar=cw[:, ci, 2:3], in1=R3[:, :, 0:rowlen - 1],
                op0=ALU.mult, op1=ALU.add), reads=['SCR', cwn], writes=['SCR'])

        def up_sconv(m, ntok, rowlen):
            AT = B['AT']
            R = SCR[:, 0:ntok]
            Tt = SCR[:, 512:512 + ntok]
            for fc in range(KC):
                tk.dma('sp', B['WUP0'][:].rearrange("p k c -> p (k c)"), scw_cu_b[m, fc], reads=need([('scw', m, 0)]), writes=wk('WUP', 0))
                tk.dma('sp', B['WUP1'][:, :, 0:128], scw_b_b[m, fc].rearrange("p (k c) -> p k c", c=128),
                       reads=need([('scw', m, 1)]), writes=wk('WUP', 1))
                fm_mm(0, B['WUP0'], wk('WUP', 0), 0, ntok)
                fm_mm(1, B['WUP0'], wk('WUP', 0), 128, ntok)
                fm_mm(2, B['WUP1'], wk('WUP', 1), 0, ntok)
                sg = B['SG0']
                tk.op('act', lambda e: e.copy(out=sg[:, 0:ntok], in_=PS[0][:, 0:ntok]), reads=['ps0'], writes=['SG0'])
                tk.op('dve', lambda e: e.tensor_tensor(out=Tt, in0=sg[:, 0:ntok], in1=PS[1][:, 0:ntok], op=ALU.mult),
                      reads=['SG0', 'ps1'], writes=['SCR'])
                conv3(R, Tt, ntok, rowlen, CW, fc, 'CW')
                tk.op('dve', lambda e, fc=fc: e.tensor_tensor(out=AT[:, fc, 0:ntok], in0=R, in1=PS[2][:, 0:ntok], op=ALU.mult),
                      reads=['SCR', 'ps2'], writes=['AT'])

        def down_proj(wsrc, nh, depfn, nt):
            AT, XT = B['AT'], B['XT']
            hn = nh // 2
            for db in range(NDB):
                deps = depfn(db)
                pbase = 4 * (db % 2)
                for half in range(2):
                    r = wctr['dn'] % 4
                    wctr['dn'] += 1
                    sl, part = r // 2, r % 2
                    key = f"WDN{sl}{'ab'[part]}"
                    W = B[f'WDN{sl}'][:, part * (NHA // 2):part * (NHA // 2) + hn, :]
                    tk.dma('sp', W, wsrc(db)[:, half * hn * 256:(half + 1) * hn * 256].rearrange("p (h c) -> p h c", c=256),
                           reads=deps, writes=[key])
                    for j in range(nt):
                        pb = pbase + j
                        for h in range(hn):
                            hc = half * hn + h
                            tk.op('pe', lambda e, W=W, h=h, hc=hc, j=j, pb=pb: e.matmul(
                                PS[pb][:, 0:256], lhsT=AT[:, hc, j * 128:(j + 1) * 128], rhs=W[:, h, :],
                                start=(hc == 0), stop=(hc == nh - 1)),
                                reads=[key, 'AT'], writes=[f'ps{pb}'])
                for j in range(nt):
                    pb = pbase + j
                    pn = f'ps{pb}'
                    xj = ('XT', j)
                    sl_ = slice(db * 256, (db + 1) * 256)
                    tk.op('dve', lambda e, pb=pb, sl_=sl_: e.tensor_tensor(
                        out=SCR[:, sl_], in0=PS[pb][:, 0:256], in1=GM[:, sl_], op=ALU.mult),
                        reads=[pn, 'GM'], writes=['SCR'])
                    tk.op('pool', lambda e, j=j, sl_=sl_: e.tensor_tensor(
                        out=XT[:, j, sl_], in0=XT[:, j, sl_], in1=SCR[:, sl_], op=ALU.add),
                        reads=['SCR', xj], writes=[xj])

        def store_block(t0, nt):
            for j in range(nt):
                tk.dma('pool', X[t0 + j * 128:t0 + (j + 1) * 128, :], B['XT'][:, j, :], reads=[('XT', j)], writes=['X'])

        def blocks(with_ctx):
            bl = []
            if with_ctx:
                for t in range(0, NCT, TB):
                    bl.append((t * 128, min(TB, NCT - t), 1))
            for t in range(NCT, NT, TB):
                bl.append((t * 128, min(TB, NT - t), 0))
            return bl

        def ffn_phase(l, i, with_ctx):
            mi = 0 if i == 0 else 2
            ii = 0 if i == 0 else 1
            currow = None
            for (t0, nt, row) in blocks(with_ctx):
                if row != currow:
                    load_consts(l, mi, mi, row, 0.5)
                    currow = row
                norm_block(t0, nt)
                up_ffn(l, ii, nt * 128)
                down_proj(lambda db: wdn_b[l][ii][db], NH, lambda db: wdeps('dn', l, ii, db), nt)
                cast_tick(2)
                store_block(t0, nt)

        def sconv_phase(l, with_ctx):
            m = l // 2
            tk.dma('act', CW[:], sccw_in[m], writes=['CW'])
            currow = None
            for (t0, nt, row) in blocks(with_ctx):
                if row != currow:
                    load_consts(l, 1, 1, row, 1.0)
                    currow = row
                norm_block(t0, nt)
                up_sconv(m, nt * 128, 64 if row == 0 else nt * 128)
                down_proj(lambda db: scwo_b[m, db], KC, lambda db: need([('scw', m, 2)]), nt)
                cast_tick(2)
                store_block(t0, nt)

        def rope_tables():
            es, sb = mk_scope()
            PAX = sb("PAX", [128, 512])
            ANG = sb("ANG", [128, 512])
            TMP = sb("TMP", [128, 512])
            RES = sb("RES", [128, 4, 512])
            INV = sb("INV", [128, 1])
            YI = sb("YI", [128, 512], mybir.dt.int32)
            YF = sb("YF", [128, 512])
            tk.dma('sp', INV[:], inv_in, writes=['INV'])
            for c0 in range(0, T, 512):
                n = min(512, T - c0)
                tk.dma('sp', PAX[:, 0:n], pax_in[:, c0:c0 + n], writes=['PAX'])
                tk.op('dve', lambda e: e.tensor_scalar(out=ANG[:, 0:n], in0=PAX[:, 0:n], scalar1=INV[:, 0:1], scalar2=None,
                                                      op0=ALU.mult), reads=['PAX', 'INV'], writes=['ANG'])
                for (ri, shift) in ((0, 0.25), (1, 0.0)):
                    tk.op('dve', lambda e, shift=shift: e.tensor_scalar(
                        out=TMP[:, 0:n], in0=ANG[:, 0:n], scalar1=1.0 / (2 * math.pi), scalar2=shift, op0=ALU.mult, op1=ALU.add),
                        reads=['ANG'], writes=['TMP'])
                    tk.op('dve', lambda e: e.tensor_copy(out=YI[:, 0:n], in_=TMP[:, 0:n]), reads=['TMP'], writes=['YI'])
                    tk.op('dve', lambda e: e.tensor_copy(out=YF[:, 0:n], in_=YI[:, 0:n]), reads=['YI'], writes=['YF'])
                    tk.op('dve', lambda e: e.tensor_tensor(out=TMP[:, 0:n], in0=TMP[:, 0:n], in1=YF[:, 0:n], op=ALU.subtract),
                          reads=['TMP', 'YF'], writes=['TMP'])
                    tk.op('dve', lambda e: e.tensor_scalar(out=YF[:, 0:n], in0=TMP[:, 0:n], scalar1=0.5, scalar2=None, op0=ALU.is_gt),
                          reads=['TMP'], writes=['YF'])
                    tk.op('dve', lambda e: e.tensor_tensor(out=TMP[:, 0:n], in0=TMP[:, 0:n], in1=YF[:, 0:n], op=ALU.subtract),
                          reads=['TMP', 'YF'], writes=['TMP'])
                    tk.op('act', lambda e, ri=ri: e.activation(out=RES[:, ri, 0:n], in_=TMP[:, 0:n], func=AF.Sin,
                                                               scale=2 * math.pi - 1e-5),
                          reads=['TMP'], writes=['RES'])
                    tk.op('pool', lambda e, ri=ri: e.tensor_scalar(
                        out=RES[:, 2 + ri, 0:n], in0=RES[:, ri, 0:n], scalar1=128 ** -0.5, scalar2=None, op0=ALU.mult),
                        reads=['RES'], writes=['RES'])
                tk.dma('pool', ROPE[:, :, c0:c0 + n].rearrange("r p t -> p r t"), RES[:, :, 0:n], reads=['RES'], writes=['ROPE'])
            end_scope(es)

        def e1_phase(l):
            m = l // 2
            MCW = B['MCW']
            currow = None
            for (t0, nt, row) in blocks(True):
                ntok = nt * 128
                rowlen = 64 if row == 0 else ntok
                if row != currow:
                    load_consts(l, 1, 1, row, 1.0)
                    currow = row
                norm_block(t0, nt)
                WS = B['AT'][:].rearrange("p h t -> p (h t)").bitcast(F32)
                tabs = [WS[:, i * 512:i * 512 + ntok] for i in range(4)]
                for i in range(4):
                    tk.dma('sp', tabs[i], ROPE[i, :, t0:t0 + ntok], reads=['ROPE'], writes=['AT'])
                SG0, SG1 = B['SG0'], B['SG1']
                for slot in range(8):
                    W, wn = load_wup(wfm_b[m, slot], [('mw', m, 0), ('mw', m, 1)])
                    fm_mm(0, W, wn, 0, ntok)
                    fm_mm(1, W, wn, 128, ntok)
                    cos, sin = (tabs[0], tabs[1]) if slot < 4 else (tabs[2], tabs[3])
                    for (oi, ta, tb, op) in ((0, cos, sin, ALU.subtract), (1, sin, cos, ALU.add)):
                        tk.op('dve', lambda e, ta=ta: e.tensor_tensor(out=SG0[:, 0:ntok], in0=PS[0][:, 0:ntok], in1=ta, op=ALU.mult),
                              reads=['ps0', 'AT'], writes=['SG0'])
                        tk.op('dve', lambda e, tb=tb: e.tensor_tensor(out=SG1[:, 0:ntok], in0=PS[1][:, 0:ntok], in1=tb, op=ALU.mult),
                              reads=['ps1', 'AT'], writes=['SG1'])
                        tk.op('pool', lambda e, oi=oi, op=op: e.tensor_tensor(
                            out=SCR[:, oi * 512:oi * 512 + ntok], in0=SG0[:, 0:ntok], in1=SG1[:, 0:ntok], op=op),
                            reads=['SG0', 'SG1'], writes=['SCR'])
                    dst = QT if slot < 4 else KT
                    h0 = 2 * (slot % 4)
                    for hh in range(2):
                        for half in range(2):
                            tk.dma('act', dst[h0 + hh, half * 64:(half + 1) * 64, t0:t0 + ntok],
                                   SCR[hh * 64:(hh + 1) * 64, half * 512:half * 512 + ntok], reads=['SCR'], writes=['QK'])
                for slot in range(6):
                    W, wn = load_wup(wfm_b[m, 8 + slot], [('mw', m, 1)])
                    for half in range(2):
                        ch = 2 * slot + half
                        fm_mm(half, W, wn, half * 128, ntok)
                        Tt = SCR[:, 512:512 + ntok]
                        R = SCR[:, 0:ntok]
                        A = SCR[:, 1024:1024 + ntok]
                        tk.op('act', lambda e, half=half: e.copy(out=Tt, in_=PS[half][:, 0:ntok]), reads=[f'ps{half}'], writes=['SCR'])
                        conv3(R, Tt, ntok, rowlen, MCW, ch, 'MCW')
                        tk.op('act', lambda e, ch=ch: e.activation(out=A, in_=R, func=AF.Silu, bias=B['MCB'][:, ch:ch + 1], scale=1.0),
                              reads=['SCR', 'MCB'], writes=['SCR'])
                        if ch >= 8:
                            g = (ch - 8) % 2
                            dst = BTd if ch < 10 else CTd
                            tk.dma('act', dst[g, :, t0:t0 + ntok], A, reads=['SCR'], writes=['BC'])
                        if ch < 10:
                            for j in range(nt):
                                tk.op('pe', lambda e, j=j: e.transpose(out=PS[2][:, j * 128:(j + 1) * 128], in_=A[:, j * 128:(j + 1) * 128],
                                                                       identity=identf[:]), reads=['SCR', 'identf'], writes=['ps2'])
                            tk.op('act', lambda e: e.copy(out=SG0[:, 0:ntok], in_=PS[2][:, 0:ntok]), reads=['ps2'], writes=['SG0'])
                            for j in range(nt):
                                r0 = t0 + j * 128
                                if ch < 8:
                                    tk.dma('act', XSd[r0:r0 + 128, ch * 128:(ch + 1) * 128], SG0[:, j * 128:(j + 1) * 128], reads=['SG0'], writes=['XS'])
                                else:
                                    tk.dma('act', BTOK[r0:r0 + 128, (ch - 8) * 128:(ch - 7) * 128], SG0[:, j * 128:(j + 1) * 128], reads=['SG0'], writes=['XS'])
                HT = B['HT']
                for blk in range(6):
                    s = (wctr['dn'] // 2) % 2
                    wctr['dn'] += 2
                    wn = f'WDN{s}'
                    W = B[wn][:].rearrange("p h c -> p (h c)")[:, 0:KC * 512].rearrange("p (k c) -> p k c", c=512)
                    tk.dma('sp', W, wtm_b[m, blk].rearrange("p (k c) -> p k c", c=512), reads=need([('mw', m, 2), ('mw', m, 3)]), writes=wk('WDN', s))
                    dst = (Vd, SGd, SZd)[blk // 2]
                    for j in range(nt):
                        pb = 4 + j % 2
                        for kc in range(KC):
                            tk.op('pe', lambda e, kc=kc, j=j, pb=pb, W=W: e.matmul(
                                PS[pb][:, :], lhsT=HT[:, kc, j * 128:(j + 1) * 128], rhs=W[:, kc, :], start=(kc == 0), stop=(kc == KC - 1)),
                                reads=wk('WDN', s) + ['HT'], writes=[f'ps{pb}'])
                        sg, sgn = B[f'SG{j % 2}'], f'SG{j % 2}'
                        if blk < 2:
                            tk.op('act', lambda e, sg=sg, pb=pb: e.copy(out=sg[:], in_=PS[pb][:, :]), reads=[f'ps{pb}'], writes=[sgn])
                        else:
                            tk.op('act', lambda e, sg=sg, pb=pb: e.activation(out=sg[:], in_=PS[pb][:, :], func=AF.Silu), reads=[f'ps{pb}'], writes=[sgn])
                        r0 = t0 + j * 128
                        tk.dma('pool', dst[r0:r0 + 128, (blk % 2) * 512:(blk % 2 + 1) * 512], sg[:], reads=[sgn], writes=['TM'])
                W = B['WUP0'][:].rearrange("p k c -> p (k c)")[:, 0:KC * 32].rearrange("p (k c) -> p k c", c=32)
                tk.dma('sp', W, wdt_b[m].rearrange("p (k c) -> p k c", c=32), reads=need([('mw', m, 4)]), writes=wk('WUP', 0))
                for j in range(nt):
                    for kc in range(KC):
                        tk.op('pe', lambda e, kc=kc, j=j: e.matmul(
                            PS[3][:, 0:32], lhsT=HT[:, kc, j * 128:(j + 1) * 128], rhs=W[:, kc, :], start=(kc == 0), stop=(kc == KC - 1)),
                            reads=wk('WUP', 0) + ['HT'], writes=['ps3'])
                    DTt = B['DTT']
                    tk.op('dve', lambda e: e.tensor_tensor(out=DTt[:, 0:32], in0=PS[3][:, 0:32], in1=B['DTB'][:], op=ALU.add),
                          reads=['ps3', 'DTB'], writes=['DTT'])
                    tk.op('act', lambda e: e.activation(out=DTt[:, 32:64], in_=DTt[:, 0:32], func=AF.Exp), reads=['DTT'], writes=['DTT'])
                    tk.op('act', lambda e: e.activation(out=DTt[:, 64:96], in_=DTt[:, 32:64], func=AF.Ln, bias=B['ONE1'][:, 0:1], scale=1.0),
                          reads=['DTT', 'ONE1'], writes=['DTT'])
                    tk.op('dve', lambda e: e.tensor_tensor(out=DTt[:, 96:128], in0=DTt[:, 64:96], in1=B['A32'][:], op=ALU.mult),
                          reads=['DTT', 'A32'], writes=['DTT'])
                    r0 = t0 + j * 128
                    tk.dma('pool', DTd[r0:r0 + 128, :], DTt[:, 64:96], reads=['DTT'], writes=['DT'])
                    tk.dma('pool', DTAd[r0:r0 + 128, :], DTt[:, 96:128], reads=['DTT'], writes=['DT'])

        def mixer_consts(m, sb):
            MCW = sb("MCW", [128, 12, 3])
            MCB = sb("MCB", [128, 12])
            DTB = sb("DTB", [128, 32])
            A32 = sb("A32", [128, 32])
            ONE1 = sb("ONE1", [128, 1])
            DTT = sb("DTT", [128, 128])
            tk.dma('act', MCW[:], mcw_in[m], writes=['MCW'])
            tk.dma('act', MCB[:], mcb_in[m], writes=['MCB'])
            tk.dma('act', DTB[:], dtb_in[m].partition_broadcast(128), writes=['DTB'])
            tk.dma('act', A32[:], alog_in[m].partition_broadcast(128), writes=['A32'])
            tk.op('act', lambda e: e.activation(out=A32[:], in_=A32[:], func=AF.Exp), reads=['A32'], writes=['A32'])
            tk.op('dve', lambda e: e.tensor_scalar(out=A32[:], in0=A32[:], scalar1=-1.0, scalar2=None, op0=ALU.mult),
                  reads=['A32'], writes=['A32'])
            tk.op('pool', lambda e: e.memset(ONE1[:], 1.0), writes=['ONE1'])

        def scan_phases(l):
            m = l // 2
            es, sb = mk_scope()
            TRIF = sb("TRIF", [128, 128]); TRIB = sb("TRIB", [128, 128])
            MF = sb("MF", [128, 128]); MB = sb("MB", [128, 128]); ONES = sb("ONES", [128, 128])
            LG = sb("LG", [128, 16]); CD = sb("CD", [128, 16]); KD = sb("KD", [128, 16])
            DC = sb("DC", [128, RET_H, 128]); DQF = sb("DQF", [128, RET_H, 128]); DQB = sb("DQB", [128, RET_H, 128])
            GNG = sb("GNG", [128, 1024]); SNG = sb("SNG", [128, 1024]); DSK = sb("DSK", [128, 16])
            TMPA = sb("TMPA", [128, 128]); TMPB = sb("TMPB", [128, 128]); PCOL = sb("PCOL", [128, 2])
            ONE1 = sb("ONE1", [128, 1])
            tk.op('pool', lambda e: e.memset(ONE1[:], 1.0), writes=['ONE1'])
            tk.op('pool', lambda e: e.memset(ONES[:], 1.0), writes=['ONES'])
            tk.op('dve', lambda e: e.tensor_scalar(out=TRIF[:], in0=IOT[:], scalar1=0.0, scalar2=None, op0=ALU.is_ge), reads=['IOT'], writes=['TRIF'])
            tk.op('dve', lambda e: e.tensor_scalar(out=TRIB[:], in0=IOT[:], scalar1=0.0, scalar2=None, op0=ALU.is_le), reads=['IOT'], writes=['TRIB'])
            tk.op('dve', lambda e: e.tensor_scalar(out=MF[:], in0=IOT[:], scalar1=0.0, scalar2=-BIG, op0=ALU.is_lt, op1=ALU.mult), reads=['IOT'], writes=['MF'])
            tk.op('dve', lambda e: e.tensor_scalar(out=MB[:], in0=IOT[:], scalar1=0.0, scalar2=-BIG, op0=ALU.is_gt, op1=ALU.mult), reads=['IOT'], writes=['MB'])
            tk.dma('act', LG[:], rld_in[m].partition_broadcast(128), writes=['LG'])
            tk.dma('act', GNG[:], gng_in[m].partition_broadcast(128), writes=['GNG'])
            tk.dma('act', SNG[:], sng_in[m].partition_broadcast(128), writes=['SNG'])
            tk.dma('act', DSK[:], ssdd_in[m].partition_broadcast(128), writes=['DSK'])
            tk.op('act', lambda e: e.activation(out=CD[:], in_=LG[:], func=AF.Exp, scale=128.0), reads=['LG'], writes=['CD'])
            tk.op('dve', lambda e: e.tensor_copy(out=PCOL[:, 0:1], in_=IOT[:, 127:128]), reads=['IOT'], writes=['PCOL'])
            tk.op('dve', lambda e: e.tensor_scalar(out=PCOL[:, 1:2], in0=IOT[:, 0:1], scalar1=-1.0, scalar2=None, op0=ALU.mult), reads=['IOT'], writes=['PCOL'])
            tk.op('dve', lambda e: e.tensor_scalar(out=KD[:, 0:8], in0=LG[:, 0:8], scalar1=PCOL[:, 0:1], scalar2=None, op0=ALU.mult), reads=['LG', 'PCOL'], writes=['KD'])
            tk.op('dve', lambda e: e.tensor_scalar(out=KD[:, 8:16], in0=LG[:, 8:16], scalar1=PCOL[:, 1:2], scalar2=None, op0=ALU.mult), reads=['LG', 'PCOL'], writes=['KD'])
            tk.op('act', lambda e: e.activation(out=KD[:], in_=KD[:], func=AF.Exp), reads=['KD'], writes=['KD'])
            TPOS = sb("TPOS", [128, 128])
            tk.op('dve', lambda e: e.tensor_scalar(out=TPOS[:], in0=IOT[:], scalar1=PCOL[:, 1:2], scalar2=None, op0=ALU.add), reads=['IOT', 'PCOL'], writes=['TPOS'])
            for h in range(RET_H):
                tk.op('dve', lambda e, h=h: e.scalar_tensor_tensor(out=TMPA[:], in0=IOT[:], scalar=LG[:, h:h + 1], in1=MF[:], op0=ALU.mult, op1=ALU.add),
                      reads=['IOT', 'LG', 'MF'], writes=['TMPA'])
                tk.op('act', lambda e: e.activation(out=TMPA[:], in_=TMPA[:], func=AF.Exp), reads=['TMPA'], writes=['TMPA'])
                tk.op('dve', lambda e, h=h: e.tensor_scalar(out=TMPB[:], in0=IOT[:], scalar1=LG[:, 8 + h:9 + h], scalar2=-1.0, op0=ALU.mult, op1=ALU.mult),
                      reads=['IOT', 'LG'], writes=['TMPB'])
                tk.op('dve', lambda e: e.tensor_tensor(out=TMPB[:], in0=TMPB[:], in1=MB[:], op=ALU.add), reads=['TMPB', 'MB'], writes=['TMPB'])
                tk.op('act', lambda e: e.activation(out=TMPB[:], in_=TMPB[:], func=AF.Exp), reads=['TMPB'], writes=['TMPB'])
                tk.op('dve', lambda e, h=h: e.tensor_tensor(out=DC[:, h, :], in0=TMPA[:], in1=TMPB[:], op=ALU.add), reads=['TMPA', 'TMPB'], writes=['DC'])
                tk.op('dve', lambda e, h=h: e.tensor_scalar(out=TMPA[:], in0=TPOS[:], scalar1=1.0, scalar2=LG[:, h:h + 1], op0=ALU.add, op1=ALU.mult),
                      reads=['TPOS', 'LG'], writes=['TMPA'])
                tk.op('act', lambda e, h=h: e.activation(out=DQF[:, h, :], in_=TMPA[:], func=AF.Exp), reads=['TMPA'], writes=['DQF'])
                tk.op('dve', lambda e, h=h: e.tensor_scalar(out=TMPB[:], in0=TPOS[:], scalar1=-128.0, scalar2=LG[:, 8 + h:9 + h], op0=ALU.add, op1=ALU.mult),
                      reads=['TPOS', 'LG'], writes=['TMPB'])
                tk.op('act', lambda e, h=h: e.activation(out=DQB[:, h, :], in_=TMPB[:], func=AF.Exp, scale=-1.0), reads=['TMPB'], writes=['DQB'])
            CH = []
            for par in range(2):
                d = {}
                for nm, shp in (("KTc", [128, RET_H, 128]), ("QTc", [128, RET_H, 128]), ("Vc", [128, 1024]), ("XSc", [128, 1024]),
                                ("SGc", [128, 1024]), ("SZc", [128, 1024]), ("BTc", [128, 2, 128]), ("CTc", [128, 2, 128]),
                                ("BKc", [128, 256]), ("DTc", [128, 32]), ("DTAc", [128, 32]),
                                ("SBi", [128, RET_H, 128]), ("HBi", [128, 2, 512])):
                    d[nm] = sb(f"{nm}{par}", shp)
                d['par'] = par
                CH.append(d)
            SF = sb("SF", [128, RET_H, 128]); HF = sb("HF", [128, 2, 512])
            KXs = [sb(f"KX{i}", [128, 128]) for i in range(2)]; PTs = [sb(f"PT{i}", [128, 128]) for i in range(2)]
            QFs = [sb(f"QF{i}", [128, 128]) for i in range(2)]; QBs = [sb(f"QB{i}", [128, 128]) for i in range(2)]
            ada_it = ada_gen([x for x in (l + 1, l + 2) if x < DEPTH] if defer_ada else [], [(GS[:], 'GS'), (SH[:], 'SH')], 3)

            def ada_tick(n=1):
                for _ in range(n):
                    next(ada_it, None)
            O8 = sb("O8", [128, RET_H, 128]); D8 = sb("D8", [128, RET_H, 128]); S8 = sb("S8", [128, 32])
            CUM = sb("CUM", [128, 32]); TOT = sb("TOT", [128, 32]); ECUM = sb("ECUM", [128, 32]); ETOT = sb("ETOT", [128, 32]); WG = sb("WG", [128, 32])
            ZH = [GM[:].rearrange("p (a t) -> p a t", t=128), SCR[:].rearrange("p (a t) -> p a t", t=128)]
            ZK = ['GM', 'SCR']
            CBs = sb("CBs", [128, 128])
            EFs = [sb(f"EF{i}", [128, 128]) for i in range(2)]; EBs = [sb(f"EB{i}", [128, 128]) for i in range(2)]
            WTs = [sb(f"WT{i}", [128, 128]) for i in range(2)]
            XWs = [sb(f"XW{i}", [128, 512]) for i in range(2)]
            YG = sb("YG", [128, 1024]); YT = sb("YT", [128, 512])
            MB16 = sb("MB16", [128, D], BF16); MTc = sb("MTc", [128, KC, 128], BF16)

            def bc8(ap8):
                return ap8.unsqueeze(2).to_broadcast([128, 8, 64])

            def interleave(gens, width):
                gens = iter(gens)
                active = []
                while True:
                    while len(active) < width:
                        g = next(gens, None)
                        if g is None:
                            break
                        active.append(g)
                    if not active:
                        break
                    for g in list(active):
                        try:
                            next(g)
                        except StopIteration:
                            active.remove(g)

            def k_(C, nm):
                return f"{nm}{C['par']}"

            def load_chunk(C, c, full, sbsrc=None):
                r0 = c * 128
                tk.dma('sp', C['KTc'][:], KT[:, :, r0:r0 + 128].rearrange("h d t -> d h t"), reads=['QK'], writes=[k_(C, 'KTc')])
                tk.dma('sp', C['Vc'][:], Vd[r0:r0 + 128, :], reads=['TM'], writes=[k_(C, 'Vc')])
                tk.dma('sp', C['XSc'][:], XSd[r0:r0 + 128, :], reads=['XS'], writes=[k_(C, 'XSc')])
                tk.dma('sp', C['BKc'][:], BTOK[r0:r0 + 128, :], reads=['XS'], writes=[k_(C, 'BKc')])
                tk.dma('sp', C['DTc'][:], DTd[r0:r0 + 128, :], reads=['DT'], writes=[k_(C, 'DTc')])
                tk.dma('sp', C['DTAc'][:], DTAd[r0:r0 + 128, :], reads=['DT'], writes=[k_(C, 'DTAc')])
                if full:
                    tk.dma('sp', C['QTc'][:], QT[:, :, r0:r0 + 128].rearrange("h d t -> d h t"), reads=['QK'], writes=[k_(C, 'QTc')])
                    tk.dma('sp', C['SGc'][:], SGd[r0:r0 + 128, :], reads=['TM'], writes=[k_(C, 'SGc')])
                    tk.dma('sp', C['SZc'][:], SZd[r0:r0 + 128, :], reads=['TM'], writes=[k_(C, 'SZc')])
                    tk.dma('sp', C['BTc'][:], BTd[:, :, r0:r0 + 128].rearrange("g n t -> n g t"), reads=['BC'], writes=[k_(C, 'BTc')])
                    tk.dma('sp', C['CTc'][:], CTd[:, :, r0:r0 + 128].rearrange("g n t -> n g t"), reads=['BC'], writes=[k_(C, 'CTc')])
                    tk.dma('sp', C['SBi'][:].rearrange("p h d -> p (h d)"), SBst[c], reads=['SBst'], writes=[(k_(C, 'SBi'), h) for h in range(RET_H)])
                    tk.dma('sp', C['HBi'][:].rearrange("p g d -> p (g d)"), HBst[c], reads=['HBst'], writes=[k_(C, 'HBi')])

            def ret_state_update(C, S, sname, h, kdcol, cdcol):
                par = h % 2
                pa, pb = (0, 2) if par == 0 else (4, 7)
                KX, kxn = KXs[par], f'KX{par}'
                KTc, Vc = C['KTc'], C['Vc']
                tk.op('pe', lambda e: e.transpose(out=PS[pa][:, 0:128], in_=KTc[:, h, :], identity=identf[:]), reads=[k_(C, 'KTc'), 'identf'], writes=[f'ps{pa}'])
                yield
                tk.op('dve', lambda e: e.tensor_scalar(out=KX[:], in0=PS[pa][:, 0:128], scalar1=KD[:, kdcol:kdcol + 1], scalar2=None, op0=ALU.mult),
                      reads=[f'ps{pa}', 'KD'], writes=[kxn])
                yield
                tk.op('pe', lambda e: e.matmul(PS[pb][:, 0:128], lhsT=KX[:], rhs=Vc[:, h * 128:(h + 1) * 128], start=True, stop=True),
                      reads=[kxn, k_(C, 'Vc')], writes=[f'ps{pb}'])
                yield
                tk.op('dve', lambda e: e.scalar_tensor_tensor(out=S[:, h, :], in0=S[:, h, :], scalar=CD[:, cdcol:cdcol + 1], in1=PS[pb][:, 0:128],
                                                             op0=ALU.mult, op1=ALU.add), reads=[(sname, h), 'CD', f'ps{pb}'], writes=[(sname, h)])
                yield

            def ssd_cols(C, dirs):
                DTAc, DTc = C['DTAc'], C['DTc']
                for d in dirs:
                    tri = TRIF if d == 0 else TRIB
                    tk.op('pe', lambda e, d=d, tri=tri: e.matmul(PS[3][:, d * 16:(d + 1) * 16], lhsT=tri[:], rhs=DTAc[:, d * 16:(d + 1) * 16], start=True, stop=True),
                          reads=['TRIF', 'TRIB', k_(C, 'DTAc')], writes=['ps3'])
                    tk.op('pe', lambda e, d=d: e.matmul(PS[3][:, 32 + d * 16:32 + (d + 1) * 16], lhsT=ONES[:], rhs=DTAc[:, d * 16:(d + 1) * 16], start=True, stop=True),
                          reads=['ONES', k_(C, 'DTAc')], writes=['ps3'])
                lo, hi = min(dirs) * 16, (max(dirs) + 1) * 16
                tk.op('act', lambda e: e.copy(out=CUM[:, lo:hi], in_=PS[3][:, lo:hi]), reads=['ps3'], writes=['CUM'])
                tk.op('act', lambda e: e.copy(out=TOT[:, lo:hi], in_=PS[3][:, 32 + lo:32 + hi]), reads=['ps3'], writes=['TOT'])
                tk.op('act', lambda e: e.activation(out=ECUM[:, lo:hi], in_=CUM[:, lo:hi], func=AF.Exp), reads=['CUM'], writes=['ECUM'])
                tk.op('act', lambda e: e.activation(out=ETOT[:, lo:hi], in_=TOT[:, lo:hi], func=AF.Exp), reads=['TOT'], writes=['ETOT'])
                tk.op('dve', lambda e: e.tensor_tensor(out=WG[:, lo:hi], in0=TOT[:, lo:hi], in1=CUM[:, lo:hi], op=ALU.subtract), reads=['TOT', 'CUM'], writes=['WG'])
                tk.op('act', lambda e: e.activation(out=WG[:, lo:hi], in_=WG[:, lo:hi], func=AF.Exp), reads=['WG'], writes=['WG'])
                tk.op('dve', lambda e: e.tensor_tensor(out=WG[:, lo:hi], in0=WG[:, lo:hi], in1=DTc[:, lo:hi], op=ALU.mult), reads=['WG', k_(C, 'DTc')], writes=['WG'])

            def ssd_state_update(C, H, hname, g, d):
                col = d * 16 + g * 8
                XW, xwn = XWs[g], f'XW{g}'
                pb = 5 if g == 0 else 6
                XSc, BKc = C['XSc'], C['BKc']
                xs3 = XSc[:, g * 512:(g + 1) * 512].rearrange("p (e q) -> p e q", q=64)
                xw3 = XW[:].rearrange("p (e q) -> p e q", q=64)
                tk.op('dve', lambda e: e.tensor_tensor(out=xw3, in0=xs3, in1=bc8(WG[:, col:col + 8]), op=ALU.mult), reads=[k_(C, 'XSc'), 'WG'], writes=[xwn])
                yield
                tk.op('pe', lambda e: e.matmul(PS[pb][:, :], lhsT=BKc[:, g * 128:(g + 1) * 128], rhs=XW[:], start=True, stop=True),
                      reads=[k_(C, 'BKc'), xwn], writes=[f'ps{pb}'])
                yield
                h3 = H[:, g, :].rearrange("p (e q) -> p e q", q=64)
                tk.op('dve', lambda e: e.tensor_tensor(out=h3, in0=h3, in1=bc8(ETOT[:, col:col + 8]), op=ALU.mult), reads=[(hname, g), 'ETOT'], writes=[(hname, g)])
                yield
                tk.op('dve', lambda e: e.tensor_tensor(out=H[:, g, :], in0=H[:, g, :], in1=PS[pb][:, :], op=ALU.add), reads=[(hname, g), f'ps{pb}'], writes=[(hname, g)])
                yield

            if stop == 'sc_const':
                end_scope(es)
                return
            SBs = sb("SBs", [128, RET_H, 128]); HBs = sb("HBs", [128, 2, 512])
            tk.op('pool', lambda e: e.memset(SBs[:], 0.0), writes=[('SBs', h) for h in range(RET_H)])
            tk.op('pool', lambda e: e.memset(HBs[:], 0.0), writes=[('HBs', g) for g in range(2)])
            tk.op('pool', lambda e: e.memset(SF[:], 0.0), writes=[('SF', h) for h in range(RET_H)])
            tk.op('pool', lambda e: e.memset(HF[:], 0.0), writes=[('HF', g) for g in range(2)])
            bchain = list(range(NCT - 1, -1, -1)) + list(range(NT - 1, NCT - 1, -1))
            load_chunk(CH[0], bchain[0], False)
            for ci, c in enumerate(bchain):
                C = CH[ci % 2]
                if ci + 1 < len(bchain):
                    load_chunk(CH[(ci + 1) % 2], bchain[ci + 1], False)
                tk.dma('pool', SBst[c], SBs[:].rearrange("p h d -> p (h d)"), reads=[('SBs', h) for h in range(RET_H)], writes=['SBst'])
                tk.dma('pool', HBst[c], HBs[:].rearrange("p g d -> p (g d)"), reads=[('HBs', g) for g in range(2)], writes=['HBst'])
                ada_tick()
                ssd_cols(C, [1])
                interleave([ssd_state_update(C, HBs, 'HBs', g, 1) for g in range(2)], 2)
                interleave([ret_state_update(C, SBs, 'SBs', h, 8 + h, 8 + h) for h in range(RET_H)], 2)
            if stop == 'e2':
                end_scope(es)
                return

            def ret_head(C, h):
                par = h % 2
                pa, pb = (0, 1) if par == 0 else (4, 5)
                PT, QF, QB = PTs[par], QFs[par], QBs[par]
                ptn, qfn, qbn = f'PT{par}', f'QF{par}', f'QB{par}'
                KTc, QTc, Vc, SBin = C['KTc'], C['QTc'], C['Vc'], C['SBi']
                tk.op('pe', lambda e: e.matmul(PS[pa][:, 0:128], lhsT=KTc[:, h, :], rhs=QTc[:, h, :], start=True, stop=True),
                      reads=[k_(C, 'KTc'), k_(C, 'QTc')], writes=[f'ps{pa}'])
                tk.op('pool', lambda e: e.tensor_tensor(out=QF[:], in0=QTc[:, h, :], in1=DQF[:, h, :], op=ALU.mult), reads=[k_(C, 'QTc'), 'DQF'], writes=[qfn])
                yield
                tk.op('dve', lambda e: e.tensor_tensor(out=PT[:], in0=PS[pa][:, 0:128], in1=DC[:, h, :], op=ALU.mult), reads=[f'ps{pa}', 'DC'], writes=[ptn])
                tk.op('pool', lambda e: e.tensor_tensor(out=QB[:], in0=QTc[:, h, :], in1=DQB[:, h, :], op=ALU.mult), reads=[k_(C, 'QTc'), 'DQB'], writes=[qbn])
                yield
                tk.op('pe', lambda e: e.matmul(PS[pb][:, 0:128], lhsT=PT[:], rhs=Vc[:, h * 128:(h + 1) * 128], start=True, stop=False),
                      reads=[ptn, k_(C, 'Vc')], writes=[f'ps{pb}'])
                tk.op('pe', lambda e: e.matmul(PS[pb][:, 0:128], lhsT=QF[:], rhs=SF[:, h, :], start=False, stop=False),
                      reads=[qfn, ('SF', h)], writes=[f'ps{pb}'])
                tk.op('pe', lambda e: e.matmul(PS[pb][:, 0:128], lhsT=QB[:], rhs=SBin[:, h, :], start=False, stop=True),
                      reads=[qbn, (k_(C, 'SBi'), h)], writes=[f'ps{pb}'])
                yield
                tk.op('act', lambda e: e.copy(out=O8[:, h, :], in_=PS[pb][:, 0:128]), reads=[f'ps{pb}'], writes=[('O8', h)])
                yield
                yield from ret_state_update(C, SF, 'SF', h, h, h)

            def ssd_head(C, g, e_):
                idx = g * 8 + e_
                par = idx % 2
                o = par * 256
                EF, EB, WT = EFs[par], EBs[par], WTs[par]
                efn, ebn, wtn = f'EF{par}', f'EB{par}', f'WT{par}'
                DTc, XSc = C['DTc'], C['XSc']
                if par == 0:
                    q = idx // 2
                    zq = ZH[q // 4][:, 4 * (q % 4):4 * (q % 4) + 4, :]
                    tk.op('pe', lambda e: e.matmul(PS[4][:, :], lhsT=ONES[:], rhs=zq.rearrange("p a t -> p (a t)"),
                                                   start=True, stop=True), reads=['ONES', ZK[q // 4]], writes=['ps4'])
                yield
                tk.op('dve', lambda e: e.scalar_tensor_tensor(out=EF[:], in0=PS[4][:, o:o + 128], scalar=CUM[:, idx:idx + 1], in1=MF[:],
                                                             op0=ALU.subtract, op1=ALU.add), reads=['ps4', 'CUM', 'MF'], writes=[efn])
                tk.op('dve', lambda e: e.scalar_tensor_tensor(out=EB[:], in0=PS[4][:, o + 128:o + 256], scalar=CUM[:, 16 + idx:17 + idx], in1=MB[:],
                                                             op0=ALU.subtract, op1=ALU.add), reads=['ps4', 'CUM', 'MB'], writes=[ebn])
                yield
                tk.op('act', lambda e: e.activation(out=EF[:], in_=EF[:], func=AF.Exp), reads=[efn], writes=[efn])
                tk.op('act', lambda e: e.activation(out=EB[:], in_=EB[:], func=AF.Exp), reads=[ebn], writes=[ebn])
                yield
                tk.op('pool', lambda e: e.tensor_scalar(out=EF[:], in0=EF[:], scalar1=DTc[:, idx:idx + 1], scalar2=None, op0=ALU.mult),
                      reads=[efn, k_(C, 'DTc')], writes=[efn])
                yield
                tk.op('dve', lambda e: e.scalar_tensor_tensor(out=EB[:], in0=EB[:], scalar=DTc[:, 16 + idx:17 + idx], in1=EF[:],
                                                             op0=ALU.mult, op1=ALU.add), reads=[ebn, k_(C, 'DTc'), efn], writes=[ebn])
                yield
                tk.op('pool', lambda e: e.tensor_tensor(out=WT[:], in0=EB[:], in1=CBs[:], op=ALU.mult), reads=[ebn, 'CBs'], writes=[wtn])
                yield
                tk.op('pe', lambda e: e.matmul(PS[6][:, e_ * 64:(e_ + 1) * 64], lhsT=WT[:], rhs=XSc[:, idx * 64:(idx + 1) * 64], start=True, stop=True),
                      reads=[wtn, k_(C, 'XSc')], writes=['ps6'])
                yield

            load_chunk(CH[0], 0, True)
            for c in range(NT):
                r0 = c * 128
                C = CH[c % 2]
                if c + 1 < NT:
                    load_chunk(CH[(c + 1) % 2], c + 1, True)
                ada_tick()
                KTc, QTc, Vc, XSc, SGc, SZc, BTc, CTc, DTc, DTAc, HBin = (C[n] for n in ('KTc', 'QTc', 'Vc', 'XSc', 'SGc', 'SZc', 'BTc', 'CTc', 'DTc', 'DTAc', 'HBi'))
                ssd_cols(C, [0, 1])
                for zh in range(2):
                    z4 = ZH[zh].rearrange("p (i d) t -> p i d t", d=2)
                    for d in range(2):
                        tri = TRIF if d == 0 else TRIB
                        c0 = d * 16 + zh * 8
                        tk.op('dve', lambda e, d=d, tri=tri, z4=z4, c0=c0: e.tensor_tensor(
                            out=z4[:, :, d, :], in0=tri[:].unsqueeze(1).to_broadcast([128, 8, 128]),
                            in1=DTAc[:, c0:c0 + 8].unsqueeze(2).to_broadcast([128, 8, 128]), op=ALU.mult),
                            reads=['TRIF', 'TRIB', k_(C, 'DTAc')], writes=[ZK[zh]])
                interleave([ret_head(C, h) for h in range(RET_H)], 2)
                o8k = [('O8', h) for h in range(RET_H)]
                tk.op('dve', lambda e: e.reduce_sum(out=S8[:, 0:8], in_=O8[:], axis=AX.X), reads=o8k, writes=['S8'])
                tk.op('dve', lambda e: e.tensor_scalar(out=S8[:, 0:8], in0=S8[:, 0:8], scalar1=1.0 / 128, scalar2=None, op0=ALU.mult), reads=['S8'], writes=['S8'])
                tk.op('dve', lambda e: e.tensor_tensor(out=D8[:], in0=O8[:], in1=S8[:, 0:8].unsqueeze(2).to_broadcast([128, 8, 128]), op=ALU.subtract),
                      reads=o8k + ['S8'], writes=['D8'])
                tk.op('pool', lambda e: e.tensor_tensor(out=O8[:], in0=D8[:], in1=D8[:], op=ALU.mult), reads=['D8'], writes=o8k)
                tk.op('dve', lambda e: e.reduce_sum(out=S8[:, 8:16], in_=O8[:], axis=AX.X), reads=o8k, writes=['S8'])
                tk.op('dve', lambda e: e.tensor_scalar(out=S8[:, 8:16], in0=S8[:, 8:16], scalar1=1.0 / 128, scalar2=EPS, op0=ALU.mult, op1=ALU.add), reads=['S8'], writes=['S8'])
                tk.op('act', lambda e: e.activation(out=S8[:, 16:24], in_=S8[:, 8:16], func=AF.Sqrt), reads=['S8'], writes=['S8'])
                tk.op('dve', lambda e: e.reciprocal(out=S8[:, 24:32], in_=S8[:, 16:24]), reads=['S8'], writes=['S8'])
                tk.op('dve', lambda e: e.tensor_tensor(out=D8[:], in0=D8[:], in1=S8[:, 24:32].unsqueeze(2).to_broadcast([128, 8, 128]), op=ALU.mult),
                      reads=['D8', 'S8'], writes=['D8'])
                d8f = D8[:].rearrange("p h d -> p (h d)")
                tk.op('pool', lambda e: e.tensor_tensor(out=d8f, in0=d8f, in1=GNG[:], op=ALU.mult), reads=['D8', 'GNG'], writes=['D8'])
                tk.op('dve', lambda e: e.tensor_tensor(out=MB16[:, 0:1024], in0=d8f, in1=SGc[:], op=ALU.mult), reads=['D8', k_(C, 'SGc')], writes=[('MB16', 0)])
                if stop == 'e3ret':
                    continue
                for g in range(2):
                    tk.op('pe', lambda e, g=g: e.matmul(PS[5][:, 0:128], lhsT=BTc[:, g, :], rhs=CTc[:, g, :], start=True, stop=True),
                          reads=[k_(C, 'BTc'), k_(C, 'CTc')], writes=['ps5'])
                    tk.op('act', lambda e: e.copy(out=CBs[:], in_=PS[5][:, 0:128]), reads=['ps5'], writes=['CBs'])
                    interleave([ssd_head(C, g, e_) for e_ in range(8)], 2)
                    yg = YG[:, g * 512:(g + 1) * 512]
                    yt3 = YT[:].rearrange("p (e q) -> p e q", q=64)
                    tk.op('act', lambda e, yg=yg: e.copy(out=yg, in_=PS[6][:, :]), reads=['ps6'], writes=[('YG', g)])
                    for d, H, hn in ((0, HF, ('HF', g)), (1, HBin, k_(C, 'HBi'))):
                        col = d * 16 + g * 8
                        tk.op('pe', lambda e, g=g, H=H: e.matmul(PS[7][:, :], lhsT=CTc[:, g, :], rhs=H[:, g, :], start=True, stop=True),
                              reads=[k_(C, 'CTc'), hn], writes=['ps7'])
                        tk.op('dve', lambda e, col=col: e.tensor_tensor(out=yt3, in0=PS[7][:, :].rearrange("p (e q) -> p e q", q=64), in1=bc8(ECUM[:, col:col + 8]), op=ALU.mult),
                              reads=['ps7', 'ECUM'], writes=['YT'])
                        tk.op('pool', lambda e, yg=yg: e.tensor_tensor(out=yg, in0=yg, in1=YT[:], op=ALU.add), reads=[('YG', g), 'YT'], writes=[('YG', g)])
                    for _ in ssd_state_update(C, HF, 'HF', g, 0):
                        pass
                ygk = [('YG', 0), ('YG', 1)]
                xs3a = XSc[:].rearrange("p (i q) -> p i q", q=64)
                tk.op('dve', lambda e: e.tensor_tensor(out=xs3a, in0=xs3a, in1=DSK[:].unsqueeze(2).to_broadcast([128, 16, 64]), op=ALU.mult),
                      reads=[k_(C, 'XSc'), 'DSK'], writes=[k_(C, 'XSc')])
                tk.op('pool', lambda e: e.tensor_tensor(out=YG[:], in0=YG[:], in1=XSc[:], op=ALU.add), reads=ygk + [k_(C, 'XSc')], writes=ygk)
                tk.op('dve', lambda e: e.tensor_tensor(out=YG[:], in0=YG[:], in1=SZc[:], op=ALU.mult), reads=ygk + [k_(C, 'SZc')], writes=ygk)
                tk.op('pool', lambda e: e.memset(ST[:, 0:1], 0.0), writes=['ST'])
                tk.op('act', lambda e: e.activation(out=XSc[:], in_=YG[:], func=AF.Square, accum_out=ST[:, 0:1]), reads=ygk + ['ST'], writes=[k_(C, 'XSc'), 'ST'])
                rstd_from_sumsq(1024)
                tk.op('dve', lambda e: e.scalar_tensor_tensor(out=MB16[:, 1024:2048], in0=YG[:], scalar=ST[:, 2:3], in1=SNG[:], op0=ALU.mult, op1=ALU.mult),
                      reads=ygk + ['ST', 'SNG'], writes=[('MB16', 1)])
                if debug:
                    tk.op('dve', lambda e: e.tensor_copy(out=SCR[:], in_=MB16[:]), reads=[('MB16', 0), ('MB16', 1)], writes=['SCR'])
                    tk.dma('pool', MIXd[r0:r0 + 128, :], SCR[:], reads=['SCR'], writes=['MIXd'])
                for k4 in range(KC // 4):
                    pi = k4 % 2
                    pname = f'ps{6 + pi}'
                    for k in range(4):
                        kc = k4 * 4 + k
                        tk.op('pe', lambda e, kc=kc, k=k, pi=pi: e.transpose(
                            out=PST[pi][:, k * 128:(k + 1) * 128], in_=MB16[:, kc * 128:(kc + 1) * 128], identity=ident[:]),
                            reads=[('MB16', kc // 8), 'ident'], writes=[pname])
                    tk.op('act', lambda e, k4=k4, pi=pi: e.copy(
                        out=MTc[:, k4 * 4:(k4 + 1) * 4, :], in_=PST[pi][:, 0:512].rearrange("p (k t) -> p k t", k=4)),
                        reads=[pname], writes=['MTc'])
                tk.dma('pool', MTd[:, :, r0:r0 + 128].rearrange("k p t -> p k t"), MTc[:], reads=['MTc'], writes=['MTd'])
            for _ in ada_it:
                pass
            end_scope(es)

        def e4_phase(l, with_ctx):
            m = l // 2
            currow = None
            for (t0, nt, row) in blocks(with_ctx):
                ntok = nt * 128
                if row != currow:
                    load_consts(l, 1, 1, row, 1.0)
                    currow = row
                for j in range(nt):
                    load_x(j, t0 + j * 128)
                tk.dma('sp', B['AT'][:, 0:KC, 0:ntok], MTd[:, :, t0:t0 + ntok].rearrange("k p t -> p k t"), reads=['MTd'], writes=['AT'])
                down_proj(lambda db: mwo_b[m, db], KC, lambda db: need([('mw', m, 5)]), nt)
                cast_tick(2)
                store_block(t0, nt)

        stop = cfg.get('stop')
        if do_mix is True:
            rope_tables()
        last_even = DEPTH - 1 - (DEPTH - 1) % 2
        for l in range(DEPTH if stop != 'rope' else 0):
            ctx_in = l <= last_even
            ctx_out = l < last_even
            esA = scope_a()
            ffn_phase(l, 0, ctx_in)
            if l % 2 == 1 and do_mix:
                sconv_phase(l, ctx_out)
            elif l % 2 == 0 and do_mix is True:
                esC, sbc = mk_scope()
                mixer_consts(l // 2, sbc)
                if stop != 'ffn':
                    e1_phase(l)
                end_scope(esC)
                end_scope(esA)
                if stop in ('e1', 'ffn'):
                    return nc
                scan_phases(l)
                if stop in ('scan', 'sc_const', 'e2', 'e3ret', 'e3z', 'e3g', 'e3post'):
                    return nc
                esA = scope_a()
                e4_phase(l, ctx_out)
            ffn_phase(l, 1, ctx_out)
            end_scope(esA)

        esA = scope_a()
        tk.dma('act', GS[:], final_g.partition_broadcast(128), writes=['GS'])
        for t in range(NCT, NT):
            j = t % TB
            norm_tile(j, t * 128, modulate=False)
            tk.dma('pool', out[(t - NCT) * 128:(t - NCT + 1) * 128, :], SCR[:], reads=['SCR'], writes=['out'])
        tk.barrier()
        esA.close()
    return nc


def prep_inputs(cfg, b, x, c, ctx, c_ctx, ada_w, ada_b, norm_g, final_g, ffn_up, ffn_down, **kw):
    DEPTH, DFF, SEQ, CTX = cfg['depth'], cfg['dff'], cfg['seq'], cfg['ctx']
    NH = DFF // 128
    T = SEQ + CTX
    f32 = np.float32
    xin = np.concatenate([ctx[b], x[b]], axis=0)
    cc = np.stack([c[b], c_ctx], axis=0)
    cT = np.ascontiguousarray(cc.reshape(2, KC, 128).transpose(2, 1, 0))
    up = ffn_up[:DEPTH].reshape(DEPTH, 2, KC, 128, 2, NH, 128)
    wup = np.ascontiguousarray(up.transpose(0, 1, 5, 3, 2, 4, 6)).reshape(DEPTH, 2, NH, 128, KC * 256)
    dn = ffn_down[:DEPTH].reshape(DEPTH, 2, NH, 128, D // 256, 256)
    wdn = np.ascontiguousarray(dn.transpose(0, 1, 4, 3, 2, 5)).reshape(DEPTH, 2, D // 256, 128, NH * 256)
    N_ODD = max(DEPTH // 2, 1)
    N_EVEN = (DEPTH + 1) // 2
    sc_w_in, sc_conv_w, sc_w_out = kw['sc_w_in'], kw['sc_conv_w'], kw['sc_w_out']
    wi = sc_w_in[:N_ODD].reshape(N_ODD, KC, 128, 3, KC, 128)
    scw_cu = np.ascontiguousarray(wi[:, :, :, 1:3].transpose(0, 4, 2, 1, 3, 5)).reshape(N_ODD, KC, 128, KC * 256)
    scw_b = np.ascontiguousarray(wi[:, :, :, 0].transpose(0, 3, 2, 1, 4)).reshape(N_ODD, KC, 128, KC * 128)
    wo = sc_w_out[:N_ODD].reshape(N_ODD, KC, 128, D // 256, 256)
    scwo = np.ascontiguousarray(wo.transpose(0, 3, 2, 1, 4)).reshape(N_ODD, D // 256, 128, KC * 256)
    sccw = np.ascontiguousarray(sc_conv_w[:N_ODD].reshape(N_ODD, 3, KC, 128).transpose(0, 3, 2, 1))
    mw = kw['mix_w_in'][:N_EVEN]

    def fm_slot(cols):
        w = mw[:, :, cols].reshape(N_EVEN, KC, 128, 256)
        return w.transpose(0, 2, 1, 3).reshape(N_EVEN, 128, KC * 256)
    slots = []
    for base in (0, 1024):
        for pr in range(4):
            h0, h1 = 2 * pr, 2 * pr + 1
            cols = np.concatenate([base + h0 * 128 + np.arange(64), base + h1 * 128 + np.arange(64),
                                   base + h0 * 128 + 64 + np.arange(64), base + h1 * 128 + 64 + np.arange(64)])
            slots.append(fm_slot(cols))
    for s in range(6):
        slots.append(fm_slot(5120 + s * 256 + np.arange(256)))
    wfm = np.ascontiguousarray(np.stack(slots, axis=1))
    tms = []
    for blk in range(6):
        w = mw[:, :, 2048 + blk * 512:2048 + (blk + 1) * 512].reshape(N_EVEN, KC, 128, 512)
        tms.append(w.transpose(0, 2, 1, 3).reshape(N_EVEN, 128, KC * 512))
    wtm = np.ascontiguousarray(np.stack(tms, axis=1))
    wdt = np.ascontiguousarray(mw[:, :, 6656:6688].reshape(N_EVEN, KC, 128, 32).transpose(0, 2, 1, 3)).reshape(N_EVEN, 128, KC * 32)
    mo = kw['mix_w_out'][:N_EVEN].reshape(N_EVEN, KC, 128, D // 256, 256)
    mwo = np.ascontiguousarray(mo.transpose(0, 3, 2, 1, 4)).reshape(N_EVEN, D // 256, 128, KC * 256)
    mcw = np.ascontiguousarray(kw['mix_conv_w'][:N_EVEN].reshape(N_EVEN, 3, 12, 128).transpose(0, 3, 2, 1))
    mcb = np.ascontiguousarray(kw['mix_conv_b'][:N_EVEN].reshape(N_EVEN, 12, 128).transpose(0, 2, 1))
    tl = np.arange(SEQ)
    pos = np.zeros((T, 3), f32)
    pos[:CTX, 0] = np.arange(CTX)
    pos[CTX:, 0] = CTX
    pos[CTX:, 1] = tl // 64
    pos[CTX:, 2] = tl % 64
    axis = np.concatenate([np.zeros(16, int), np.ones(24, int), 2 * np.ones(24, int)])
    inv = np.concatenate([10000.0 ** (-np.arange(n, dtype=f32) / n) for n in (16, 24, 24)]).astype(f32)
    pax64 = pos[:, axis].T
    pax = np.ascontiguousarray(np.concatenate([pax64, pax64], axis=0)).astype(f32)
    inv128 = np.concatenate([inv, inv]).reshape(128, 1).astype(f32)
    iota = (np.arange(128)[None, :] - np.arange(128)[:, None]).astype(f32)
    return {
        "scw_cu": scw_cu, "scw_b": scw_b, "scwo": scwo, "sccw": sccw,
        "wfm": wfm, "wtm": wtm, "wdt": wdt, "mwo": mwo, "mcw": mcw, "mcb": mcb,
        "rld": np.ascontiguousarray(kw['ret_log_decay'][:N_EVEN].reshape(N_EVEN, 1, 16)),
        "gng": np.ascontiguousarray(kw['ret_gn_g'][:N_EVEN].reshape(N_EVEN, 1, 1024)),
        "alog": np.ascontiguousarray(kw['ssd_a_log'][:N_EVEN].reshape(N_EVEN, 1, 32)),
        "dtb": np.ascontiguousarray(kw['ssd_dt_bias'][:N_EVEN].reshape(N_EVEN, 1, 32)),
        "ssdd": np.ascontiguousarray(kw['ssd_d'][:N_EVEN].reshape(N_EVEN, 1, 16)),
        "sng": np.ascontiguousarray(kw['ssd_norm_g'][:N_EVEN].reshape(N_EVEN, 1, 1024)),
        "pax": pax, "inv": inv128, "iota_in": iota,
        "xin": np.ascontiguousarray(xin), "cT": cT,
        "ada_w": np.ascontiguousarray(ada_w[:DEPTH]), "ada_b": np.ascontiguousarray(ada_b[:DEPTH]),
        "norm_g": np.ascontiguousarray(norm_g[:DEPTH]), "final_g": np.ascontiguousarray(final_g.reshape(1, D)),
        "wup": wup, "wdn": wdn, "ident_in": np.eye(128, dtype=np.float32),
    }


def run(cfg, inputs, full=False):
    nc = build(cfg)
    nb = inputs['x'].shape[0]
    in_maps = [prep_inputs(cfg, b, **inputs) for b in range(nb)]
    res = run_bass_kernel_spmd(nc, in_maps, core_ids=list(range(nb)))
    if full:
        return res.results
    return np.stack([r["out"] for r in res.results], axis=0)


def kernel(**inputs):
    inputs = {k: np.asarray(v) for k, v in inputs.items()}
    cfg = dict(depth=4, seq=4096, ctx=256, dff=5632)
    return run(cfg, inputs).astype(np.float32)
```

```python
import numpy as np
import concourse.bass as bass
import concourse.mybir as mybir
from concourse.bass_utils import run_bass_kernel_spmd

F32 = mybir.dt.float32
BF16 = mybir.dt.bfloat16
AF = mybir.ActivationFunctionType
ALU = mybir.AluOpType
AX = mybir.AxisListType

D = 2048
KC = D // 128
EPS = 1e-6
N_MOD = 9


class TK:
    NS = {'sp': 16, 'pool': 8, 'act': 8}

    def __init__(self, nc):
        self.nc = nc
        self.names = ['pe', 'act', 'dve', 'pool', 'sp']
        self.q = {k: [] for k in self.names}
        self.cnt = {k: 0 for k in self.names}
        self.dcnt = {k: 0 for k in self.NS}
        self.waited = {k: {} for k in self.names}
        self.lastw = {}
        self.reads = {}
        self.semh = {}
        self.engs = {'pe': nc.tensor, 'act': nc.scalar, 'dve': nc.vector, 'pool': nc.gpsimd, 'sp': nc.sync}

    def semkeys(self):
        ks = list(self.names)
        for qn, n in self.NS.items():
            ks += [(qn, i) for i in range(n)]
        return ks

    def _deps(self, eng, reads, writes):
        deps = {}

        def add(ev):
            if ev is None:
                return
            k, v = ev
            if k == 'pe' and eng == 'pe':
                return
            if deps.get(k, 0) < v:
                deps[k] = v
        for r in reads:
            add(self.lastw.get(r))
        for w in writes:
            add(self.lastw.get(w))
            for k, v in self.reads.get(w, {}).items():
                add((k, v))
        out = []
        wd = self.waited[eng]
        for k, v in deps.items():
            if wd.get(k, 0) < v:
                wd[k] = v
                out.append((k, v))
        return out

    def _record(self, ev, reads, writes):
        for w in writes:
            self.lastw[w] = ev
            self.reads[w] = {}
        for r in reads:
            if r in writes:
                continue
            d = self.reads.setdefault(r, {})
            if d.get(ev[0], 0) < ev[1]:
                d[ev[0]] = ev[1]

    def op(self, eng, fn, reads=(), writes=()):
        waits = self._deps(eng, reads, writes)
        self.cnt[eng] += 1
        ev = (eng, self.cnt[eng])
        self._record(ev, reads, writes)
        semh = self.semh

        e = self.engs[eng]
        for k, v in waits:
            e.wait_ge(semh[k], v)
        fn(e).then_inc(semh[eng], 1)
        return ev

    def dma(self, qn, out, in_, reads=(), writes=(), **kw):
        waits = self._deps(qn, reads, writes)
        i = self.dcnt[qn]
        self.dcnt[qn] += 1
        ns = self.NS[qn]
        key = (qn, i % ns)
        val = 16 * (i // ns + 1)
        if i >= ns and self.waited[qn].get(key, 0) < val - 16:
            self.waited[qn][key] = val - 16
            waits.append((key, val - 16))
        ev = (key, val)
        self._record(ev, reads, writes)
        semh = self.semh

        e = self.engs[qn]
        for k, v in waits:
            e.wait_ge(semh[k], v)
        e.dma_start(out=out, in_=in_, **kw).then_inc(semh[key], 16)
        return ev

    def wait_all(self, eng):
        evs = {}
        for ev in self.lastw.values():
            if evs.get(ev[0], 0) < ev[1]:
                evs[ev[0]] = ev[1]
        semh = self.semh
        lst = list(evs.items())

        e = self.engs[eng]
        wd = self.waited[eng]
        for k, v in lst:
            if wd.get(k, 0) < v:
                wd[k] = v
                e.wait_ge(semh[k], v)

    def barrier(self):
        for eng in self.names:
            self.wait_all(eng)


RET_H = 8
SSD_H = 16
XBC_W = 1536
BIG = 30000.0
import contextlib
import math


def build(cfg):
    DEPTH, SEQ, CTX, DFF = cfg['depth'], cfg['seq'], cfg['ctx'], cfg['dff']
    do_mix = cfg.get('mix', True)
    debug = cfg.get('debug', False)
    T = SEQ + CTX
    NT = T // 128
    NCT = CTX // 128
    NH = DFF // 128
    NHA = max(NH, 32)
    NDB = D // 256
    N_ODD = max(DEPTH // 2, 1)
    N_EVEN = (DEPTH + 1) // 2
    nc = bass.Bass("TRN2", target_bir_lowering=False)
    tk = TK(nc)

    def dram_in(name, shape, dt=F32):
        return nc.dram_tensor(name, list(shape), dt, kind="ExternalInput").ap()

    def dram(name, shape, dt=F32, dbg=False):
        if dbg and debug:
            return nc.dram_tensor(name, list(shape), dt, kind="ExternalOutput").ap()
        return nc.dram_tensor(name, list(shape), dt).ap()

    xin = dram_in("xin", [T, D])
    cT = dram_in("cT", [128, KC, 2])
    ada_w = dram_in("ada_w", [DEPTH, D, N_MOD * D])
    ada_b = dram_in("ada_b", [DEPTH, N_MOD * D])
    norm_g = dram_in("norm_g", [DEPTH, 3, D])
    final_g = dram_in("final_g", [1, D])
    wup_in = dram_in("wup", [DEPTH, 2, NH, 128, KC * 256])
    wdn_in = dram_in("wdn", [DEPTH, 2, NDB, 128, NH * 256])
    ident_in = dram_in("ident_in", [128, 128])
    iota_in = dram_in("iota_in", [128, 128])
    scw_cu_in = dram_in("scw_cu", [N_ODD, KC, 128, KC * 256])
    scw_b_in = dram_in("scw_b", [N_ODD, KC, 128, KC * 128])
    scwo_in = dram_in("scwo", [N_ODD, NDB, 128, KC * 256])
    sccw_in = dram_in("sccw", [N_ODD, 128, KC, 3])
    wfm_in = dram_in("wfm", [N_EVEN, 14, 128, KC * 256])
    wtm_in = dram_in("wtm", [N_EVEN, 6, 128, KC * 512])
    wdt_in = dram_in("wdt", [N_EVEN, 128, KC * 32])
    mwo_in = dram_in("mwo", [N_EVEN, NDB, 128, KC * 256])
    mcw_in = dram_in("mcw", [N_EVEN, 128, 12, 3])
    mcb_in = dram_in("mcb", [N_EVEN, 128, 12])
    rld_in = dram_in("rld", [N_EVEN, 1, 16])
    gng_in = dram_in("gng", [N_EVEN, 1, 1024])
    alog_in = dram_in("alog", [N_EVEN, 1, 32])
    dtb_in = dram_in("dtb", [N_EVEN, 1, 32])
    ssdd_in = dram_in("ssdd", [N_EVEN, 1, 16])
    sng_in = dram_in("sng", [N_EVEN, 1, 1024])
    pax_in = dram_in("pax", [128, T])
    inv_in = dram_in("inv", [128, 1])
    out = nc.dram_tensor("out", [SEQ, D], F32, kind="ExternalOutput").ap()

    X = dram("X", [T, D])
    MOD = dram("MOD", [2, DEPTH * N_MOD * D])
    wup_b = [[dram(f"wup_b{l}_{i}", [NH, 128, KC * 256], BF16) for i in range(2)] for l in range(DEPTH)]
    wdn_b = [[dram(f"wdn_b{l}_{i}", [NDB, 128, NH * 256], BF16) for i in range(2)] for l in range(DEPTH)]
    scw_cu_b = dram("scw_cu_b", [N_ODD, KC, 128, KC * 256], BF16)
    scw_b_b = dram("scw_b_b", [N_ODD, KC, 128, KC * 128], BF16)
    scwo_b = dram("scwo_b", [N_ODD, NDB, 128, KC * 256], BF16)
    wfm_b = dram("wfm_b", [N_EVEN, 14, 128, KC * 256], BF16)
    wtm_b = dram("wtm_b", [N_EVEN, 6, 128, KC * 512], BF16)
    wdt_b = dram("wdt_b", [N_EVEN, 128, KC * 32], BF16)
    mwo_b = dram("mwo_b", [N_EVEN, NDB, 128, KC * 256], BF16)
    ROPE = dram("ROPE", [4, 128, T], dbg=True)
    QT = dram("QT", [RET_H, 128, T], dbg=True)
    KT = dram("KT", [RET_H, 128, T], dbg=True)
    Vd = dram("Vd", [T, 1024], dbg=True)
    SGd = dram("SGd", [T, 1024], dbg=True)
    SZd = dram("SZd", [T, 1024], dbg=True)
    XSd = dram("XSd", [T, 1024], dbg=True)
    BTOK = dram("BTOK", [T, 256], dbg=True)
    BTd = dram("BTd", [2, 128, T], dbg=True)
    CTd = dram("CTd", [2, 128, T], dbg=True)
    DTd = dram("DTd", [T, 32], dbg=True)
    DTAd = dram("DTAd", [T, 32], dbg=True)
    SBst = dram("SBst", [NT, 128, RET_H * 128])
    HBst = dram("HBst", [NT, 128, 1024])
    MTd = dram("MTd", [KC, 128, T], BF16)
    MIXd = dram("MIXd", [T, D], dbg=True)

    uid = [0]
    B = {}

    def mk_scope():
        es = contextlib.ExitStack()

        def sb(name, shape, dt=F32):
            uid[0] += 1
            t = es.enter_context(nc.sbuf_tensor(f"{name}_{uid[0]}", list(shape), dt))
            B[name] = t
            return t
        return es, sb

    TB = 4
    NTOK = TB * 128
    with contextlib.ExitStack() as es0:
        for k in tk.semkeys():
            nm = k if isinstance(k, str) else f"{k[0]}{k[1]}"
            tk.semh[k] = es0.enter_context(nc.semaphore("s_" + nm))

        def sb0(name, shape, dt=F32):
            return es0.enter_context(nc.sbuf_tensor(name, list(shape), dt))

        ident = sb0("ident", [128, 128], BF16)
        identf = sb0("identf", [128, 128], F32)
        GS = sb0("GS", [128, D])
        SH = sb0("SH", [128, D])
        GM = sb0("GM", [128, D])
        SCR = sb0("SCR", [128, D])
        HBL = [sb0("HB", [128, D], BF16), sb0("HB1", [128, D], BF16)]
        ST = sb0("ST", [128, 8])
        CW = sb0("CW", [128, KC, 3])
        cs = sb0("cs", [128, KC, 2])
        csg = sb0("csg", [128, KC, 2])
        ADB = sb0("ADB", [2, 512])
        ADO = sb0("ADO", [2, 512])
        IOT = sb0("IOT", [128, 128])
        PS = [es0.enter_context(nc.psum_tensor(f"ps{i}", [128, 512], F32)) for i in range(8)]
        PST = [PS[6][:].bitcast(BF16), PS[7][:].bitcast(BF16)]

        def scope_a():
            es, sb = mk_scope()
            sb("XT", [128, TB, D])
            sb("HT", [128, KC, NTOK], BF16)
            sb("AT", [128, NHA, NTOK], BF16)
            for i in range(2):
                sb(f"WUP{i}", [128, KC, 256], BF16)
                sb(f"WDN{i}", [128, NHA, 256], BF16)
                sb(f"SG{i}", [128, NTOK])
            return es

        def end_scope(es):
            tk.barrier()
            es.close()

        tk.dma('sp', identf[:], ident_in, writes=['identf'])
        tk.op('dve', lambda e: e.tensor_copy(out=ident[:], in_=identf[:]), reads=['identf'], writes=['ident'])
        tk.dma('sp', IOT[:], iota_in, writes=['IOT'])
        tk.dma('sp', X, xin, writes=['X'])
        cast_jobs = []
        GU = 11

        def add_ffn_casts(l, i):
            for h0 in range(0, NH, GU):
                h1 = min(NH, h0 + GU)
                cast_jobs.append((('wupb', l, i, h0), wup_b[l][i][h0:h1], wup_in[l, i, h0:h1]))
            for d0 in range(0, NDB, 2):
                cast_jobs.append((('wdnb', l, i, d0), wdn_b[l][i][d0:d0 + 2], wdn_in[l, i, d0:d0 + 2]))
        for l in range(DEPTH):
            add_ffn_casts(l, 0)
            m = l // 2
            if l % 2 == 1:
                cast_jobs.append((('scw', m, 0), scw_cu_b[m], scw_cu_in[m]))
                cast_jobs.append((('scw', m, 1), scw_b_b[m], scw_b_in[m]))
                cast_jobs.append((('scw', m, 2), scwo_b[m], scwo_in[m]))
            elif do_mix is True:
                cast_jobs.append((('mw', m, 0), wfm_b[m, 0:7], wfm_in[m, 0:7]))
                cast_jobs.append((('mw', m, 1), wfm_b[m, 7:14], wfm_in[m, 7:14]))
                cast_jobs.append((('mw', m, 2), wtm_b[m, 0:3], wtm_in[m, 0:3]))
                cast_jobs.append((('mw', m, 3), wtm_b[m, 3:6], wtm_in[m, 3:6]))
                cast_jobs.append((('mw', m, 4), wdt_b[m], wdt_in[m]))
                cast_jobs.append((('mw', m, 5), mwo_b[m], mwo_in[m]))
            add_ffn_casts(l, 1)
        cast_pos = [0]
        cast_done = set()

        def cast_tick(n):
            for _ in range(n):
                if cast_pos[0] < len(cast_jobs):
                    key, o_, i_ = cast_jobs[cast_pos[0]]
                    cast_pos[0] += 1
                    tk.dma('pool', o_, i_, writes=[key])
                    cast_done.add(key)

        def need(keys):
            for k in keys:
                while k not in cast_done:
                    cast_tick(1)
            return list(keys)

        def wdeps(kind, l, i, idx):
            if kind == 'up':
                return need([('wupb', l, i, (idx // GU) * GU)])
            return need([('wdnb', l, i, (idx // 2) * 2)])

        cast_tick(8)

        tk.dma('sp', cs[:], cT, writes=['cs'])
        tk.op('act', lambda e: e.activation(out=csg[:], in_=cs[:], func=AF.Silu), reads=['cs'], writes=['csg'])
        ncb = N_MOD * D // 512

        def ada_gen(layers, AW, pi):
            it = 0
            pn = f'ps{pi}'
            for l in layers:
                for cb in range(ncb):
                    tk.dma('act', ADB[:], ada_b[l:l + 1, cb * 512:(cb + 1) * 512].partition_broadcast(2), writes=['ADB'])
                    for kq in range(4):
                        buf, bname = AW[it % 2]
                        it += 1
                        bv = buf.rearrange("p (k c) -> p k c", k=4)
                        tk.dma('sp', bv, ada_w[l, kq * 512:(kq + 1) * 512, cb * 512:(cb + 1) * 512].rearrange(
                            "(k p) c -> p k c", p=128), writes=[bname])
                        for k in range(4):
                            kc = kq * 4 + k
                            tk.op('pe', lambda e, bv=bv, k=k, kc=kc: e.matmul(
                                PS[pi][0:2, :], lhsT=csg[:, kc, :], rhs=bv[:, k, :], start=(kc == 0), stop=(kc == KC - 1)),
                                reads=['csg', bname], writes=[pn])
                    tk.op('dve', lambda e: e.tensor_tensor(out=ADO[:], in0=PS[pi][0:2, :], in1=ADB[:], op=ALU.add),
                          reads=[pn, 'ADB'], writes=['ADO'])
                    off = l * N_MOD * D + cb * 512
                    tk.dma('act', MOD[:, off:off + 512], ADO[:], reads=['ADO'], writes=['MOD'])
                    yield

        defer_ada = (do_mix is True)
        for _ in ada_gen([0] if defer_ada else list(range(DEPTH)), [(SCR[:], 'SCR'), (GM[:], 'GM')], 0):
            pass

        def modrow(l, j, row):
            off = (l * N_MOD + j) * D
            return MOD[row:row + 1, off:off + D].partition_broadcast(128)

        def load_consts(l, gi, mi, row, gate_scale):
            tk.dma('act', SCR[:], norm_g[l, gi:gi + 1, :].partition_broadcast(128), writes=['SCR'])
            tk.dma('act', GS[:], modrow(l, 3 * mi + 1, row), reads=['MOD'], writes=['GS'])
            tk.dma('act', SH[:], modrow(l, 3 * mi, row), reads=['MOD'], writes=['SH'])
            tk.dma('act', GM[:], modrow(l, 3 * mi + 2, row), reads=['MOD'], writes=['GM'])
            tk.op('dve', lambda e: e.scalar_tensor_tensor(out=GS[:], in0=GS[:], scalar=1.0, in1=SCR[:],
                                                         op0=ALU.add, op1=ALU.mult),
                  reads=['GS', 'SCR'], writes=['GS'])
            if gate_scale != 1.0:
                tk.op('pool', lambda e: e.tensor_scalar(out=GM[:], in0=GM[:], scalar1=gate_scale, scalar2=None,
                                                       op0=ALU.mult), reads=['GM'], writes=['GM'])

        def rstd_from_sumsq(n):
            tk.op('dve', lambda e: e.tensor_scalar(out=ST[:, 1:2], in0=ST[:, 0:1], scalar1=1.0 / n, scalar2=EPS,
                                                  op0=ALU.mult, op1=ALU.add), reads=['ST'], writes=['ST'])
            tk.op('act', lambda e: e.activation(out=ST[:, 3:4], in_=ST[:, 1:2], func=AF.Sqrt), reads=['ST'], writes=['ST'])
            tk.op('dve', lambda e: e.reciprocal(out=ST[:, 2:3], in_=ST[:, 3:4]), reads=['ST'], writes=['ST'])

        def load_x(j, t0):
            tk.dma('sp', B['XT'][:, j, :], X[t0:t0 + 128, :], reads=['X'], writes=[('XT', j)])

        def norm_tile(j, t0, modulate=True):
            XT = B['XT']
            xj = ('XT', j)
            load_x(j, t0)
            tk.op('pool', lambda e: e.memset(ST[:, 0:1], 0.0), writes=['ST'])
            tk.op('act', lambda e: e.activation(out=SCR[:], in_=XT[:, j, :], func=AF.Square, accum_out=ST[:, 0:1]),
                  reads=[xj, 'ST'], writes=['SCR', 'ST'])
            rstd_from_sumsq(D)
            tk.op('dve', lambda e: e.scalar_tensor_tensor(out=SCR[:], in0=XT[:, j, :], scalar=ST[:, 2:3], in1=GS[:],
                                                         op0=ALU.mult, op1=ALU.mult),
                  reads=[xj, 'ST', 'GS'], writes=['SCR'])
            if modulate:
                HB = HBL[j % 2]
                tk.op('dve', lambda e: e.tensor_tensor(out=HB[:], in0=SCR[:], in1=SH[:], op=ALU.add),
                      reads=['SCR', 'SH'], writes=[f'HB{j % 2}'])

        def transpose_hb(dst, j, dname):
            for k4 in range(KC // 4):
                pi = k4 % 2
                pname = f'ps{6 + pi}'
                for k in range(4):
                    kc = k4 * 4 + k
                    tk.op('pe', lambda e, kc=kc, k=k, pi=pi: e.transpose(
                        out=PST[pi][:, k * 128:(k + 1) * 128], in_=HBL[j % 2][:, kc * 128:(kc + 1) * 128], identity=ident[:]),
                        reads=[f'HB{j % 2}', 'ident'], writes=[pname])
                tk.op('act', lambda e, k4=k4, pi=pi: e.copy(
                    out=dst[:, k4 * 4:(k4 + 1) * 4, j * 128:(j + 1) * 128],
                    in_=PST[pi][:, 0:512].rearrange("p (k t) -> p k t", k=4)),
                    reads=[pname], writes=[dname])

        def norm_block(t0, nt):
            for j in range(nt):
                norm_tile(j, t0 + j * 128)
                transpose_hb(B['HT'], j, 'HT')

        wctr = {'up': 0, 'dn': 0}

        def wk(base, s):
            return [f'{base}{s}', f'{base}{s}a', f'{base}{s}b']

        def load_wup(src, deps):
            s = (wctr['up'] // 2) % 2
            wctr['up'] += 2
            tk.dma('sp', B[f'WUP{s}'][:].rearrange("p k c -> p (k c)"), src, reads=need(deps), writes=wk('WUP', s))
            return B[f'WUP{s}'], wk('WUP', s)

        def fm_mm(pi, W, wn, c0, ntok, m=128):
            HT = B['HT']
            for kc in range(KC):
                tk.op('pe', lambda e, kc=kc: e.matmul(
                    PS[pi][0:m, 0:ntok], lhsT=W[:, kc, c0:c0 + m], rhs=HT[:, kc, 0:ntok], start=(kc == 0), stop=(kc == KC - 1)),
                    reads=(wn if isinstance(wn, list) else [wn]) + ['HT'], writes=[f'ps{pi}'])

        def up_ffn(l, ii, ntok):
            AT, HT = B['AT'], B['HT']
            HK = KC // 2
            for hc in range(NH):
                a = 2 * (hc % 2)
                deps = wdeps('up', l, ii, hc)
                for half in range(2):
                    r = wctr['up'] % 4
                    wctr['up'] += 1
                    sl, part = r // 2, r % 2
                    key = f"WUP{sl}{'ab'[part]}"
                    W = B[f'WUP{sl}'][:, part * HK:(part + 1) * HK, :]
                    tk.dma('sp', W, wup_b[l][ii][hc][:, half * HK * 256:(half + 1) * HK * 256].rearrange("p (k c) -> p k c", c=256),
                           reads=deps, writes=[key])
                    for (pi, c0) in ((a, 0), (a + 1, 128)):
                        for k in range(HK):
                            kc = half * HK + k
                            tk.op('pe', lambda e, k=k, kc=kc, pi=pi, c0=c0, W=W: e.matmul(
                                PS[pi][:, 0:ntok], lhsT=W[:, k, c0:c0 + 128], rhs=HT[:, kc, 0:ntok], start=(kc == 0), stop=(kc == KC - 1)),
                                reads=[key, 'HT'], writes=[f'ps{pi}'])
                sg, sgn = B[f'SG{hc % 2}'], f'SG{hc % 2}'
                tk.op('act', lambda e, sg=sg, a=a: e.activation(out=sg[:, 0:ntok], in_=PS[a][:, 0:ntok], func=AF.Silu),
                      reads=[f'ps{a}'], writes=[sgn])
                tk.op('dve', lambda e, sg=sg, a=a, hc=hc: e.tensor_tensor(
                    out=AT[:, hc, 0:ntok], in0=sg[:, 0:ntok], in1=PS[a + 1][:, 0:ntok], op=ALU.mult),
                    reads=[sgn, f'ps{a + 1}'], writes=['AT'])

        def conv3(R, Tt, ntok, rowlen, cw, ci, cwn):
            R3 = R.rearrange("p (r w) -> p r w", w=rowlen)
            T3 = Tt.rearrange("p (r w) -> p r w", w=rowlen)
            tk.op('dve', lambda e: e.tensor_scalar(out=R, in0=Tt, scalar1=cw[:, ci, 1:2], scalar2=None, op0=ALU.mult),
                  reads=['SCR', cwn], writes=['SCR'])
            tk.op('dve', lambda e: e.scalar_tensor_tensor(
                out=R3[:, :, 1:rowlen], in0=T3[:, :, 0:rowlen - 1], scalar=cw[:, ci, 0:1], in1=R3[:, :, 1:rowlen],
                op0=ALU.mult, op1=ALU.add), reads=['SCR', cwn], writes=['SCR'])
            tk.op('dve', lambda e: e.scalar_tensor_tensor(
                out=R3[:, :, 0:rowlen - 1], in0=T3[:, :, 1:rowlen], scalar=cw[:, ci, 2:3], in1=R3[:, :, 0:rowlen - 1],
                op0=ALU.mult, op1=ALU.add), reads=['SCR', cwn], writes=['SCR'])

        def up_sconv(m, ntok, rowlen):
            AT = B['AT']
            R = SCR[:, 0:ntok]
            Tt = SCR[:, 512:512 + ntok]
            for fc in range(KC):
                tk.dma('sp', B['WUP0'][:].rearrange("p k c -> p (k c)"), scw_cu_b[m, fc], reads=need([('scw', m, 0)]), writes=wk('WUP', 0))
                tk.dma('sp', B['WUP1'][:, :, 0:128], scw_b_b[m, fc].rearrange("p (k c) -> p k c", c=128),
                       reads=need([('scw', m, 1)]), writes=wk('WUP', 1))
                fm_mm(0, B['WUP0'], wk('WUP', 0), 0, ntok)
                fm_mm(1, B['WUP0'], wk('WUP', 0), 128, ntok)
                fm_mm(2, B['WUP1'], wk('WUP', 1), 0, ntok)
                sg = B['SG0']
                tk.op('act', lambda e: e.copy(out=sg[:, 0:ntok], in_=PS[0][:, 0:ntok]), reads=['ps0'], writes=['SG0'])
                tk.op('dve', lambda e: e.tensor_tensor(out=Tt, in0=sg[:, 0:ntok], in1=PS[1][:, 0:ntok], op=ALU.mult),
                      reads=['SG0', 'ps1'], writes=['SCR'])
                conv3(R, Tt, ntok, rowlen, CW, fc, 'CW')
                tk.op('dve', lambda e, fc=fc: e.tensor_tensor(out=AT[:, fc, 0:ntok], in0=R, in1=PS[2][:, 0:ntok], op=ALU.mult),
                      reads=['SCR', 'ps2'], writes=['AT'])

        def down_proj(wsrc, nh, depfn, nt):
            AT, XT = B['AT'], B['XT']
            hn = nh // 2
            for db in range(NDB):
                deps = depfn(db)
                pbase = 4 * (db % 2)
                for half in range(2):
                    r = wctr['dn'] % 4
                    wctr['dn'] += 1
                    sl, part = r // 2, r % 2
                    key = f"WDN{sl}{'ab'[part]}"
                    W = B[f'WDN{sl}'][:, part * (NHA // 2):part * (NHA // 2) + hn, :]
                    tk.dma('sp', W, wsrc(db)[:, half * hn * 256:(half + 1) * hn * 256].rearrange("p (h c) -> p h c", c=256),
                           reads=deps, writes=[key])
                    for j in range(nt):
                        pb = pbase + j
                        for h in range(hn):
                            hc = half * hn + h
                            tk.op('pe', lambda e, W=W, h=h, hc=hc, j=j, pb=pb: e.matmul(
                                PS[pb][:, 0:256], lhsT=AT[:, hc, j * 128:(j + 1) * 128], rhs=W[:, h, :],
                                start=(hc == 0), stop=(hc == nh - 1)),
                                reads=[key, 'AT'], writes=[f'ps{pb}'])
                for j in range(nt):
                    pb = pbase + j
                    pn = f'ps{pb}'
                    xj = ('XT', j)
                    sl_ = slice(db * 256, (db + 1) * 256)
                    tk.op('dve', lambda e, pb=pb, sl_=sl_: e.tensor_tensor(
                        out=SCR[:, sl_], in0=PS[pb][:, 0:256], in1=GM[:, sl_], op=ALU.mult),
                        reads=[pn, 'GM'], writes=['SCR'])
                    tk.op('pool', lambda e, j=j, sl_=sl_: e.tensor_tensor(
                        out=XT[:, j, sl_], in0=XT[:, j, sl_], in1=SCR[:, sl_], op=ALU.add),
                        reads=['SCR', xj], writes=[xj])

        def store_block(t0, nt):
            for j in range(nt):
                tk.dma('act', X[t0 + j * 128:t0 + (j + 1) * 128, :], B['XT'][:, j, :], reads=[('XT', j)], writes=['X'])

        def blocks(with_ctx):
            bl = []
            if with_ctx:
                for t in range(0, NCT, TB):
                    bl.append((t * 128, min(TB, NCT - t), 1))
            for t in range(NCT, NT, TB):
                bl.append((t * 128, min(TB, NT - t), 0))
            return bl

        def ffn_phase(l, i, with_ctx):
            mi = 0 if i == 0 else 2
            ii = 0 if i == 0 else 1
            currow = None
            for (t0, nt, row) in blocks(with_ctx):
                if row != currow:
                    load_consts(l, mi, mi, row, 0.5)
                    currow = row
                norm_block(t0, nt)
                up_ffn(l, ii, nt * 128)
                down_proj(lambda db: wdn_b[l][ii][db], NH, lambda db: wdeps('dn', l, ii, db), nt)
                store_block(t0, nt)
                cast_tick(2)

        def sconv_phase(l, with_ctx):
            m = l // 2
            tk.dma('act', CW[:], sccw_in[m], writes=['CW'])
            currow = None
            for (t0, nt, row) in blocks(with_ctx):
                if row != currow:
                    load_consts(l, 1, 1, row, 1.0)
                    currow = row
                norm_block(t0, nt)
                up_sconv(m, nt * 128, 64 if row == 0 else nt * 128)
                down_proj(lambda db: scwo_b[m, db], KC, lambda db: need([('scw', m, 2)]), nt)
                store_block(t0, nt)
                cast_tick(2)

        def rope_tables():
            es, sb = mk_scope()
            PAX = sb("PAX", [128, 512])
            ANG = sb("ANG", [128, 512])
            TMP = sb("TMP", [128, 512])
            RES = sb("RES", [128, 4, 512])
            INV = sb("INV", [128, 1])
            YI = sb("YI", [128, 512], mybir.dt.int32)
            YF = sb("YF", [128, 512])
            tk.dma('sp', INV[:], inv_in, writes=['INV'])
            for c0 in range(0, T, 512):
                n = min(512, T - c0)
                tk.dma('sp', PAX[:, 0:n], pax_in[:, c0:c0 + n], writes=['PAX'])
                tk.op('dve', lambda e: e.tensor_scalar(out=ANG[:, 0:n], in0=PAX[:, 0:n], scalar1=INV[:, 0:1], scalar2=None,
                                                      op0=ALU.mult), reads=['PAX', 'INV'], writes=['ANG'])
                for (ri, shift) in ((0, 0.25), (1, 0.0)):
                    tk.op('dve', lambda e, shift=shift: e.tensor_scalar(
                        out=TMP[:, 0:n], in0=ANG[:, 0:n], scalar1=1.0 / (2 * math.pi), scalar2=shift, op0=ALU.mult, op1=ALU.add),
                        reads=['ANG'], writes=['TMP'])
                    tk.op('dve', lambda e: e.tensor_copy(out=YI[:, 0:n], in_=TMP[:, 0:n]), reads=['TMP'], writes=['YI'])
                    tk.op('dve', lambda e: e.tensor_copy(out=YF[:, 0:n], in_=YI[:, 0:n]), reads=['YI'], writes=['YF'])
                    tk.op('dve', lambda e: e.tensor_tensor(out=TMP[:, 0:n], in0=TMP[:, 0:n], in1=YF[:, 0:n], op=ALU.subtract),
                          reads=['TMP', 'YF'], writes=['TMP'])
                    tk.op('dve', lambda e: e.tensor_scalar(out=YF[:, 0:n], in0=TMP[:, 0:n], scalar1=0.5, scalar2=None, op0=ALU.is_gt),
                          reads=['TMP'], writes=['YF'])
                    tk.op('dve', lambda e: e.tensor_tensor(out=TMP[:, 0:n], in0=TMP[:, 0:n], in1=YF[:, 0:n], op=ALU.subtract),
                          reads=['TMP', 'YF'], writes=['TMP'])
                    tk.op('act', lambda e, ri=ri: e.activation(out=RES[:, ri, 0:n], in_=TMP[:, 0:n], func=AF.Sin,
                                                               scale=2 * math.pi - 1e-5),
                          reads=['TMP'], writes=['RES'])
                    tk.op('pool', lambda e, ri=ri: e.tensor_scalar(
                        out=RES[:, 2 + ri, 0:n], in0=RES[:, ri, 0:n], scalar1=128 ** -0.5, scalar2=None, op0=ALU.mult),
                        reads=['RES'], writes=['RES'])
                tk.dma('pool', ROPE[:, :, c0:c0 + n].rearrange("r p t -> p r t"), RES[:, :, 0:n], reads=['RES'], writes=['ROPE'])
            end_scope(es)

        def e1_phase(l):
            m = l // 2
            MCW = B['MCW']
            currow = None
            for (t0, nt, row) in blocks(True):
                ntok = nt * 128
                rowlen = 64 if row == 0 else ntok
                if row != currow:
                    load_consts(l, 1, 1, row, 1.0)
                    currow = row
                norm_block(t0, nt)
                WS = B['AT'][:].rearrange("p h t -> p (h t)").bitcast(F32)
                tabs = [WS[:, i * 512:i * 512 + ntok] for i in range(4)]
                for i in range(4):
                    tk.dma('sp', tabs[i], ROPE[i, :, t0:t0 + ntok], reads=['ROPE'], writes=['AT'])
                SG0, SG1 = B['SG0'], B['SG1']
                for slot in range(8):
                    W, wn = load_wup(wfm_b[m, slot], [('mw', m, 0), ('mw', m, 1)])
                    fm_mm(0, W, wn, 0, ntok)
                    fm_mm(1, W, wn, 128, ntok)
                    cos, sin = (tabs[0], tabs[1]) if slot < 4 else (tabs[2], tabs[3])
                    for (oi, ta, tb, op) in ((0, cos, sin, ALU.subtract), (1, sin, cos, ALU.add)):
                        tk.op('dve', lambda e, ta=ta: e.tensor_tensor(out=SG0[:, 0:ntok], in0=PS[0][:, 0:ntok], in1=ta, op=ALU.mult),
                              reads=['ps0', 'AT'], writes=['SG0'])
                        tk.op('dve', lambda e, tb=tb: e.tensor_tensor(out=SG1[:, 0:ntok], in0=PS[1][:, 0:ntok], in1=tb, op=ALU.mult),
                              reads=['ps1', 'AT'], writes=['SG1'])
                        tk.op('pool', lambda e, oi=oi, op=op: e.tensor_tensor(
                            out=SCR[:, oi * 512:oi * 512 + ntok], in0=SG0[:, 0:ntok], in1=SG1[:, 0:ntok], op=op),
                            reads=['SG0', 'SG1'], writes=['SCR'])
                    dst = QT if slot < 4 else KT
                    h0 = 2 * (slot % 4)
                    for hh in range(2):
                        for half in range(2):
                            tk.dma('act', dst[h0 + hh, half * 64:(half + 1) * 64, t0:t0 + ntok],
                                   SCR[hh * 64:(hh + 1) * 64, half * 512:half * 512 + ntok], reads=['SCR'], writes=['QK'])
                for slot in range(6):
                    W, wn = load_wup(wfm_b[m, 8 + slot], [('mw', m, 1)])
                    for half in range(2):
                        ch = 2 * slot + half
                        fm_mm(half, W, wn, half * 128, ntok)
                        Tt = SCR[:, 512:512 + ntok]
                        R = SCR[:, 0:ntok]
                        A = SCR[:, 1024:1024 + ntok]
                        tk.op('act', lambda e, half=half: e.copy(out=Tt, in_=PS[half][:, 0:ntok]), reads=[f'ps{half}'], writes=['SCR'])
                        conv3(R, Tt, ntok, rowlen, MCW, ch, 'MCW')
                        tk.op('act', lambda e, ch=ch: e.activation(out=A, in_=R, func=AF.Silu, bias=B['MCB'][:, ch:ch + 1], scale=1.0),
                              reads=['SCR', 'MCB'], writes=['SCR'])
                        if ch >= 8:
                            g = (ch - 8) % 2
                            dst = BTd if ch < 10 else CTd
                            tk.dma('act', dst[g, :, t0:t0 + ntok], A, reads=['SCR'], writes=['BC'])
                        if ch < 10:
                            for j in range(nt):
                                tk.op('pe', lambda e, j=j: e.transpose(out=PS[2][:, j * 128:(j + 1) * 128], in_=A[:, j * 128:(j + 1) * 128],
                                                                       identity=identf[:]), reads=['SCR', 'identf'], writes=['ps2'])
                            tk.op('act', lambda e: e.copy(out=SG0[:, 0:ntok], in_=PS[2][:, 0:ntok]), reads=['ps2'], writes=['SG0'])
                            for j in range(nt):
                                r0 = t0 + j * 128
                                if ch < 8:
                                    tk.dma('act', XSd[r0:r0 + 128, ch * 128:(ch + 1) * 128], SG0[:, j * 128:(j + 1) * 128], reads=['SG0'], writes=['XS'])
                                else:
                                    tk.dma('act', BTOK[r0:r0 + 128, (ch - 8) * 128:(ch - 7) * 128], SG0[:, j * 128:(j + 1) * 128], reads=['SG0'], writes=['XS'])
                HT = B['HT']
                for blk in range(6):
                    s = (wctr['dn'] // 2) % 2
                    wctr['dn'] += 2
                    wn = f'WDN{s}'
                    W = B[wn][:].rearrange("p h c -> p (h c)")[:, 0:KC * 512].rearrange("p (k c) -> p k c", c=512)
                    tk.dma('sp', W, wtm_b[m, blk].rearrange("p (k c) -> p k c", c=512), reads=need([('mw', m, 2), ('mw', m, 3)]), writes=wk('WDN', s))
                    dst = (Vd, SGd, SZd)[blk // 2]
                    for j in range(nt):
                        pb = 4 + j % 2
                        for kc in range(KC):
                            tk.op('pe', lambda e, kc=kc, j=j, pb=pb, W=W: e.matmul(
                                PS[pb][:, :], lhsT=HT[:, kc, j * 128:(j + 1) * 128], rhs=W[:, kc, :], start=(kc == 0), stop=(kc == KC - 1)),
                                reads=wk('WDN', s) + ['HT'], writes=[f'ps{pb}'])
                        sg, sgn = B[f'SG{j % 2}'], f'SG{j % 2}'
                        if blk < 2:
                            tk.op('act', lambda e, sg=sg, pb=pb: e.copy(out=sg[:], in_=PS[pb][:, :]), reads=[f'ps{pb}'], writes=[sgn])
                        else:
                            tk.op('act', lambda e, sg=sg, pb=pb: e.activation(out=sg[:], in_=PS[pb][:, :], func=AF.Silu), reads=[f'ps{pb}'], writes=[sgn])
                        r0 = t0 + j * 128
                        tk.dma('pool', dst[r0:r0 + 128, (blk % 2) * 512:(blk % 2 + 1) * 512], sg[:], reads=[sgn], writes=['TM'])
                W = B['WUP0'][:].rearrange("p k c -> p (k c)")[:, 0:KC * 32].rearrange("p (k c) -> p k c", c=32)
                tk.dma('sp', W, wdt_b[m].rearrange("p (k c) -> p k c", c=32), reads=need([('mw', m, 4)]), writes=wk('WUP', 0))
                for j in range(nt):
                    for kc in range(KC):
                        tk.op('pe', lambda e, kc=kc, j=j: e.matmul(
                            PS[3][:, 0:32], lhsT=HT[:, kc, j * 128:(j + 1) * 128], rhs=W[:, kc, :], start=(kc == 0), stop=(kc == KC - 1)),
                            reads=wk('WUP', 0) + ['HT'], writes=['ps3'])
                    DTt = B['DTT']
                    tk.op('dve', lambda e: e.tensor_tensor(out=DTt[:, 0:32], in0=PS[3][:, 0:32], in1=B['DTB'][:], op=ALU.add),
                          reads=['ps3', 'DTB'], writes=['DTT'])
                    tk.op('act', lambda e: e.activation(out=DTt[:, 32:64], in_=DTt[:, 0:32], func=AF.Exp), reads=['DTT'], writes=['DTT'])
                    tk.op('act', lambda e: e.activation(out=DTt[:, 64:96], in_=DTt[:, 32:64], func=AF.Ln, bias=B['ONE1'][:, 0:1], scale=1.0),
                          reads=['DTT', 'ONE1'], writes=['DTT'])
                    tk.op('dve', lambda e: e.tensor_tensor(out=DTt[:, 96:128], in0=DTt[:, 64:96], in1=B['A32'][:], op=ALU.mult),
                          reads=['DTT', 'A32'], writes=['DTT'])
                    r0 = t0 + j * 128
                    tk.dma('pool', DTd[r0:r0 + 128, :], DTt[:, 64:96], reads=['DTT'], writes=['DT'])
                    tk.dma('pool', DTAd[r0:r0 + 128, :], DTt[:, 96:128], reads=['DTT'], writes=['DT'])

        def mixer_consts(m, sb):
            MCW = sb("MCW", [128, 12, 3])
            MCB = sb("MCB", [128, 12])
            DTB = sb("DTB", [128, 32])
            A32 = sb("A32", [128, 32])
            ONE1 = sb("ONE1", [128, 1])
            DTT = sb("DTT", [128, 128])
            tk.dma('act', MCW[:], mcw_in[m], writes=['MCW'])
            tk.dma('act', MCB[:], mcb_in[m], writes=['MCB'])
            tk.dma('act', DTB[:], dtb_in[m].partition_broadcast(128), writes=['DTB'])
            tk.dma('act', A32[:], alog_in[m].partition_broadcast(128), writes=['A32'])
            tk.op('act', lambda e: e.activation(out=A32[:], in_=A32[:], func=AF.Exp), reads=['A32'], writes=['A32'])
            tk.op('dve', lambda e: e.tensor_scalar(out=A32[:], in0=A32[:], scalar1=-1.0, scalar2=None, op0=ALU.mult),
                  reads=['A32'], writes=['A32'])
            tk.op('pool', lambda e: e.memset(ONE1[:], 1.0), writes=['ONE1'])

        def scan_phases(l):
            m = l // 2
            es, sb = mk_scope()
            TRIF = sb("TRIF", [128, 128]); TRIB = sb("TRIB", [128, 128])
            MF = sb("MF", [128, 128]); MB = sb("MB", [128, 128]); ONES = sb("ONES", [128, 128])
            LG = sb("LG", [128, 16]); CD = sb("CD", [128, 16]); KD = sb("KD", [128, 16])
            DC = sb("DC", [128, RET_H, 128]); DQF = sb("DQF", [128, RET_H, 128]); DQB = sb("DQB", [128, RET_H, 128])
            GNG = sb("GNG", [128, 1024]); SNG = sb("SNG", [128, 1024]); DSK = sb("DSK", [128, 16])
            TMPA = sb("TMPA", [128, 128]); TMPB = sb("TMPB", [128, 128]); PCOL = sb("PCOL", [128, 2])
            ONE1 = sb("ONE1", [128, 1])
            tk.op('pool', lambda e: e.memset(ONE1[:], 1.0), writes=['ONE1'])
            tk.op('pool', lambda e: e.memset(ONES[:], 1.0), writes=['ONES'])
            tk.op('dve', lambda e: e.tensor_scalar(out=TRIF[:], in0=IOT[:], scalar1=0.0, scalar2=None, op0=ALU.is_ge), reads=['IOT'], writes=['TRIF'])
            tk.op('dve', lambda e: e.tensor_scalar(out=TRIB[:], in0=IOT[:], scalar1=0.0, scalar2=None, op0=ALU.is_le), reads=['IOT'], writes=['TRIB'])
            tk.op('dve', lambda e: e.tensor_scalar(out=MF[:], in0=IOT[:], scalar1=0.0, scalar2=-BIG, op0=ALU.is_lt, op1=ALU.mult), reads=['IOT'], writes=['MF'])
            tk.op('dve', lambda e: e.tensor_scalar(out=MB[:], in0=IOT[:], scalar1=0.0, scalar2=-BIG, op0=ALU.is_gt, op1=ALU.mult), reads=['IOT'], writes=['MB'])
            tk.dma('act', LG[:], rld_in[m].partition_broadcast(128), writes=['LG'])
            tk.dma('act', GNG[:], gng_in[m].partition_broadcast(128), writes=['GNG'])
            tk.dma('act', SNG[:], sng_in[m].partition_broadcast(128), writes=['SNG'])
            tk.dma('act', DSK[:], ssdd_in[m].partition_broadcast(128), writes=['DSK'])
            tk.op('act', lambda e: e.activation(out=CD[:], in_=LG[:], func=AF.Exp, scale=128.0), reads=['LG'], writes=['CD'])
            tk.op('dve', lambda e: e.tensor_copy(out=PCOL[:, 0:1], in_=IOT[:, 127:128]), reads=['IOT'], writes=['PCOL'])
            tk.op('dve', lambda e: e.tensor_scalar(out=PCOL[:, 1:2], in0=IOT[:, 0:1], scalar1=-1.0, scalar2=None, op0=ALU.mult), reads=['IOT'], writes=['PCOL'])
            tk.op('dve', lambda e: e.tensor_scalar(out=KD[:, 0:8], in0=LG[:, 0:8], scalar1=PCOL[:, 0:1], scalar2=None, op0=ALU.mult), reads=['LG', 'PCOL'], writes=['KD'])
            tk.op('dve', lambda e: e.tensor_scalar(out=KD[:, 8:16], in0=LG[:, 8:16], scalar1=PCOL[:, 1:2], scalar2=None, op0=ALU.mult), reads=['LG', 'PCOL'], writes=['KD'])
            tk.op('act', lambda e: e.activation(out=KD[:], in_=KD[:], func=AF.Exp), reads=['KD'], writes=['KD'])
            TPOS = sb("TPOS", [128, 128])
            tk.op('dve', lambda e: e.tensor_scalar(out=TPOS[:], in0=IOT[:], scalar1=PCOL[:, 1:2], scalar2=None, op0=ALU.add), reads=['IOT', 'PCOL'], writes=['TPOS'])
            for h in range(RET_H):
                tk.op('dve', lambda e, h=h: e.scalar_tensor_tensor(out=TMPA[:], in0=IOT[:], scalar=LG[:, h:h + 1], in1=MF[:], op0=ALU.mult, op1=ALU.add),
                      reads=['IOT', 'LG', 'MF'], writes=['TMPA'])
                tk.op('act', lambda e: e.activation(out=TMPA[:], in_=TMPA[:], func=AF.Exp), reads=['TMPA'], writes=['TMPA'])
                tk.op('dve', lambda e, h=h: e.tensor_scalar(out=TMPB[:], in0=IOT[:], scalar1=LG[:, 8 + h:9 + h], scalar2=-1.0, op0=ALU.mult, op1=ALU.mult),
                      reads=['IOT', 'LG'], writes=['TMPB'])
                tk.op('dve', lambda e: e.tensor_tensor(out=TMPB[:], in0=TMPB[:], in1=MB[:], op=ALU.add), reads=['TMPB', 'MB'], writes=['TMPB'])
                tk.op('act', lambda e: e.activation(out=TMPB[:], in_=TMPB[:], func=AF.Exp), reads=['TMPB'], writes=['TMPB'])
                tk.op('dve', lambda e, h=h: e.tensor_tensor(out=DC[:, h, :], in0=TMPA[:], in1=TMPB[:], op=ALU.add), reads=['TMPA', 'TMPB'], writes=['DC'])
                tk.op('dve', lambda e, h=h: e.tensor_scalar(out=TMPA[:], in0=TPOS[:], scalar1=1.0, scalar2=LG[:, h:h + 1], op0=ALU.add, op1=ALU.mult),
                      reads=['TPOS', 'LG'], writes=['TMPA'])
                tk.op('act', lambda e, h=h: e.activation(out=DQF[:, h, :], in_=TMPA[:], func=AF.Exp), reads=['TMPA'], writes=['DQF'])
                tk.op('dve', lambda e, h=h: e.tensor_scalar(out=TMPB[:], in0=TPOS[:], scalar1=-128.0, scalar2=LG[:, 8 + h:9 + h], op0=ALU.add, op1=ALU.mult),
                      reads=['TPOS', 'LG'], writes=['TMPB'])
                tk.op('act', lambda e, h=h: e.activation(out=DQB[:, h, :], in_=TMPB[:], func=AF.Exp, scale=-1.0), reads=['TMPB'], writes=['DQB'])
            CH = []
            for par in range(2):
                d = {}
                for nm, shp in (("KTc", [128, RET_H, 128]), ("QTc", [128, RET_H, 128]), ("Vc", [128, 1024]), ("XSc", [128, 1024]),
                                ("SGc", [128, 1024]), ("SZc", [128, 1024]), ("BTc", [128, 2, 128]), ("CTc", [128, 2, 128]),
                                ("BKc", [128, 256]), ("DTc", [128, 32]), ("DTAc", [128, 32]),
                                ("SBi", [128, RET_H, 128]), ("HBi", [128, 2, 512])):
                    d[nm] = sb(f"{nm}{par}", shp)
                d['par'] = par
                CH.append(d)
            SF = sb("SF", [128, RET_H, 128]); HF = sb("HF", [128, 2, 512])
            KXs = [sb(f"KX{i}", [128, 128]) for i in range(2)]; PTs = [sb(f"PT{i}", [128, 128]) for i in range(2)]
            QFs = [sb(f"QF{i}", [128, 128]) for i in range(2)]; QBs = [sb(f"QB{i}", [128, 128]) for i in range(2)]
            ada_it = ada_gen([x for x in (l + 1, l + 2) if x < DEPTH] if defer_ada else [], [(GS[:], 'GS'), (SH[:], 'SH')], 3)

            def ada_tick(n=1):
                for _ in range(n):
                    next(ada_it, None)
            O8 = sb("O8", [128, RET_H, 128]); D8 = sb("D8", [128, RET_H, 128]); S8 = sb("S8", [128, 32])
            CUM = sb("CUM", [128, 32]); TOT = sb("TOT", [128, 32]); ECUM = sb("ECUM", [128, 32]); ETOT = sb("ETOT", [128, 32]); WG = sb("WG", [128, 32])
            ZH = [GM[:].rearrange("p (a t) -> p a t", t=128), SCR[:].rearrange("p (a t) -> p a t", t=128)]
            ZK = ['GM', 'SCR']
            CBs = sb("CBs", [128, 128])
            EFs = [sb(f"EF{i}", [128, 128]) for i in range(2)]; EBs = [sb(f"EB{i}", [128, 128]) for i in range(2)]
            WTs = [sb(f"WT{i}", [128, 128]) for i in range(2)]
            XWs = [sb(f"XW{i}", [128, 512]) for i in range(2)]
            YG = sb("YG", [128, 1024]); YT = sb("YT", [128, 512])
            MB16 = sb("MB16", [128, D], BF16); MTc = sb("MTc", [128, KC, 128], BF16)

            def bc8(ap8):
                return ap8.unsqueeze(2).to_broadcast([128, 8, 64])

            def interleave(gens, width):
                gens = iter(gens)
                active = []
                while True:
                    while len(active) < width:
                        g = next(gens, None)
                        if g is None:
                            break
                        active.append(g)
                    if not active:
                        break
                    for g in list(active):
                        try:
                            next(g)
                        except StopIteration:
                            active.remove(g)

            def k_(C, nm):
                return f"{nm}{C['par']}"

            def load_chunk(C, c, full, sbsrc=None):
                r0 = c * 128
                tk.dma('sp', C['KTc'][:], KT[:, :, r0:r0 + 128].rearrange("h d t -> d h t"), reads=['QK'], writes=[k_(C, 'KTc')])
                tk.dma('sp', C['Vc'][:], Vd[r0:r0 + 128, :], reads=['TM'], writes=[k_(C, 'Vc')])
                tk.dma('sp', C['XSc'][:], XSd[r0:r0 + 128, :], reads=['XS'], writes=[k_(C, 'XSc')])
                tk.dma('sp', C['BKc'][:], BTOK[r0:r0 + 128, :], reads=['XS'], writes=[k_(C, 'BKc')])
                tk.dma('sp', C['DTc'][:], DTd[r0:r0 + 128, :], reads=['DT'], writes=[k_(C, 'DTc')])
                tk.dma('sp', C['DTAc'][:], DTAd[r0:r0 + 128, :], reads=['DT'], writes=[k_(C, 'DTAc')])
                if full:
                    tk.dma('sp', C['QTc'][:], QT[:, :, r0:r0 + 128].rearrange("h d t -> d h t"), reads=['QK'], writes=[k_(C, 'QTc')])
                    tk.dma('sp', C['SGc'][:], SGd[r0:r0 + 128, :], reads=['TM'], writes=[k_(C, 'SGc')])
                    tk.dma('sp', C['SZc'][:], SZd[r0:r0 + 128, :], reads=['TM'], writes=[k_(C, 'SZc')])
                    tk.dma('sp', C['BTc'][:], BTd[:, :, r0:r0 + 128].rearrange("g n t -> n g t"), reads=['BC'], writes=[k_(C, 'BTc')])
                    tk.dma('sp', C['CTc'][:], CTd[:, :, r0:r0 + 128].rearrange("g n t -> n g t"), reads=['BC'], writes=[k_(C, 'CTc')])
                    tk.dma('sp', C['SBi'][:].rearrange("p h d -> p (h d)"), SBst[c], reads=['SBst'], writes=[(k_(C, 'SBi'), h) for h in range(RET_H)])
                    tk.dma('sp', C['HBi'][:].rearrange("p g d -> p (g d)"), HBst[c], reads=['HBst'], writes=[k_(C, 'HBi')])

            def ret_state_update(C, S, sname, h, kdcol, cdcol):
                par = h % 2
                pa, pb = (0, 2) if par == 0 else (4, 7)
                KX, kxn = KXs[par], f'KX{par}'
                KTc, Vc = C['KTc'], C['Vc']
                tk.op('pe', lambda e: e.transpose(out=PS[pa][:, 0:128], in_=KTc[:, h, :], identity=identf[:]), reads=[k_(C, 'KTc'), 'identf'], writes=[f'ps{pa}'])
                yield
                tk.op('dve', lambda e: e.tensor_scalar(out=KX[:], in0=PS[pa][:, 0:128], scalar1=KD[:, kdcol:kdcol + 1], scalar2=None, op0=ALU.mult),
                      reads=[f'ps{pa}', 'KD'], writes=[kxn])
                yield
                tk.op('pe', lambda e: e.matmul(PS[pb][:, 0:128], lhsT=KX[:], rhs=Vc[:, h * 128:(h + 1) * 128], start=True, stop=True),
                      reads=[kxn, k_(C, 'Vc')], writes=[f'ps{pb}'])
                yield
                tk.op('dve', lambda e: e.scalar_tensor_tensor(out=S[:, h, :], in0=S[:, h, :], scalar=CD[:, cdcol:cdcol + 1], in1=PS[pb][:, 0:128],
                                                             op0=ALU.mult, op1=ALU.add), reads=[(sname, h), 'CD', f'ps{pb}'], writes=[(sname, h)])
                yield

            def ssd_cols(C, dirs):
                DTAc, DTc = C['DTAc'], C['DTc']
                for d in dirs:
                    tri = TRIF if d == 0 else TRIB
                    tk.op('pe', lambda e, d=d, tri=tri: e.matmul(PS[3][:, d * 16:(d + 1) * 16], lhsT=tri[:], rhs=DTAc[:, d * 16:(d + 1) * 16], start=True, stop=True),
                          reads=['TRIF', 'TRIB', k_(C, 'DTAc')], writes=['ps3'])
                    tk.op('pe', lambda e, d=d: e.matmul(PS[3][:, 32 + d * 16:32 + (d + 1) * 16], lhsT=ONES[:], rhs=DTAc[:, d * 16:(d + 1) * 16], start=True, stop=True),
                          reads=['ONES', k_(C, 'DTAc')], writes=['ps3'])
                lo, hi = min(dirs) * 16, (max(dirs) + 1) * 16
                tk.op('act', lambda e: e.copy(out=CUM[:, lo:hi], in_=PS[3][:, lo:hi]), reads=['ps3'], writes=['CUM'])
                tk.op('act', lambda e: e.copy(out=TOT[:, lo:hi], in_=PS[3][:, 32 + lo:32 + hi]), reads=['ps3'], writes=['TOT'])
                tk.op('act', lambda e: e.activation(out=ECUM[:, lo:hi], in_=CUM[:, lo:hi], func=AF.Exp), reads=['CUM'], writes=['ECUM'])
                tk.op('act', lambda e: e.activation(out=ETOT[:, lo:hi], in_=TOT[:, lo:hi], func=AF.Exp), reads=['TOT'], writes=['ETOT'])
                tk.op('dve', lambda e: e.tensor_tensor(out=WG[:, lo:hi], in0=TOT[:, lo:hi], in1=CUM[:, lo:hi], op=ALU.subtract), reads=['TOT', 'CUM'], writes=['WG'])
                tk.op('act', lambda e: e.activation(out=WG[:, lo:hi], in_=WG[:, lo:hi], func=AF.Exp), reads=['WG'], writes=['WG'])
                tk.op('dve', lambda e: e.tensor_tensor(out=WG[:, lo:hi], in0=WG[:, lo:hi], in1=DTc[:, lo:hi], op=ALU.mult), reads=['WG', k_(C, 'DTc')], writes=['WG'])

            def ssd_state_update(C, H, hname, g, d):
                col = d * 16 + g * 8
                XW, xwn = XWs[g], f'XW{g}'
                pb = 5 if g == 0 else 6
                XSc, BKc = C['XSc'], C['BKc']
                xs3 = XSc[:, g * 512:(g + 1) * 512].rearrange("p (e q) -> p e q", q=64)
                xw3 = XW[:].rearrange("p (e q) -> p e q", q=64)
                tk.op('dve', lambda e: e.tensor_tensor(out=xw3, in0=xs3, in1=bc8(WG[:, col:col + 8]), op=ALU.mult), reads=[k_(C, 'XSc'), 'WG'], writes=[xwn])
                yield
                tk.op('pe', lambda e: e.matmul(PS[pb][:, :], lhsT=BKc[:, g * 128:(g + 1) * 128], rhs=XW[:], start=True, stop=True),
                      reads=[k_(C, 'BKc'), xwn], writes=[f'ps{pb}'])
                yield
                h3 = H[:, g, :].rearrange("p (e q) -> p e q", q=64)
                tk.op('dve', lambda e: e.tensor_tensor(out=h3, in0=h3, in1=bc8(ETOT[:, col:col + 8]), op=ALU.mult), reads=[(hname, g), 'ETOT'], writes=[(hname, g)])
                yield
                tk.op('dve', lambda e: e.tensor_tensor(out=H[:, g, :], in0=H[:, g, :], in1=PS[pb][:, :], op=ALU.add), reads=[(hname, g), f'ps{pb}'], writes=[(hname, g)])
                yield

            if stop == 'sc_const':
                end_scope(es)
                return
            SBs = sb("SBs", [128, RET_H, 128]); HBs = sb("HBs", [128, 2, 512])
            tk.op('pool', lambda e: e.memset(SBs[:], 0.0), writes=[('SBs', h) for h in range(RET_H)])
            tk.op('pool', lambda e: e.memset(HBs[:], 0.0), writes=[('HBs', g) for g in range(2)])
            tk.op('pool', lambda e: e.memset(SF[:], 0.0), writes=[('SF', h) for h in range(RET_H)])
            tk.op('pool', lambda e: e.memset(HF[:], 0.0), writes=[('HF', g) for g in range(2)])
            bchain = list(range(NCT - 1, -1, -1)) + list(range(NT - 1, NCT - 1, -1))
            load_chunk(CH[0], bchain[0], False)
            for ci, c in enumerate(bchain):
                C = CH[ci % 2]
                if ci + 1 < len(bchain):
                    load_chunk(CH[(ci + 1) % 2], bchain[ci + 1], False)
                tk.dma('pool', SBst[c], SBs[:].rearrange("p h d -> p (h d)"), reads=[('SBs', h) for h in range(RET_H)], writes=['SBst'])
                tk.dma('pool', HBst[c], HBs[:].rearrange("p g d -> p (g d)"), reads=[('HBs', g) for g in range(2)], writes=['HBst'])
                ada_tick()
                ssd_cols(C, [1])
                interleave([ssd_state_update(C, HBs, 'HBs', g, 1) for g in range(2)], 2)
                interleave([ret_state_update(C, SBs, 'SBs', h, 8 + h, 8 + h) for h in range(RET_H)], 2)
            if stop == 'e2':
                end_scope(es)
                return

            def ret_head(C, h):
                par = h % 2
                pa, pb = (0, 1) if par == 0 else (4, 5)
                PT, QF, QB = PTs[par], QFs[par], QBs[par]
                ptn, qfn, qbn = f'PT{par}', f'QF{par}', f'QB{par}'
                KTc, QTc, Vc, SBin = C['KTc'], C['QTc'], C['Vc'], C['SBi']
                tk.op('pe', lambda e: e.matmul(PS[pa][:, 0:128], lhsT=KTc[:, h, :], rhs=QTc[:, h, :], start=True, stop=True),
                      reads=[k_(C, 'KTc'), k_(C, 'QTc')], writes=[f'ps{pa}'])
                tk.op('pool', lambda e: e.tensor_tensor(out=QF[:], in0=QTc[:, h, :], in1=DQF[:, h, :], op=ALU.mult), reads=[k_(C, 'QTc'), 'DQF'], writes=[qfn])
                yield
                tk.op('dve', lambda e: e.tensor_tensor(out=PT[:], in0=PS[pa][:, 0:128], in1=DC[:, h, :], op=ALU.mult), reads=[f'ps{pa}', 'DC'], writes=[ptn])
                tk.op('pool', lambda e: e.tensor_tensor(out=QB[:], in0=QTc[:, h, :], in1=DQB[:, h, :], op=ALU.mult), reads=[k_(C, 'QTc'), 'DQB'], writes=[qbn])
                yield
                tk.op('pe', lambda e: e.matmul(PS[pb][:, 0:128], lhsT=PT[:], rhs=Vc[:, h * 128:(h + 1) * 128], start=True, stop=False),
                      reads=[ptn, k_(C, 'Vc')], writes=[f'ps{pb}'])
                tk.op('pe', lambda e: e.matmul(PS[pb][:, 0:128], lhsT=QF[:], rhs=SF[:, h, :], start=False, stop=False),
                      reads=[qfn, ('SF', h)], writes=[f'ps{pb}'])
                tk.op('pe', lambda e: e.matmul(PS[pb][:, 0:128], lhsT=QB[:], rhs=SBin[:, h, :], start=False, stop=True),
                      reads=[qbn, (k_(C, 'SBi'), h)], writes=[f'ps{pb}'])
                yield
                tk.op('act', lambda e: e.copy(out=O8[:, h, :], in_=PS[pb][:, 0:128]), reads=[f'ps{pb}'], writes=[('O8', h)])
                yield
                yield from ret_state_update(C, SF, 'SF', h, h, h)

            def ssd_head(C, g, e_):
                idx = g * 8 + e_
                par = idx % 2
                o = par * 256
                EF, EB, WT = EFs[par], EBs[par], WTs[par]
                efn, ebn, wtn = f'EF{par}', f'EB{par}', f'WT{par}'
                DTc, XSc = C['DTc'], C['XSc']
                if par == 0:
                    q = idx // 2
                    zq = ZH[q // 4][:, 4 * (q % 4):4 * (q % 4) + 4, :]
                    tk.op('pe', lambda e: e.matmul(PS[4][:, :], lhsT=ONES[:], rhs=zq.rearrange("p a t -> p (a t)"),
                                                   start=True, stop=True), reads=['ONES', ZK[q // 4]], writes=['ps4'])
                yield
                tk.op('dve', lambda e: e.scalar_tensor_tensor(out=EF[:], in0=PS[4][:, o:o + 128], scalar=CUM[:, idx:idx + 1], in1=MF[:],
                                                             op0=ALU.subtract, op1=ALU.add), reads=['ps4', 'CUM', 'MF'], writes=[efn])
                tk.op('dve', lambda e: e.scalar_tensor_tensor(out=EB[:], in0=PS[4][:, o + 128:o + 256], scalar=CUM[:, 16 + idx:17 + idx], in1=MB[:],
                                                             op0=ALU.subtract, op1=ALU.add), reads=['ps4', 'CUM', 'MB'], writes=[ebn])
                yield
                tk.op('act', lambda e: e.activation(out=EF[:], in_=EF[:], func=AF.Exp), reads=[efn], writes=[efn])
                tk.op('act', lambda e: e.activation(out=EB[:], in_=EB[:], func=AF.Exp), reads=[ebn], writes=[ebn])
                yield
                tk.op('pool', lambda e: e.tensor_scalar(out=EF[:], in0=EF[:], scalar1=DTc[:, idx:idx + 1], scalar2=None, op0=ALU.mult),
                      reads=[efn, k_(C, 'DTc')], writes=[efn])
                yield
                tk.op('dve', lambda e: e.scalar_tensor_tensor(out=EB[:], in0=EB[:], scalar=DTc[:, 16 + idx:17 + idx], in1=EF[:],
                                                             op0=ALU.mult, op1=ALU.add), reads=[ebn, k_(C, 'DTc'), efn], writes=[ebn])
                yield
                tk.op('pool', lambda e: e.tensor_tensor(out=WT[:], in0=EB[:], in1=CBs[:], op=ALU.mult), reads=[ebn, 'CBs'], writes=[wtn])
                yield
                tk.op('pe', lambda e: e.matmul(PS[6][:, e_ * 64:(e_ + 1) * 64], lhsT=WT[:], rhs=XSc[:, idx * 64:(idx + 1) * 64], start=True, stop=True),
                      reads=[wtn, k_(C, 'XSc')], writes=['ps6'])
                yield

            load_chunk(CH[0], 0, True)
            for c in range(NT):
                r0 = c * 128
                C = CH[c % 2]
                if c + 1 < NT:
                    load_chunk(CH[(c + 1) % 2], c + 1, True)
                ada_tick()
                KTc, QTc, Vc, XSc, SGc, SZc, BTc, CTc, DTc, DTAc, HBin = (C[n] for n in ('KTc', 'QTc', 'Vc', 'XSc', 'SGc', 'SZc', 'BTc', 'CTc', 'DTc', 'DTAc', 'HBi'))
                ssd_cols(C, [0, 1])
                for zh in range(2):
                    z4 = ZH[zh].rearrange("p (i d) t -> p i d t", d=2)
                    for d in range(2):
                        tri = TRIF if d == 0 else TRIB
                        c0 = d * 16 + zh * 8
                        tk.op('dve', lambda e, d=d, tri=tri, z4=z4, c0=c0: e.tensor_tensor(
                            out=z4[:, :, d, :], in0=tri[:].unsqueeze(1).to_broadcast([128, 8, 128]),
                            in1=DTAc[:, c0:c0 + 8].unsqueeze(2).to_broadcast([128, 8, 128]), op=ALU.mult),
                            reads=['TRIF', 'TRIB', k_(C, 'DTAc')], writes=[ZK[zh]])
                interleave([ret_head(C, h) for h in range(RET_H)], 2)
                o8k = [('O8', h) for h in range(RET_H)]
                tk.op('dve', lambda e: e.reduce_sum(out=S8[:, 0:8], in_=O8[:], axis=AX.X), reads=o8k, writes=['S8'])
                tk.op('dve', lambda e: e.tensor_scalar(out=S8[:, 0:8], in0=S8[:, 0:8], scalar1=1.0 / 128, scalar2=None, op0=ALU.mult), reads=['S8'], writes=['S8'])
                tk.op('dve', lambda e: e.tensor_tensor(out=D8[:], in0=O8[:], in1=S8[:, 0:8].unsqueeze(2).to_broadcast([128, 8, 128]), op=ALU.subtract),
                      reads=o8k + ['S8'], writes=['D8'])
                tk.op('pool', lambda e: e.tensor_tensor(out=O8[:], in0=D8[:], in1=D8[:], op=ALU.mult), reads=['D8'], writes=o8k)
                tk.op('dve', lambda e: e.reduce_sum(out=S8[:, 8:16], in_=O8[:], axis=AX.X), reads=o8k, writes=['S8'])
                tk.op('dve', lambda e: e.tensor_scalar(out=S8[:, 8:16], in0=S8[:, 8:16], scalar1=1.0 / 128, scalar2=EPS, op0=ALU.mult, op1=ALU.add), reads=['S8'], writes=['S8'])
                tk.op('act', lambda e: e.activation(out=S8[:, 16:24], in_=S8[:, 8:16], func=AF.Sqrt), reads=['S8'], writes=['S8'])
                tk.op('dve', lambda e: e.reciprocal(out=S8[:, 24:32], in_=S8[:, 16:24]), reads=['S8'], writes=['S8'])
                tk.op('dve', lambda e: e.tensor_tensor(out=D8[:], in0=D8[:], in1=S8[:, 24:32].unsqueeze(2).to_broadcast([128, 8, 128]), op=ALU.mult),
                      reads=['D8', 'S8'], writes=['D8'])
                d8f = D8[:].rearrange("p h d -> p (h d)")
                tk.op('pool', lambda e: e.tensor_tensor(out=d8f, in0=d8f, in1=GNG[:], op=ALU.mult), reads=['D8', 'GNG'], writes=['D8'])
                tk.op('dve', lambda e: e.tensor_tensor(out=MB16[:, 0:1024], in0=d8f, in1=SGc[:], op=ALU.mult), reads=['D8', k_(C, 'SGc')], writes=[('MB16', 0)])
                if stop == 'e3ret':
                    continue
                for g in range(2):
                    tk.op('pe', lambda e, g=g: e.matmul(PS[5][:, 0:128], lhsT=BTc[:, g, :], rhs=CTc[:, g, :], start=True, stop=True),
                          reads=[k_(C, 'BTc'), k_(C, 'CTc')], writes=['ps5'])
                    tk.op('act', lambda e: e.copy(out=CBs[:], in_=PS[5][:, 0:128]), reads=['ps5'], writes=['CBs'])
                    interleave([ssd_head(C, g, e_) for e_ in range(8)], 2)
                    yg = YG[:, g * 512:(g + 1) * 512]
                    yt3 = YT[:].rearrange("p (e q) -> p e q", q=64)
                    tk.op('act', lambda e, yg=yg: e.copy(out=yg, in_=PS[6][:, :]), reads=['ps6'], writes=[('YG', g)])
                    for d, H, hn in ((0, HF, ('HF', g)), (1, HBin, k_(C, 'HBi'))):
                        col = d * 16 + g * 8
                        tk.op('pe', lambda e, g=g, H=H: e.matmul(PS[7][:, :], lhsT=CTc[:, g, :], rhs=H[:, g, :], start=True, stop=True),
                              reads=[k_(C, 'CTc'), hn], writes=['ps7'])
                        tk.op('dve', lambda e, col=col: e.tensor_tensor(out=yt3, in0=PS[7][:, :].rearrange("p (e q) -> p e q", q=64), in1=bc8(ECUM[:, col:col + 8]), op=ALU.mult),
                              reads=['ps7', 'ECUM'], writes=['YT'])
                        tk.op('pool', lambda e, yg=yg: e.tensor_tensor(out=yg, in0=yg, in1=YT[:], op=ALU.add), reads=[('YG', g), 'YT'], writes=[('YG', g)])
                    for _ in ssd_state_update(C, HF, 'HF', g, 0):
                        pass
                ygk = [('YG', 0), ('YG', 1)]
                xs3a = XSc[:].rearrange("p (i q) -> p i q", q=64)
                tk.op('dve', lambda e: e.tensor_tensor(out=xs3a, in0=xs3a, in1=DSK[:].unsqueeze(2).to_broadcast([128, 16, 64]), op=ALU.mult),
                      reads=[k_(C, 'XSc'), 'DSK'], writes=[k_(C, 'XSc')])
                tk.op('pool', lambda e: e.tensor_tensor(out=YG[:], in0=YG[:], in1=XSc[:], op=ALU.add), reads=ygk + [k_(C, 'XSc')], writes=ygk)
                tk.op('dve', lambda e: e.tensor_tensor(out=YG[:], in0=YG[:], in1=SZc[:], op=ALU.mult), reads=ygk + [k_(C, 'SZc')], writes=ygk)
                tk.op('pool', lambda e: e.memset(ST[:, 0:1], 0.0), writes=['ST'])
                tk.op('act', lambda e: e.activation(out=XSc[:], in_=YG[:], func=AF.Square, accum_out=ST[:, 0:1]), reads=ygk + ['ST'], writes=[k_(C, 'XSc'), 'ST'])
                rstd_from_sumsq(1024)
                tk.op('dve', lambda e: e.scalar_tensor_tensor(out=MB16[:, 1024:2048], in0=YG[:], scalar=ST[:, 2:3], in1=SNG[:], op0=ALU.mult, op1=ALU.mult),
                      reads=ygk + ['ST', 'SNG'], writes=[('MB16', 1)])
                if debug:
                    tk.op('dve', lambda e: e.tensor_copy(out=SCR[:], in_=MB16[:]), reads=[('MB16', 0), ('MB16', 1)], writes=['SCR'])
                    tk.dma('pool', MIXd[r0:r0 + 128, :], SCR[:], reads=['SCR'], writes=['MIXd'])
                for k4 in range(KC // 4):
                    pi = k4 % 2
                    pname = f'ps{6 + pi}'
                    for k in range(4):
                        kc = k4 * 4 + k
                        tk.op('pe', lambda e, kc=kc, k=k, pi=pi: e.transpose(
                            out=PST[pi][:, k * 128:(k + 1) * 128], in_=MB16[:, kc * 128:(kc + 1) * 128], identity=ident[:]),
                            reads=[('MB16', kc // 8), 'ident'], writes=[pname])
                    tk.op('act', lambda e, k4=k4, pi=pi: e.copy(
                        out=MTc[:, k4 * 4:(k4 + 1) * 4, :], in_=PST[pi][:, 0:512].rearrange("p (k t) -> p k t", k=4)),
                        reads=[pname], writes=['MTc'])
                tk.dma('pool', MTd[:, :, r0:r0 + 128].rearrange("k p t -> p k t"), MTc[:], reads=['MTc'], writes=['MTd'])
            for _ in ada_it:
                pass
            end_scope(es)

        def e4_phase(l, with_ctx):
            m = l // 2
            currow = None
            for (t0, nt, row) in blocks(with_ctx):
                ntok = nt * 128
                if row != currow:
                    load_consts(l, 1, 1, row, 1.0)
                    currow = row
                for j in range(nt):
                    load_x(j, t0 + j * 128)
                tk.dma('sp', B['AT'][:, 0:KC, 0:ntok], MTd[:, :, t0:t0 + ntok].rearrange("k p t -> p k t"), reads=['MTd'], writes=['AT'])
                down_proj(lambda db: mwo_b[m, db], KC, lambda db: need([('mw', m, 5)]), nt)
                store_block(t0, nt)
                cast_tick(2)

        stop = cfg.get('stop')
        if do_mix is True:
            rope_tables()
        last_even = DEPTH - 1 - (DEPTH - 1) % 2
        for l in range(DEPTH if stop != 'rope' else 0):
            ctx_in = l <= last_even
            ctx_out = l < last_even
            esA = scope_a()
            ffn_phase(l, 0, ctx_in)
            if l % 2 == 1 and do_mix:
                sconv_phase(l, ctx_out)
            elif l % 2 == 0 and do_mix is True:
                esC, sbc = mk_scope()
                mixer_consts(l // 2, sbc)
                if stop != 'ffn':
                    e1_phase(l)
                end_scope(esC)
                end_scope(esA)
                if stop in ('e1', 'ffn'):
                    return nc
                scan_phases(l)
                if stop in ('scan', 'sc_const', 'e2', 'e3ret', 'e3z', 'e3g', 'e3post'):
                    return nc
                esA = scope_a()
                e4_phase(l, ctx_out)
            ffn_phase(l, 1, ctx_out)
            end_scope(esA)

        esA = scope_a()
        tk.dma('act', GS[:], final_g.partition_broadcast(128), writes=['GS'])
        for t in range(NCT, NT):
            j = t % TB
            norm_tile(j, t * 128, modulate=False)
            tk.dma('pool', out[(t - NCT) * 128:(t - NCT + 1) * 128, :], SCR[:], reads=['SCR'], writes=['out'])
        tk.barrier()
        esA.close()
    return nc


def prep_inputs(cfg, b, x, c, ctx, c_ctx, ada_w, ada_b, norm_g, final_g, ffn_up, ffn_down, **kw):
    DEPTH, DFF, SEQ, CTX = cfg['depth'], cfg['dff'], cfg['seq'], cfg['ctx']
    NH = DFF // 128
    T = SEQ + CTX
    f32 = np.float32
    xin = np.concatenate([ctx[b], x[b]], axis=0)
    cc = np.stack([c[b], c_ctx], axis=0)
    cT = np.ascontiguousarray(cc.reshape(2, KC, 128).transpose(2, 1, 0))
    up = ffn_up[:DEPTH].reshape(DEPTH, 2, KC, 128, 2, NH, 128)
    wup = np.ascontiguousarray(up.transpose(0, 1, 5, 3, 2, 4, 6)).reshape(DEPTH, 2, NH, 128, KC * 256)
    dn = ffn_down[:DEPTH].reshape(DEPTH, 2, NH, 128, D // 256, 256)
    wdn = np.ascontiguousarray(dn.transpose(0, 1, 4, 3, 2, 5)).reshape(DEPTH, 2, D // 256, 128, NH * 256)
    N_ODD = max(DEPTH // 2, 1)
    N_EVEN = (DEPTH + 1) // 2
    sc_w_in, sc_conv_w, sc_w_out = kw['sc_w_in'], kw['sc_conv_w'], kw['sc_w_out']
    wi = sc_w_in[:N_ODD].reshape(N_ODD, KC, 128, 3, KC, 128)
    scw_cu = np.ascontiguousarray(wi[:, :, :, 1:3].transpose(0, 4, 2, 1, 3, 5)).reshape(N_ODD, KC, 128, KC * 256)
    scw_b = np.ascontiguousarray(wi[:, :, :, 0].transpose(0, 3, 2, 1, 4)).reshape(N_ODD, KC, 128, KC * 128)
    wo = sc_w_out[:N_ODD].reshape(N_ODD, KC, 128, D // 256, 256)
    scwo = np.ascontiguousarray(wo.transpose(0, 3, 2, 1, 4)).reshape(N_ODD, D // 256, 128, KC * 256)
    sccw = np.ascontiguousarray(sc_conv_w[:N_ODD].reshape(N_ODD, 3, KC, 128).transpose(0, 3, 2, 1))
    mw = kw['mix_w_in'][:N_EVEN]

    def fm_slot(cols):
        w = mw[:, :, cols].reshape(N_EVEN, KC, 128, 256)
        return w.transpose(0, 2, 1, 3).reshape(N_EVEN, 128, KC * 256)
    slots = []
    for base in (0, 1024):
        for pr in range(4):
            h0, h1 = 2 * pr, 2 * pr + 1
            cols = np.concatenate([base + h0 * 128 + np.arange(64), base + h1 * 128 + np.arange(64),
                                   base + h0 * 128 + 64 + np.arange(64), base + h1 * 128 + 64 + np.arange(64)])
            slots.append(fm_slot(cols))
    for s in range(6):
        slots.append(fm_slot(5120 + s * 256 + np.arange(256)))
    wfm = np.ascontiguousarray(np.stack(slots, axis=1))
    tms = []
    for blk in range(6):
        w = mw[:, :, 2048 + blk * 512:2048 + (blk + 1) * 512].reshape(N_EVEN, KC, 128, 512)
        tms.append(w.transpose(0, 2, 1, 3).reshape(N_EVEN, 128, KC * 512))
    wtm = np.ascontiguousarray(np.stack(tms, axis=1))
    wdt = np.ascontiguousarray(mw[:, :, 6656:6688].reshape(N_EVEN, KC, 128, 32).transpose(0, 2, 1, 3)).reshape(N_EVEN, 128, KC * 32)
    mo = kw['mix_w_out'][:N_EVEN].reshape(N_EVEN, KC, 128, D // 256, 256)
    mwo = np.ascontiguousarray(mo.transpose(0, 3, 2, 1, 4)).reshape(N_EVEN, D // 256, 128, KC * 256)
    mcw = np.ascontiguousarray(kw['mix_conv_w'][:N_EVEN].reshape(N_EVEN, 3, 12, 128).transpose(0, 3, 2, 1))
    mcb = np.ascontiguousarray(kw['mix_conv_b'][:N_EVEN].reshape(N_EVEN, 12, 128).transpose(0, 2, 1))
    tl = np.arange(SEQ)
    pos = np.zeros((T, 3), f32)
    pos[:CTX, 0] = np.arange(CTX)
    pos[CTX:, 0] = CTX
    pos[CTX:, 1] = tl // 64
    pos[CTX:, 2] = tl % 64
    axis = np.concatenate([np.zeros(16, int), np.ones(24, int), 2 * np.ones(24, int)])
    inv = np.concatenate([10000.0 ** (-np.arange(n, dtype=f32) / n) for n in (16, 24, 24)]).astype(f32)
    pax64 = pos[:, axis].T
    pax = np.ascontiguousarray(np.concatenate([pax64, pax64], axis=0)).astype(f32)
    inv128 = np.concatenate([inv, inv]).reshape(128, 1).astype(f32)
    iota = (np.arange(128)[None, :] - np.arange(128)[:, None]).astype(f32)
    return {
        "scw_cu": scw_cu, "scw_b": scw_b, "scwo": scwo, "sccw": sccw,
        "wfm": wfm, "wtm": wtm, "wdt": wdt, "mwo": mwo, "mcw": mcw, "mcb": mcb,
        "rld": np.ascontiguousarray(kw['ret_log_decay'][:N_EVEN].reshape(N_EVEN, 1, 16)),
        "gng": np.ascontiguousarray(kw['ret_gn_g'][:N_EVEN].reshape(N_EVEN, 1, 1024)),
        "alog": np.ascontiguousarray(kw['ssd_a_log'][:N_EVEN].reshape(N_EVEN, 1, 32)),
        "dtb": np.ascontiguousarray(kw['ssd_dt_bias'][:N_EVEN].reshape(N_EVEN, 1, 32)),
        "ssdd": np.ascontiguousarray(kw['ssd_d'][:N_EVEN].reshape(N_EVEN, 1, 16)),
        "sng": np.ascontiguousarray(kw['ssd_norm_g'][:N_EVEN].reshape(N_EVEN, 1, 1024)),
        "pax": pax, "inv": inv128, "iota_in": iota,
        "xin": np.ascontiguousarray(xin), "cT": cT,
        "ada_w": np.ascontiguousarray(ada_w[:DEPTH]), "ada_b": np.ascontiguousarray(ada_b[:DEPTH]),
        "norm_g": np.ascontiguousarray(norm_g[:DEPTH]), "final_g": np.ascontiguousarray(final_g.reshape(1, D)),
        "wup": wup, "wdn": wdn, "ident_in": np.eye(128, dtype=np.float32),
    }


def run(cfg, inputs, full=False):
    nc = build(cfg)
    nb = inputs['x'].shape[0]
    in_maps = [prep_inputs(cfg, b, **inputs) for b in range(nb)]
    res = run_bass_kernel_spmd(nc, in_maps, core_ids=list(range(nb)))
    if full:
        return res.results
    return np.stack([r["out"] for r in res.results], axis=0)


def kernel(**inputs):
    inputs = {k: np.asarray(v) for k, v in inputs.items()}
    cfg = dict(depth=4, seq=4096, ctx=256, dff=5632)
    return run(cfg, inputs).astype(np.float32)
```

```python
import numpy as np
import concourse.bass as bass
import concourse.mybir as mybir
from concourse.bass_utils import run_bass_kernel_spmd

F32 = mybir.dt.float32
BF16 = mybir.dt.bfloat16
AF = mybir.ActivationFunctionType
ALU = mybir.AluOpType
AX = mybir.AxisListType

D = 2048
KC = D // 128
EPS = 1e-6
N_MOD = 9


class TK:
    NS = {'sp': 16, 'pool': 8, 'act': 8}

    def __init__(self, nc):
        self.nc = nc
        self.names = ['pe', 'act', 'dve', 'pool', 'sp']
        self.q = {k: [] for k in self.names}
        self.cnt = {k: 0 for k in self.names}
        self.dcnt = {k: 0 for k in self.NS}
        self.waited = {k: {} for k in self.names}
        self.lastw = {}
        self.reads = {}
        self.semh = {}
        self.engs = {'pe': nc.tensor, 'act': nc.scalar, 'dve': nc.vector, 'pool': nc.gpsimd, 'sp': nc.sync}

    def semkeys(self):
        ks = list(self.names)
        for qn, n in self.NS.items():
            ks += [(qn, i) for i in range(n)]
        return ks

    def _deps(self, eng, reads, writes):
        deps = {}

        def add(ev):
            if ev is None:
                return
            k, v = ev
            if k == 'pe' and eng == 'pe':
                return
            if deps.get(k, 0) < v:
                deps[k] = v
        for r in reads:
            add(self.lastw.get(r))
        for w in writes:
            add(self.lastw.get(w))
            for k, v in self.reads.get(w, {}).items():
                add((k, v))
        out = []
        wd = self.waited[eng]
        for k, v in deps.items():
            if wd.get(k, 0) < v:
                wd[k] = v
                out.append((k, v))
        return out

    def _record(self, ev, reads, writes):
        for w in writes:
            self.lastw[w] = ev
            self.reads[w] = {}
        for r in reads:
            if r in writes:
                continue
            d = self.reads.setdefault(r, {})
            if d.get(ev[0], 0) < ev[1]:
                d[ev[0]] = ev[1]

    def op(self, eng, fn, reads=(), writes=()):
        waits = self._deps(eng, reads, writes)
        self.cnt[eng] += 1
        ev = (eng, self.cnt[eng])
        self._record(ev, reads, writes)
        semh = self.semh

        e = self.engs[eng]
        for k, v in waits:
            e.wait_ge(semh[k], v)
        fn(e).then_inc(semh[eng], 1)
        return ev

    def dma(self, qn, out, in_, reads=(), writes=(), **kw):
        waits = self._deps(qn, reads, writes)
        i = self.dcnt[qn]
        self.dcnt[qn] += 1
        ns = self.NS[qn]
        key = (qn, i % ns)
        val = 16 * (i // ns + 1)
        if i >= ns and self.waited[qn].get(key, 0) < val - 16:
            self.waited[qn][key] = val - 16
            waits.append((key, val - 16))
        ev = (key, val)
        self._record(ev, reads, writes)
        semh = self.semh

        e = self.engs[qn]
        for k, v in waits:
            e.wait_ge(semh[k], v)
        e.dma_start(out=out, in_=in_, **kw).then_inc(semh[key], 16)
        return ev

    def wait_all(self, eng):
        evs = {}
        for ev in self.lastw.values():
            if evs.get(ev[0], 0) < ev[1]:
                evs[ev[0]] = ev[1]
        semh = self.semh
        lst = list(evs.items())

        e = self.engs[eng]
        wd = self.waited[eng]
        for k, v in lst:
            if wd.get(k, 0) < v:
                wd[k] = v
                e.wait_ge(semh[k], v)

    def barrier(self):
        for eng in self.names:
            self.wait_all(eng)


RET_H = 8
SSD_H = 16
XBC_W = 1536
BIG = 30000.0
import contextlib
import math


def build(cfg):
    DEPTH, SEQ, CTX, DFF = cfg['depth'], cfg['seq'], cfg['ctx'], cfg['dff']
    do_mix = cfg.get('mix', True)
    debug = cfg.get('debug', False)
    T = SEQ + CTX
    NT = T // 128
    NCT = CTX // 128
    NH = DFF // 128
    NHA = max(NH, 32)
    NDB = D // 256
    N_ODD = max(DEPTH // 2, 1)
    N_EVEN = (DEPTH + 1) // 2
    nc = bass.Bass("TRN2", target_bir_lowering=False)
    tk = TK(nc)

    def dram_in(name, shape, dt=F32):
        return nc.dram_tensor(name, list(shape), dt, kind="ExternalInput").ap()

    def dram(name, shape, dt=F32, dbg=False):
        if dbg and debug:
            return nc.dram_tensor(name, list(shape), dt, kind="ExternalOutput").ap()
        return nc.dram_tensor(name, list(shape), dt).ap()

    xin = dram_in("xin", [T, D])
    cT = dram_in("cT", [128, KC, 2])
    ada_w = dram_in("ada_w", [DEPTH, D, N_MOD * D])
    ada_b = dram_in("ada_b", [DEPTH, N_MOD * D])
    norm_g = dram_in("norm_g", [DEPTH, 3, D])
    final_g = dram_in("final_g", [1, D])
    wup_in = dram_in("wup", [DEPTH, 2, NH, 128, KC * 256])
    wdn_in = dram_in("wdn", [DEPTH, 2, NDB, 128, NH * 256])
    ident_in = dram_in("ident_in", [128, 128])
    iota_in = dram_in("iota_in", [128, 128])
    scw_cu_in = dram_in("scw_cu", [N_ODD, KC, 128, KC * 256])
    scw_b_in = dram_in("scw_b", [N_ODD, KC, 128, KC * 128])
    scwo_in = dram_in("scwo", [N_ODD, NDB, 128, KC * 256])
    sccw_in = dram_in("sccw", [N_ODD, 128, KC, 3])
    wfm_in = dram_in("wfm", [N_EVEN, 14, 128, KC * 256])
    wtm_in = dram_in("wtm", [N_EVEN, 6, 128, KC * 512])
    wdt_in = dram_in("wdt", [N_EVEN, 128, KC * 32])
    mwo_in = dram_in("mwo", [N_EVEN, NDB, 128, KC * 256])
    mcw_in = dram_in("mcw", [N_EVEN, 128, 12, 3])
    mcb_in = dram_in("mcb", [N_EVEN, 128, 12])
    rld_in = dram_in("rld", [N_EVEN, 1, 16])
    gng_in = dram_in("gng", [N_EVEN, 1, 1024])
    alog_in = dram_in("alog", [N_EVEN, 1, 32])
    dtb_in = dram_in("dtb", [N_EVEN, 1, 32])
    ssdd_in = dram_in("ssdd", [N_EVEN, 1, 16])
    sng_in = dram_in("sng", [N_EVEN, 1, 1024])
    pax_in = dram_in("pax", [128, T])
    inv_in = dram_in("inv", [128, 1])
    out = nc.dram_tensor("out", [SEQ, D], F32, kind="ExternalOutput").ap()

    X = dram("X", [T, D])
    MOD = dram("MOD", [2, DEPTH * N_MOD * D])
    wup_b = [[dram(f"wup_b{l}_{i}", [NH, 128, KC * 256], BF16) for i in range(2)] for l in range(DEPTH)]
    wdn_b = [[dram(f"wdn_b{l}_{i}", [NDB, 128, NH * 256], BF16) for i in range(2)] for l in range(DEPTH)]
    scw_cu_b = dram("scw_cu_b", [N_ODD, KC, 128, KC * 256], BF16)
    scw_b_b = dram("scw_b_b", [N_ODD, KC, 128, KC * 128], BF16)
    scwo_b = dram("scwo_b", [N_ODD, NDB, 128, KC * 256], BF16)
    wfm_b = dram("wfm_b", [N_EVEN, 14, 128, KC * 256], BF16)
    wtm_b = dram("wtm_b", [N_EVEN, 6, 128, KC * 512], BF16)
    wdt_b = dram("wdt_b", [N_EVEN, 128, KC * 32], BF16)
    mwo_b = dram("mwo_b", [N_EVEN, NDB, 128, KC * 256], BF16)
    ROPE = dram("ROPE", [4, 128, T], dbg=True)
    QT = dram("QT", [RET_H, 128, T], dbg=True)
    KT = dram("KT", [RET_H, 128, T], dbg=True)
    Vd = dram("Vd", [T, 1024], dbg=True)
    SGd = dram("SGd", [T, 1024], dbg=True)
    SZd = dram("SZd", [T, 1024], dbg=True)
    XSd = dram("XSd", [T, 1024], dbg=True)
    BTOK = dram("BTOK", [T, 256], dbg=True)
    BTd = dram("BTd", [2, 128, T], dbg=True)
    CTd = dram("CTd", [2, 128, T], dbg=True)
    DTd = dram("DTd", [T, 32], dbg=True)
    DTAd = dram("DTAd", [T, 32], dbg=True)
    SBst = dram("SBst", [NT, 128, RET_H * 128])
    HBst = dram("HBst", [NT, 128, 1024])
    MTd = dram("MTd", [KC, 128, T], BF16)
    MIXd = dram("MIXd", [T, D], dbg=True)

    uid = [0]
    B = {}

    def mk_scope():
        es = contextlib.ExitStack()

        def sb(name, shape, dt=F32):
            uid[0] += 1
            t = es.enter_context(nc.sbuf_tensor(f"{name}_{uid[0]}", list(shape), dt))
            B[name] = t
            return t
        return es, sb

    TB = 4
    NTOK = TB * 128
    with contextlib.ExitStack() as es0:
        for k in tk.semkeys():
            nm = k if isinstance(k, str) else f"{k[0]}{k[1]}"
            tk.semh[k] = es0.enter_context(nc.semaphore("s_" + nm))

        def sb0(name, shape, dt=F32):
            return es0.enter_context(nc.sbuf_tensor(name, list(shape), dt))

        ident = sb0("ident", [128, 128], BF16)
        identf = sb0("identf", [128, 128], F32)
        GS = sb0("GS", [128, D])
        SH = sb0("SH", [128, D])
        GM = sb0("GM", [128, D])
        SCR = sb0("SCR", [128, D])
        HBL = [sb0("HB", [128, D], BF16), sb0("HB1", [128, D], BF16)]
        ST = sb0("ST", [128, 8])
        CW = sb0("CW", [128, KC, 3])
        cs = sb0("cs", [128, KC, 2])
        csg = sb0("csg", [128, KC, 2])
        ADB = sb0("ADB", [2, 512])
        ADO = sb0("ADO", [2, 512])
        IOT = sb0("IOT", [128, 128])
        PS = [es0.enter_context(nc.psum_tensor(f"ps{i}", [128, 512], F32)) for i in range(8)]
        PST = [PS[6][:].bitcast(BF16), PS[7][:].bitcast(BF16)]

        def scope_a():
            es, sb = mk_scope()
            sb("XT", [128, TB, D])
            sb("HT", [128, KC, NTOK], BF16)
            sb("AT", [128, NHA, NTOK], BF16)
            for i in range(2):
                sb(f"WUP{i}", [128, KC, 256], BF16)
                sb(f"WDN{i}", [128, NHA, 256], BF16)
                sb(f"SG{i}", [128, NTOK])
            return es

        def end_scope(es):
            tk.barrier()
            es.close()

        tk.dma('sp', identf[:], ident_in, writes=['identf'])
        tk.op('dve', lambda e: e.tensor_copy(out=ident[:], in_=identf[:]), reads=['identf'], writes=['ident'])
        tk.dma('sp', IOT[:], iota_in, writes=['IOT'])
        tk.dma('sp', X, xin, writes=['X'])
        cast_jobs = []

        def add_ffn_casts(l, i):
            for hc in range(NH):
                cast_jobs.append((('wupb', l, i, hc), wup_b[l][i][hc], wup_in[l, i, hc]))
            for db in range(NDB):
                cast_jobs.append((('wdnb', l, i, db), wdn_b[l][i][db], wdn_in[l, i, db]))
        for l in range(DEPTH):
            add_ffn_casts(l, 0)
            m = l // 2
            if l % 2 == 1:
                for fc in range(KC):
                    cast_jobs.append((('scw', m, 'cu', fc), scw_cu_b[m, fc], scw_cu_in[m, fc]))
                    cast_jobs.append((('scw', m, 'b', fc), scw_b_b[m, fc], scw_b_in[m, fc]))
                for db in range(NDB):
                    cast_jobs.append((('scw', m, 'o', db), scwo_b[m, db], scwo_in[m, db]))
            elif do_mix is True:
                for sl_ in range(14):
                    cast_jobs.append((('mw', m, 'fm', sl_), wfm_b[m, sl_], wfm_in[m, sl_]))
                for blk in range(6):
                    cast_jobs.append((('mw', m, 'tm', blk), wtm_b[m, blk], wtm_in[m, blk]))
                cast_jobs.append((('mw', m, 'dt'), wdt_b[m], wdt_in[m]))
                for db in range(NDB):
                    cast_jobs.append((('mw', m, 'o', db), mwo_b[m, db], mwo_in[m, db]))
            add_ffn_casts(l, 1)
        cast_pos = [0]
        cast_done = set()

        def cast_tick(n):
            for _ in range(n):
                if cast_pos[0] < len(cast_jobs):
                    key, o_, i_ = cast_jobs[cast_pos[0]]
                    cast_pos[0] += 1
                    tk.dma('pool', o_, i_, writes=[key])
                    cast_done.add(key)

        def need(keys):
            for k in keys:
                while k not in cast_done:
                    cast_tick(1)
            return list(keys)

        def wdeps(kind, l, i, idx):
            if kind == 'up':
                return need([('wupb', l, i, idx)])
            return need([('wdnb', l, i, idx)])

        cast_tick(8)

        tk.dma('sp', cs[:], cT, writes=['cs'])
        tk.op('act', lambda e: e.activation(out=csg[:], in_=cs[:], func=AF.Silu), reads=['cs'], writes=['csg'])
        ncb = N_MOD * D // 512

        def ada_gen(layers, AW, pi):
            it = 0
            pn = f'ps{pi}'
            for l in layers:
                for cb in range(ncb):
                    tk.dma('act', ADB[:], ada_b[l:l + 1, cb * 512:(cb + 1) * 512].partition_broadcast(2), writes=['ADB'])
                    for kq in range(4):
                        buf, bname = AW[it % 2]
                        it += 1
                        bv = buf.rearrange("p (k c) -> p k c", k=4)
                        tk.dma('sp', bv, ada_w[l, kq * 512:(kq + 1) * 512, cb * 512:(cb + 1) * 512].rearrange(
                            "(k p) c -> p k c", p=128), writes=[bname])
                        for k in range(4):
                            kc = kq * 4 + k
                            tk.op('pe', lambda e, bv=bv, k=k, kc=kc: e.matmul(
                                PS[pi][0:2, :], lhsT=csg[:, kc, :], rhs=bv[:, k, :], start=(kc == 0), stop=(kc == KC - 1)),
                                reads=['csg', bname], writes=[pn])
                    tk.op('dve', lambda e: e.tensor_tensor(out=ADO[:], in0=PS[pi][0:2, :], in1=ADB[:], op=ALU.add),
                          reads=[pn, 'ADB'], writes=['ADO'])
                    off = l * N_MOD * D + cb * 512
                    tk.dma('act', MOD[:, off:off + 512], ADO[:], reads=['ADO'], writes=['MOD'])
                    yield

        defer_ada = (do_mix is True)
        for _ in ada_gen([0] if defer_ada else list(range(DEPTH)), [(SCR[:], 'SCR'), (GM[:], 'GM')], 0):
            pass

        def modrow(l, j, row):
            off = (l * N_MOD + j) * D
            return MOD[row:row + 1, off:off + D].partition_broadcast(128)

        def load_consts(l, gi, mi, row, gate_scale):
            tk.dma('act', SCR[:], norm_g[l, gi:gi + 1, :].partition_broadcast(128), writes=['SCR'])
            tk.dma('act', GS[:], modrow(l, 3 * mi + 1, row), reads=['MOD'], writes=['GS'])
            tk.dma('act', SH[:], modrow(l, 3 * mi, row), reads=['MOD'], writes=['SH'])
            tk.dma('act', GM[:], modrow(l, 3 * mi + 2, row), reads=['MOD'], writes=['GM'])
            tk.op('dve', lambda e: e.scalar_tensor_tensor(out=GS[:], in0=GS[:], scalar=1.0, in1=SCR[:],
                                                         op0=ALU.add, op1=ALU.mult),
                  reads=['GS', 'SCR'], writes=['GS'])
            if gate_scale != 1.0:
                tk.op('pool', lambda e: e.tensor_scalar(out=GM[:], in0=GM[:], scalar1=gate_scale, scalar2=None,
                                                       op0=ALU.mult), reads=['GM'], writes=['GM'])

        def rstd_from_sumsq(n):
            tk.op('dve', lambda e: e.tensor_scalar(out=ST[:, 1:2], in0=ST[:, 0:1], scalar1=1.0 / n, scalar2=EPS,
                                                  op0=ALU.mult, op1=ALU.add), reads=['ST'], writes=['ST'])
            tk.op('act', lambda e: e.activation(out=ST[:, 3:4], in_=ST[:, 1:2], func=AF.Sqrt), reads=['ST'], writes=['ST'])
            tk.op('dve', lambda e: e.reciprocal(out=ST[:, 2:3], in_=ST[:, 3:4]), reads=['ST'], writes=['ST'])

        def load_x(j, t0):
            tk.dma('sp', B['XT'][:, j, :], X[t0:t0 + 128, :], reads=['X'], writes=[('XT', j)])

        def norm_tile(j, t0, modulate=True):
            XT = B['XT']
            xj = ('XT', j)
            load_x(j, t0)
            tk.op('pool', lambda e: e.memset(ST[:, 0:1], 0.0), writes=['ST'])
            tk.op('act', lambda e: e.activation(out=SCR[:], in_=XT[:, j, :], func=AF.Square, accum_out=ST[:, 0:1]),
                  reads=[xj, 'ST'], writes=['SCR', 'ST'])
            rstd_from_sumsq(D)
            tk.op('dve', lambda e: e.scalar_tensor_tensor(out=SCR[:], in0=XT[:, j, :], scalar=ST[:, 2:3], in1=GS[:],
                                                         op0=ALU.mult, op1=ALU.mult),
                  reads=[xj, 'ST', 'GS'], writes=['SCR'])
            if modulate:
                HB = HBL[j % 2]
                tk.op('dve', lambda e: e.tensor_tensor(out=HB[:], in0=SCR[:], in1=SH[:], op=ALU.add),
                      reads=['SCR', 'SH'], writes=[f'HB{j % 2}'])

        def transpose_hb(dst, j, dname):
            for k4 in range(KC // 4):
                pi = k4 % 2
                pname = f'ps{6 + pi}'
                for k in range(4):
                    kc = k4 * 4 + k
                    tk.op('pe', lambda e, kc=kc, k=k, pi=pi: e.transpose(
                        out=PST[pi][:, k * 128:(k + 1) * 128], in_=HBL[j % 2][:, kc * 128:(kc + 1) * 128], identity=ident[:]),
                        reads=[f'HB{j % 2}', 'ident'], writes=[pname])
                tk.op('act', lambda e, k4=k4, pi=pi: e.copy(
                    out=dst[:, k4 * 4:(k4 + 1) * 4, j * 128:(j + 1) * 128],
                    in_=PST[pi][:, 0:512].rearrange("p (k t) -> p k t", k=4)),
                    reads=[pname], writes=[dname])

        def norm_block(t0, nt):
            for j in range(nt):
                norm_tile(j, t0 + j * 128)
                transpose_hb(B['HT'], j, 'HT')

        wctr = {'up': 0, 'dn': 0}

        def wk(base, s):
            return [f'{base}{s}', f'{base}{s}a', f'{base}{s}b']

        def load_wup(src, deps):
            s = (wctr['up'] // 2) % 2
            wctr['up'] += 2
            tk.dma('sp', B[f'WUP{s}'][:].rearrange("p k c -> p (k c)"), src, reads=need(deps), writes=wk('WUP', s))
            return B[f'WUP{s}'], wk('WUP', s)

        def fm_mm(pi, W, wn, c0, ntok, m=128):
            HT = B['HT']
            for kc in range(KC):
                tk.op('pe', lambda e, kc=kc: e.matmul(
                    PS[pi][0:m, 0:ntok], lhsT=W[:, kc, c0:c0 + m], rhs=HT[:, kc, 0:ntok], start=(kc == 0), stop=(kc == KC - 1)),
                    reads=(wn if isinstance(wn, list) else [wn]) + ['HT'], writes=[f'ps{pi}'])

        def up_ffn(l, ii, ntok):
            AT, HT = B['AT'], B['HT']
            HK = KC // 2
            for hc in range(NH):
                a = 2 * (hc % 2)
                deps = wdeps('up', l, ii, hc)
                if hc % 4 == 3:
                    cast_tick(1)
                for half in range(2):
                    r = wctr['up'] % 4
                    wctr['up'] += 1
                    sl, part = r // 2, r % 2
                    key = f"WUP{sl}{'ab'[part]}"
                    W = B[f'WUP{sl}'][:, part * HK:(part + 1) * HK, :]
                    tk.dma('sp', W, wup_b[l][ii][hc][:, half * HK * 256:(half + 1) * HK * 256].rearrange("p (k c) -> p k c", c=256),
                           reads=deps, writes=[key])
                    for (pi, c0) in ((a, 0), (a + 1, 128)):
                        for k in range(HK):
                            kc = half * HK + k
                            tk.op('pe', lambda e, k=k, kc=kc, pi=pi, c0=c0, W=W: e.matmul(
                                PS[pi][:, 0:ntok], lhsT=W[:, k, c0:c0 + 128], rhs=HT[:, kc, 0:ntok], start=(kc == 0), stop=(kc == KC - 1)),
                                reads=[key, 'HT'], writes=[f'ps{pi}'])
                sg, sgn = B[f'SG{hc % 2}'], f'SG{hc % 2}'
                tk.op('act', lambda e, sg=sg, a=a: e.activation(out=sg[:, 0:ntok], in_=PS[a][:, 0:ntok], func=AF.Silu),
                      reads=[f'ps{a}'], writes=[sgn])
                tk.op('dve', lambda e, sg=sg, a=a, hc=hc: e.tensor_tensor(
                    out=AT[:, hc, 0:ntok], in0=sg[:, 0:ntok], in1=PS[a + 1][:, 0:ntok], op=ALU.mult),
                    reads=[sgn, f'ps{a + 1}'], writes=['AT'])

        def conv3(R, Tt, ntok, rowlen, cw, ci, cwn):
            R3 = R.rearrange("p (r w) -> p r w", w=rowlen)
            T3 = Tt.rearrange("p (r w) -> p r w", w=rowlen)
            tk.op('dve', lambda e: e.tensor_scalar(out=R, in0=Tt, scalar1=cw[:, ci, 1:2], scalar2=None, op0=ALU.mult),
                  reads=['SCR', cwn], writes=['SCR'])
            tk.op('dve', lambda e: e.scalar_tensor_tensor(
                out=R3[:, :, 1:rowlen], in0=T3[:, :, 0:rowlen - 1], scalar=cw[:, ci, 0:1], in1=R3[:, :, 1:rowlen],
                op0=ALU.mult, op1=ALU.add), reads=['SCR', cwn], writes=['SCR'])
            tk.op('dve', lambda e: e.scalar_tensor_tensor(
                out=R3[:, :, 0:rowlen - 1], in0=T3[:, :, 1:rowlen], scalar=cw[:, ci, 2:3], in1=R3[:, :, 0:rowlen - 1],
                op0=ALU.mult, op1=ALU.add), reads=['SCR', cwn], writes=['SCR'])

        def up_sconv(m, ntok, rowlen):
            AT = B['AT']
            R = SCR[:, 0:ntok]
            Tt = SCR[:, 512:512 + ntok]
            for fc in range(KC):
                tk.dma('sp', B['WUP0'][:].rearrange("p k c -> p (k c)"), scw_cu_b[m, fc], reads=need([('scw', m, 'cu', fc)]), writes=wk('WUP', 0))
                tk.dma('sp', B['WUP1'][:, :, 0:128], scw_b_b[m, fc].rearrange("p (k c) -> p k c", c=128),
                       reads=need([('scw', m, 'b', fc)]), writes=wk('WUP', 1))
                fm_mm(0, B['WUP0'], wk('WUP', 0), 0, ntok)
                fm_mm(1, B['WUP0'], wk('WUP', 0), 128, ntok)
                fm_mm(2, B['WUP1'], wk('WUP', 1), 0, ntok)
                sg = B['SG0']
                tk.op('act', lambda e: e.copy(out=sg[:, 0:ntok], in_=PS[0][:, 0:ntok]), reads=['ps0'], writes=['SG0'])
                tk.op('dve', lambda e: e.tensor_tensor(out=Tt, in0=sg[:, 0:ntok], in1=PS[1][:, 0:ntok], op=ALU.mult),
                      reads=['SG0', 'ps1'], writes=['SCR'])
                conv3(R, Tt, ntok, rowlen, CW, fc, 'CW')
                tk.op('dve', lambda e, fc=fc: e.tensor_tensor(out=AT[:, fc, 0:ntok], in0=R, in1=PS[2][:, 0:ntok], op=ALU.mult),
                      reads=['SCR', 'ps2'], writes=['AT'])

        def down_proj(wsrc, nh, depfn, nt):
            AT, XT = B['AT'], B['XT']
            hn = nh // 2
            for db in range(NDB):
                deps = depfn(db)
                pbase = 4 * (db % 2)
                cast_tick(1)
                for half in range(2):
                    r = wctr['dn'] % 4
                    wctr['dn'] += 1
                    sl, part = r // 2, r % 2
                    key = f"WDN{sl}{'ab'[part]}"
                    W = B[f'WDN{sl}'][:, part * (NHA // 2):part * (NHA // 2) + hn, :]
                    tk.dma('sp', W, wsrc(db)[:, half * hn * 256:(half + 1) * hn * 256].rearrange("p (h c) -> p h c", c=256),
                           reads=deps, writes=[key])
                    for j in range(nt):
                        pb = pbase + j
                        for h in range(hn):
                            hc = half * hn + h
                            tk.op('pe', lambda e, W=W, h=h, hc=hc, j=j, pb=pb: e.matmul(
                                PS[pb][:, 0:256], lhsT=AT[:, hc, j * 128:(j + 1) * 128], rhs=W[:, h, :],
                                start=(hc == 0), stop=(hc == nh - 1)),
                                reads=[key, 'AT'], writes=[f'ps{pb}'])
                for j in range(nt):
                    pb = pbase + j
                    pn = f'ps{pb}'
                    xj = ('XT', j)
                    sl_ = slice(db * 256, (db + 1) * 256)
                    tk.op('dve', lambda e, pb=pb, sl_=sl_: e.tensor_tensor(
                        out=SCR[:, sl_], in0=PS[pb][:, 0:256], in1=GM[:, sl_], op=ALU.mult),
                        reads=[pn, 'GM'], writes=['SCR'])
                    tk.op('pool', lambda e, j=j, sl_=sl_: e.tensor_tensor(
                        out=XT[:, j, sl_], in0=XT[:, j, sl_], in1=SCR[:, sl_], op=ALU.add),
                        reads=['SCR', xj], writes=[xj])

        def store_block(t0, nt):
            for j in range(nt):
                tk.dma('act', X[t0 + j * 128:t0 + (j + 1) * 128, :], B['XT'][:, j, :], reads=[('XT', j)], writes=['X'])

        def blocks(with_ctx):
            bl = []
            if with_ctx:
                for t in range(0, NCT, TB):
                    bl.append((t * 128, min(TB, NCT - t), 1))
            for t in range(NCT, NT, TB):
                bl.append((t * 128, min(TB, NT - t), 0))
            return bl

        def ffn_phase(l, i, with_ctx):
            mi = 0 if i == 0 else 2
            ii = 0 if i == 0 else 1
            currow = None
            for (t0, nt, row) in blocks(with_ctx):
                if row != currow:
                    load_consts(l, mi, mi, row, 0.5)
                    currow = row
                norm_block(t0, nt)
                up_ffn(l, ii, nt * 128)
                down_proj(lambda db: wdn_b[l][ii][db], NH, lambda db: wdeps('dn', l, ii, db), nt)
                store_block(t0, nt)

        def sconv_phase(l, with_ctx):
            m = l // 2
            tk.dma('act', CW[:], sccw_in[m], writes=['CW'])
            currow = None
            for (t0, nt, row) in blocks(with_ctx):
                if row != currow:
                    load_consts(l, 1, 1, row, 1.0)
                    currow = row
                norm_block(t0, nt)
                up_sconv(m, nt * 128, 64 if row == 0 else nt * 128)
                down_proj(lambda db: scwo_b[m, db], KC, lambda db: need([('scw', m, 'o', db)]), nt)
                store_block(t0, nt)

        def rope_tables():
            es, sb = mk_scope()
            PAX = sb("PAX", [128, 512])
            ANG = sb("ANG", [128, 512])
            TMP = sb("TMP", [128, 512])
            RES = sb("RES", [128, 4, 512])
            INV = sb("INV", [128, 1])
            YI = sb("YI", [128, 512], mybir.dt.int32)
            YF = sb("YF", [128, 512])
            tk.dma('sp', INV[:], inv_in, writes=['INV'])
            for c0 in range(0, T, 512):
                n = min(512, T - c0)
                tk.dma('sp', PAX[:, 0:n], pax_in[:, c0:c0 + n], writes=['PAX'])
                tk.op('dve', lambda e: e.tensor_scalar(out=ANG[:, 0:n], in0=PAX[:, 0:n], scalar1=INV[:, 0:1], scalar2=None,
                                                      op0=ALU.mult), reads=['PAX', 'INV'], writes=['ANG'])
                for (ri, shift) in ((0, 0.25), (1, 0.0)):
                    tk.op('dve', lambda e, shift=shift: e.tensor_scalar(
                        out=TMP[:, 0:n], in0=ANG[:, 0:n], scalar1=1.0 / (2 * math.pi), scalar2=shift, op0=ALU.mult, op1=ALU.add),
                        reads=['ANG'], writes=['TMP'])
                    tk.op('dve', lambda e: e.tensor_copy(out=YI[:, 0:n], in_=TMP[:, 0:n]), reads=['TMP'], writes=['YI'])
                    tk.op('dve', lambda e: e.tensor_copy(out=YF[:, 0:n], in_=YI[:, 0:n]), reads=['YI'], writes=['YF'])
                    tk.op('dve', lambda e: e.tensor_tensor(out=TMP[:, 0:n], in0=TMP[:, 0:n], in1=YF[:, 0:n], op=ALU.subtract),
                          reads=['TMP', 'YF'], writes=['TMP'])
                    tk.op('dve', lambda e: e.tensor_scalar(out=YF[:, 0:n], in0=TMP[:, 0:n], scalar1=0.5, scalar2=None, op0=ALU.is_gt),
                          reads=['TMP'], writes=['YF'])
                    tk.op('dve', lambda e: e.tensor_tensor(out=TMP[:, 0:n], in0=TMP[:, 0:n], in1=YF[:, 0:n], op=ALU.subtract),
                          reads=['TMP', 'YF'], writes=['TMP'])
                    tk.op('act', lambda e, ri=ri: e.activation(out=RES[:, ri, 0:n], in_=TMP[:, 0:n], func=AF.Sin,
                                                               scale=2 * math.pi - 1e-5),
                          reads=['TMP'], writes=['RES'])
                    tk.op('pool', lambda e, ri=ri: e.tensor_scalar(
                        out=RES[:, 2 + ri, 0:n], in0=RES[:, ri, 0:n], scalar1=128 ** -0.5, scalar2=None, op0=ALU.mult),
                        reads=['RES'], writes=['RES'])
                tk.dma('pool', ROPE[:, :, c0:c0 + n].rearrange("r p t -> p r t"), RES[:, :, 0:n], reads=['RES'], writes=['ROPE'])
            end_scope(es)

        def e1_phase(l):
            m = l // 2
            MCW = B['MCW']
            currow = None
            for (t0, nt, row) in blocks(True):
                ntok = nt * 128
                rowlen = 64 if row == 0 else ntok
                if row != currow:
                    load_consts(l, 1, 1, row, 1.0)
                    currow = row
                norm_block(t0, nt)
                WS = B['AT'][:].rearrange("p h t -> p (h t)").bitcast(F32)
                tabs = [WS[:, i * 512:i * 512 + ntok] for i in range(4)]
                for i in range(4):
                    tk.dma('sp', tabs[i], ROPE[i, :, t0:t0 + ntok], reads=['ROPE'], writes=['AT'])
                SG0, SG1 = B['SG0'], B['SG1']
                for slot in range(8):
                    W, wn = load_wup(wfm_b[m, slot], [('mw', m, 'fm', slot)])
                    fm_mm(0, W, wn, 0, ntok)
                    fm_mm(1, W, wn, 128, ntok)
                    cos, sin = (tabs[0], tabs[1]) if slot < 4 else (tabs[2], tabs[3])
                    for (oi, ta, tb, op) in ((0, cos, sin, ALU.subtract), (1, sin, cos, ALU.add)):
                        tk.op('dve', lambda e, ta=ta: e.tensor_tensor(out=SG0[:, 0:ntok], in0=PS[0][:, 0:ntok], in1=ta, op=ALU.mult),
                              reads=['ps0', 'AT'], writes=['SG0'])
                        tk.op('dve', lambda e, tb=tb: e.tensor_tensor(out=SG1[:, 0:ntok], in0=PS[1][:, 0:ntok], in1=tb, op=ALU.mult),
                              reads=['ps1', 'AT'], writes=['SG1'])
                        tk.op('pool', lambda e, oi=oi, op=op: e.tensor_tensor(
                            out=SCR[:, oi * 512:oi * 512 + ntok], in0=SG0[:, 0:ntok], in1=SG1[:, 0:ntok], op=op),
                            reads=['SG0', 'SG1'], writes=['SCR'])
                    dst = QT if slot < 4 else KT
                    h0 = 2 * (slot % 4)
                    for hh in range(2):
                        for half in range(2):
                            tk.dma('act', dst[h0 + hh, half * 64:(half + 1) * 64, t0:t0 + ntok],
                                   SCR[hh * 64:(hh + 1) * 64, half * 512:half * 512 + ntok], reads=['SCR'], writes=['QK'])
                for slot in range(6):
                    W, wn = load_wup(wfm_b[m, 8 + slot], [('mw', m, 'fm', 8 + slot)])
                    for half in range(2):
                        ch = 2 * slot + half
                        fm_mm(half, W, wn, half * 128, ntok)
                        Tt = SCR[:, 512:512 + ntok]
                        R = SCR[:, 0:ntok]
                        A = SCR[:, 1024:1024 + ntok]
                        tk.op('act', lambda e, half=half: e.copy(out=Tt, in_=PS[half][:, 0:ntok]), reads=[f'ps{half}'], writes=['SCR'])
                        conv3(R, Tt, ntok, rowlen, MCW, ch, 'MCW')
                        tk.op('act', lambda e, ch=ch: e.activation(out=A, in_=R, func=AF.Silu, bias=B['MCB'][:, ch:ch + 1], scale=1.0),
                              reads=['SCR', 'MCB'], writes=['SCR'])
                        if ch >= 8:
                            g = (ch - 8) % 2
                            dst = BTd if ch < 10 else CTd
                            tk.dma('act', dst[g, :, t0:t0 + ntok], A, reads=['SCR'], writes=['BC'])
                        if ch < 10:
                            for j in range(nt):
                                tk.op('pe', lambda e, j=j: e.transpose(out=PS[2][:, j * 128:(j + 1) * 128], in_=A[:, j * 128:(j + 1) * 128],
                                                                       identity=identf[:]), reads=['SCR', 'identf'], writes=['ps2'])
                            tk.op('act', lambda e: e.copy(out=SG0[:, 0:ntok], in_=PS[2][:, 0:ntok]), reads=['ps2'], writes=['SG0'])
                            for j in range(nt):
                                r0 = t0 + j * 128
                                if ch < 8:
                                    tk.dma('act', XSd[r0:r0 + 128, ch * 128:(ch + 1) * 128], SG0[:, j * 128:(j + 1) * 128], reads=['SG0'], writes=['XS'])
                                else:
                                    tk.dma('act', BTOK[r0:r0 + 128, (ch - 8) * 128:(ch - 7) * 128], SG0[:, j * 128:(j + 1) * 128], reads=['SG0'], writes=['XS'])
                HT = B['HT']
                for blk in range(6):
                    s = (wctr['dn'] // 2) % 2
                    wctr['dn'] += 2
                    wn = f'WDN{s}'
                    W = B[wn][:].rearrange("p h c -> p (h c)")[:, 0:KC * 512].rearrange("p (k c) -> p k c", c=512)
                    tk.dma('sp', W, wtm_b[m, blk].rearrange("p (k c) -> p k c", c=512), reads=need([('mw', m, 'tm', blk)]), writes=wk('WDN', s))
                    dst = (Vd, SGd, SZd)[blk // 2]
                    for j in range(nt):
                        pb = 4 + j % 2
                        for kc in range(KC):
                            tk.op('pe', lambda e, kc=kc, j=j, pb=pb, W=W: e.matmul(
                                PS[pb][:, :], lhsT=HT[:, kc, j * 128:(j + 1) * 128], rhs=W[:, kc, :], start=(kc == 0), stop=(kc == KC - 1)),
                                reads=wk('WDN', s) + ['HT'], writes=[f'ps{pb}'])
                        sg, sgn = B[f'SG{j % 2}'], f'SG{j % 2}'
                        if blk < 2:
                            tk.op('act', lambda e, sg=sg, pb=pb: e.copy(out=sg[:], in_=PS[pb][:, :]), reads=[f'ps{pb}'], writes=[sgn])
                        else:
                            tk.op('act', lambda e, sg=sg, pb=pb: e.activation(out=sg[:], in_=PS[pb][:, :], func=AF.Silu), reads=[f'ps{pb}'], writes=[sgn])
                        r0 = t0 + j * 128
                        tk.dma('pool', dst[r0:r0 + 128, (blk % 2) * 512:(blk % 2 + 1) * 512], sg[:], reads=[sgn], writes=['TM'])
                W = B['WUP0'][:].rearrange("p k c -> p (k c)")[:, 0:KC * 32].rearrange("p (k c) -> p k c", c=32)
                tk.dma('sp', W, wdt_b[m].rearrange("p (k c) -> p k c", c=32), reads=need([('mw', m, 'dt')]), writes=wk('WUP', 0))
                for j in range(nt):
                    for kc in range(KC):
                        tk.op('pe', lambda e, kc=kc, j=j: e.matmul(
                            PS[3][:, 0:32], lhsT=HT[:, kc, j * 128:(j + 1) * 128], rhs=W[:, kc, :], start=(kc == 0), stop=(kc == KC - 1)),
                            reads=wk('WUP', 0) + ['HT'], writes=['ps3'])
                    DTt = B['DTT']
                    tk.op('dve', lambda e: e.tensor_tensor(out=DTt[:, 0:32], in0=PS[3][:, 0:32], in1=B['DTB'][:], op=ALU.add),
                          reads=['ps3', 'DTB'], writes=['DTT'])
                    tk.op('act', lambda e: e.activation(out=DTt[:, 32:64], in_=DTt[:, 0:32], func=AF.Exp), reads=['DTT'], writes=['DTT'])
                    tk.op('act', lambda e: e.activation(out=DTt[:, 64:96], in_=DTt[:, 32:64], func=AF.Ln, bias=B['ONE1'][:, 0:1], scale=1.0),
                          reads=['DTT', 'ONE1'], writes=['DTT'])
                    tk.op('dve', lambda e: e.tensor_tensor(out=DTt[:, 96:128], in0=DTt[:, 64:96], in1=B['A32'][:], op=ALU.mult),
                          reads=['DTT', 'A32'], writes=['DTT'])
                    r0 = t0 + j * 128
                    tk.dma('pool', DTd[r0:r0 + 128, :], DTt[:, 64:96], reads=['DTT'], writes=['DT'])
                    tk.dma('pool', DTAd[r0:r0 + 128, :], DTt[:, 96:128], reads=['DTT'], writes=['DT'])

        def mixer_consts(m, sb):
            MCW = sb("MCW", [128, 12, 3])
            MCB = sb("MCB", [128, 12])
            DTB = sb("DTB", [128, 32])
            A32 = sb("A32", [128, 32])
            ONE1 = sb("ONE1", [128, 1])
            DTT = sb("DTT", [128, 128])
            tk.dma('act', MCW[:], mcw_in[m], writes=['MCW'])
            tk.dma('act', MCB[:], mcb_in[m], writes=['MCB'])
            tk.dma('act', DTB[:], dtb_in[m].partition_broadcast(128), writes=['DTB'])
            tk.dma('act', A32[:], alog_in[m].partition_broadcast(128), writes=['A32'])
            tk.op('act', lambda e: e.activation(out=A32[:], in_=A32[:], func=AF.Exp), reads=['A32'], writes=['A32'])
            tk.op('dve', lambda e: e.tensor_scalar(out=A32[:], in0=A32[:], scalar1=-1.0, scalar2=None, op0=ALU.mult),
                  reads=['A32'], writes=['A32'])
            tk.op('pool', lambda e: e.memset(ONE1[:], 1.0), writes=['ONE1'])

        def scan_phases(l):
            m = l // 2
            es, sb = mk_scope()
            TRIF = sb("TRIF", [128, 128]); TRIB = sb("TRIB", [128, 128])
            MF = sb("MF", [128, 128]); MB = sb("MB", [128, 128]); ONES = sb("ONES", [128, 128])
            LG = sb("LG", [128, 16]); CD = sb("CD", [128, 16]); KD = sb("KD", [128, 16])
            DC = sb("DC", [128, RET_H, 128]); DQF = sb("DQF", [128, RET_H, 128]); DQB = sb("DQB", [128, RET_H, 128])
            GNG = sb("GNG", [128, 1024]); SNG = sb("SNG", [128, 1024]); DSK = sb("DSK", [128, 16])
            TMPA = sb("TMPA", [128, 128]); TMPB = sb("TMPB", [128, 128]); PCOL = sb("PCOL", [128, 2])
            ONE1 = sb("ONE1", [128, 1])
            tk.op('pool', lambda e: e.memset(ONE1[:], 1.0), writes=['ONE1'])
            tk.op('pool', lambda e: e.memset(ONES[:], 1.0), writes=['ONES'])
            tk.op('dve', lambda e: e.tensor_scalar(out=TRIF[:], in0=IOT[:], scalar1=0.0, scalar2=None, op0=ALU.is_ge), reads=['IOT'], writes=['TRIF'])
            tk.op('dve', lambda e: e.tensor_scalar(out=TRIB[:], in0=IOT[:], scalar1=0.0, scalar2=None, op0=ALU.is_le), reads=['IOT'], writes=['TRIB'])
            tk.op('dve', lambda e: e.tensor_scalar(out=MF[:], in0=IOT[:], scalar1=0.0, scalar2=-BIG, op0=ALU.is_lt, op1=ALU.mult), reads=['IOT'], writes=['MF'])
            tk.op('dve', lambda e: e.tensor_scalar(out=MB[:], in0=IOT[:], scalar1=0.0, scalar2=-BIG, op0=ALU.is_gt, op1=ALU.mult), reads=['IOT'], writes=['MB'])
            tk.dma('act', LG[:], rld_in[m].partition_broadcast(128), writes=['LG'])
            tk.dma('act', GNG[:], gng_in[m].partition_broadcast(128), writes=['GNG'])
            tk.dma('act', SNG[:], sng_in[m].partition_broadcast(128), writes=['SNG'])
            tk.dma('act', DSK[:], ssdd_in[m].partition_broadcast(128), writes=['DSK'])
            tk.op('act', lambda e: e.activation(out=CD[:], in_=LG[:], func=AF.Exp, scale=128.0), reads=['LG'], writes=['CD'])
            tk.op('dve', lambda e: e.tensor_copy(out=PCOL[:, 0:1], in_=IOT[:, 127:128]), reads=['IOT'], writes=['PCOL'])
            tk.op('dve', lambda e: e.tensor_scalar(out=PCOL[:, 1:2], in0=IOT[:, 0:1], scalar1=-1.0, scalar2=None, op0=ALU.mult), reads=['IOT'], writes=['PCOL'])
            tk.op('dve', lambda e: e.tensor_scalar(out=KD[:, 0:8], in0=LG[:, 0:8], scalar1=PCOL[:, 0:1], scalar2=None, op0=ALU.mult), reads=['LG', 'PCOL'], writes=['KD'])
            tk.op('dve', lambda e: e.tensor_scalar(out=KD[:, 8:16], in0=LG[:, 8:16], scalar1=PCOL[:, 1:2], scalar2=None, op0=ALU.mult), reads=['LG', 'PCOL'], writes=['KD'])
            tk.op('act', lambda e: e.activation(out=KD[:], in_=KD[:], func=AF.Exp), reads=['KD'], writes=['KD'])
            TPOS = sb("TPOS", [128, 128])
            tk.op('dve', lambda e: e.tensor_scalar(out=TPOS[:], in0=IOT[:], scalar1=PCOL[:, 1:2], scalar2=None, op0=ALU.add), reads=['IOT', 'PCOL'], writes=['TPOS'])
            for h in range(RET_H):
                tk.op('dve', lambda e, h=h: e.scalar_tensor_tensor(out=TMPA[:], in0=IOT[:], scalar=LG[:, h:h + 1], in1=MF[:], op0=ALU.mult, op1=ALU.add),
                      reads=['IOT', 'LG', 'MF'], writes=['TMPA'])
                tk.op('act', lambda e: e.activation(out=TMPA[:], in_=TMPA[:], func=AF.Exp), reads=['TMPA'], writes=['TMPA'])
                tk.op('dve', lambda e, h=h: e.tensor_scalar(out=TMPB[:], in0=IOT[:], scalar1=LG[:, 8 + h:9 + h], scalar2=-1.0, op0=ALU.mult, op1=ALU.mult),
                      reads=['IOT', 'LG'], writes=['TMPB'])
                tk.op('dve', lambda e: e.tensor_tensor(out=TMPB[:], in0=TMPB[:], in1=MB[:], op=ALU.add), reads=['TMPB', 'MB'], writes=['TMPB'])
                tk.op('act', lambda e: e.activation(out=TMPB[:], in_=TMPB[:], func=AF.Exp), reads=['TMPB'], writes=['TMPB'])
                tk.op('dve', lambda e, h=h: e.tensor_tensor(out=DC[:, h, :], in0=TMPA[:], in1=TMPB[:], op=ALU.add), reads=['TMPA', 'TMPB'], writes=['DC'])
                tk.op('dve', lambda e, h=h: e.tensor_scalar(out=TMPA[:], in0=TPOS[:], scalar1=1.0, scalar2=LG[:, h:h + 1], op0=ALU.add, op1=ALU.mult),
                      reads=['TPOS', 'LG'], writes=['TMPA'])
                tk.op('act', lambda e, h=h: e.activation(out=DQF[:, h, :], in_=TMPA[:], func=AF.Exp), reads=['TMPA'], writes=['DQF'])
                tk.op('dve', lambda e, h=h: e.tensor_scalar(out=TMPB[:], in0=TPOS[:], scalar1=-128.0, scalar2=LG[:, 8 + h:9 + h], op0=ALU.add, op1=ALU.mult),
                      reads=['TPOS', 'LG'], writes=['TMPB'])
                tk.op('act', lambda e, h=h: e.activation(out=DQB[:, h, :], in_=TMPB[:], func=AF.Exp, scale=-1.0), reads=['TMPB'], writes=['DQB'])
            CH = []
            for par in range(2):
                d = {}
                for nm, shp in (("KTc", [128, RET_H, 128]), ("QTc", [128, RET_H, 128]), ("Vc", [128, 1024]), ("XSc", [128, 1024]),
                                ("SGc", [128, 1024]), ("SZc", [128, 1024]), ("BTc", [128, 2, 128]), ("CTc", [128, 2, 128]),
                                ("BKc", [128, 256]), ("DTc", [128, 32]), ("DTAc", [128, 32]),
                                ("SBi", [128, RET_H, 128]), ("HBi", [128, 2, 512])):
                    d[nm] = sb(f"{nm}{par}", shp)
                d['par'] = par
                CH.append(d)
            SF = sb("SF", [128, RET_H, 128]); HF = sb("HF", [128, 2, 512])
            KXs = [sb(f"KX{i}", [128, 128]) for i in range(2)]; PTs = [sb(f"PT{i}", [128, 128]) for i in range(2)]
            QFs = [sb(f"QF{i}", [128, 128]) for i in range(2)]; QBs = [sb(f"QB{i}", [128, 128]) for i in range(2)]
            ada_it = ada_gen([x for x in (l + 1, l + 2) if x < DEPTH] if defer_ada else [], [(GS[:], 'GS'), (SH[:], 'SH')], 3)

            def ada_tick(n=1):
                for _ in range(n):
                    next(ada_it, None)
            O8 = sb("O8", [128, RET_H, 128]); D8 = sb("D8", [128, RET_H, 128]); S8 = sb("S8", [128, 32])
            CUM = sb("CUM", [128, 32]); TOT = sb("TOT", [128, 32]); ECUM = sb("ECUM", [128, 32]); ETOT = sb("ETOT", [128, 32]); WG = sb("WG", [128, 32])
            ZH = [GM[:].rearrange("p (a t) -> p a t", t=128), SCR[:].rearrange("p (a t) -> p a t", t=128)]
            ZK = ['GM', 'SCR']
            CBs = sb("CBs", [128, 128])
            EFs = [sb(f"EF{i}", [128, 128]) for i in range(2)]; EBs = [sb(f"EB{i}", [128, 128]) for i in range(2)]
            WTs = [sb(f"WT{i}", [128, 128]) for i in range(2)]
            XWs = [sb(f"XW{i}", [128, 512]) for i in range(2)]
            YG = sb("YG", [128, 1024]); YT = sb("YT", [128, 512])
            MB16 = sb("MB16", [128, D], BF16); MTc = sb("MTc", [128, KC, 128], BF16)

            def bc8(ap8):
                return ap8.unsqueeze(2).to_broadcast([128, 8, 64])

            def interleave(gens, width):
                gens = iter(gens)
                active = []
                while True:
                    while len(active) < width:
                        g = next(gens, None)
                        if g is None:
                            break
                        active.append(g)
                    if not active:
                        break
                    for g in list(active):
                        try:
                            next(g)
                        except StopIteration:
                            active.remove(g)

            def k_(C, nm):
                return f"{nm}{C['par']}"

            def load_chunk(C, c, full, sbsrc=None):
                r0 = c * 128
                tk.dma('sp', C['KTc'][:], KT[:, :, r0:r0 + 128].rearrange("h d t -> d h t"), reads=['QK'], writes=[k_(C, 'KTc')])
                tk.dma('sp', C['Vc'][:], Vd[r0:r0 + 128, :], reads=['TM'], writes=[k_(C, 'Vc')])
                tk.dma('sp', C['XSc'][:], XSd[r0:r0 + 128, :], reads=['XS'], writes=[k_(C, 'XSc')])
                tk.dma('sp', C['BKc'][:], BTOK[r0:r0 + 128, :], reads=['XS'], writes=[k_(C, 'BKc')])
                tk.dma('sp', C['DTc'][:], DTd[r0:r0 + 128, :], reads=['DT'], writes=[k_(C, 'DTc')])
                tk.dma('sp', C['DTAc'][:], DTAd[r0:r0 + 128, :], reads=['DT'], writes=[k_(C, 'DTAc')])
                if full:
                    tk.dma('sp', C['QTc'][:], QT[:, :, r0:r0 + 128].rearrange("h d t -> d h t"), reads=['QK'], writes=[k_(C, 'QTc')])
                    tk.dma('sp', C['SGc'][:], SGd[r0:r0 + 128, :], reads=['TM'], writes=[k_(C, 'SGc')])
                    tk.dma('sp', C['SZc'][:], SZd[r0:r0 + 128, :], reads=['TM'], writes=[k_(C, 'SZc')])
                    tk.dma('sp', C['BTc'][:], BTd[:, :, r0:r0 + 128].rearrange("g n t -> n g t"), reads=['BC'], writes=[k_(C, 'BTc')])
                    tk.dma('sp', C['CTc'][:], CTd[:, :, r0:r0 + 128].rearrange("g n t -> n g t"), reads=['BC'], writes=[k_(C, 'CTc')])
                    tk.dma('sp', C['SBi'][:].rearrange("p h d -> p (h d)"), SBst[c], reads=['SBst'], writes=[(k_(C, 'SBi'), h) for h in range(RET_H)])
                    tk.dma('sp', C['HBi'][:].rearrange("p g d -> p (g d)"), HBst[c], reads=['HBst'], writes=[k_(C, 'HBi')])

            def ret_state_update(C, S, sname, h, kdcol, cdcol):
                par = h % 2
                pa, pb = (0, 2) if par == 0 else (4, 7)
                KX, kxn = KXs[par], f'KX{par}'
                KTc, Vc = C['KTc'], C['Vc']
                tk.op('pe', lambda e: e.transpose(out=PS[pa][:, 0:128], in_=KTc[:, h, :], identity=identf[:]), reads=[k_(C, 'KTc'), 'identf'], writes=[f'ps{pa}'])
                yield
                tk.op('dve', lambda e: e.tensor_scalar(out=KX[:], in0=PS[pa][:, 0:128], scalar1=KD[:, kdcol:kdcol + 1], scalar2=None, op0=ALU.mult),
                      reads=[f'ps{pa}', 'KD'], writes=[kxn])
                yield
                tk.op('pe', lambda e: e.matmul(PS[pb][:, 0:128], lhsT=KX[:], rhs=Vc[:, h * 128:(h + 1) * 128], start=True, stop=True),
                      reads=[kxn, k_(C, 'Vc')], writes=[f'ps{pb}'])
                yield
                tk.op('dve', lambda e: e.scalar_tensor_tensor(out=S[:, h, :], in0=S[:, h, :], scalar=CD[:, cdcol:cdcol + 1], in1=PS[pb][:, 0:128],
                                                             op0=ALU.mult, op1=ALU.add), reads=[(sname, h), 'CD', f'ps{pb}'], writes=[(sname, h)])
                yield

            def ssd_cols(C, dirs):
                DTAc, DTc = C['DTAc'], C['DTc']
                for d in dirs:
                    tri = TRIF if d == 0 else TRIB
                    tk.op('pe', lambda e, d=d, tri=tri: e.matmul(PS[3][:, d * 16:(d + 1) * 16], lhsT=tri[:], rhs=DTAc[:, d * 16:(d + 1) * 16], start=True, stop=True),
                          reads=['TRIF', 'TRIB', k_(C, 'DTAc')], writes=['ps3'])
                    tk.op('pe', lambda e, d=d: e.matmul(PS[3][:, 32 + d * 16:32 + (d + 1) * 16], lhsT=ONES[:], rhs=DTAc[:, d * 16:(d + 1) * 16], start=True, stop=True),
                          reads=['ONES', k_(C, 'DTAc')], writes=['ps3'])
                lo, hi = min(dirs) * 16, (max(dirs) + 1) * 16
                tk.op('act', lambda e: e.copy(out=CUM[:, lo:hi], in_=PS[3][:, lo:hi]), reads=['ps3'], writes=['CUM'])
                tk.op('act', lambda e: e.copy(out=TOT[:, lo:hi], in_=PS[3][:, 32 + lo:32 + hi]), reads=['ps3'], writes=['TOT'])
                tk.op('act', lambda e: e.activation(out=ECUM[:, lo:hi], in_=CUM[:, lo:hi], func=AF.Exp), reads=['CUM'], writes=['ECUM'])
                tk.op('act', lambda e: e.activation(out=ETOT[:, lo:hi], in_=TOT[:, lo:hi], func=AF.Exp), reads=['TOT'], writes=['ETOT'])
                tk.op('dve', lambda e: e.tensor_tensor(out=WG[:, lo:hi], in0=TOT[:, lo:hi], in1=CUM[:, lo:hi], op=ALU.subtract), reads=['TOT', 'CUM'], writes=['WG'])
                tk.op('act', lambda e: e.activation(out=WG[:, lo:hi], in_=WG[:, lo:hi], func=AF.Exp), reads=['WG'], writes=['WG'])
                tk.op('dve', lambda e: e.tensor_tensor(out=WG[:, lo:hi], in0=WG[:, lo:hi], in1=DTc[:, lo:hi], op=ALU.mult), reads=['WG', k_(C, 'DTc')], writes=['WG'])

            def ssd_state_update(C, H, hname, g, d):
                col = d * 16 + g * 8
                XW, xwn = XWs[g], f'XW{g}'
                pb = 5 if g == 0 else 6
                XSc, BKc = C['XSc'], C['BKc']
                xs3 = XSc[:, g * 512:(g + 1) * 512].rearrange("p (e q) -> p e q", q=64)
                xw3 = XW[:].rearrange("p (e q) -> p e q", q=64)
                tk.op('dve', lambda e: e.tensor_tensor(out=xw3, in0=xs3, in1=bc8(WG[:, col:col + 8]), op=ALU.mult), reads=[k_(C, 'XSc'), 'WG'], writes=[xwn])
                yield
                tk.op('pe', lambda e: e.matmul(PS[pb][:, :], lhsT=BKc[:, g * 128:(g + 1) * 128], rhs=XW[:], start=True, stop=True),
                      reads=[k_(C, 'BKc'), xwn], writes=[f'ps{pb}'])
                yield
                h3 = H[:, g, :].rearrange("p (e q) -> p e q", q=64)
                tk.op('dve', lambda e: e.tensor_tensor(out=h3, in0=h3, in1=bc8(ETOT[:, col:col + 8]), op=ALU.mult), reads=[(hname, g), 'ETOT'], writes=[(hname, g)])
                yield
                tk.op('dve', lambda e: e.tensor_tensor(out=H[:, g, :], in0=H[:, g, :], in1=PS[pb][:, :], op=ALU.add), reads=[(hname, g), f'ps{pb}'], writes=[(hname, g)])
                yield

            if stop == 'sc_const':
                end_scope(es)
                return
            SBs = sb("SBs", [128, RET_H, 128]); HBs = sb("HBs", [128, 2, 512])
            tk.op('pool', lambda e: e.memset(SBs[:], 0.0), writes=[('SBs', h) for h in range(RET_H)])
            tk.op('pool', lambda e: e.memset(HBs[:], 0.0), writes=[('HBs', g) for g in range(2)])
            tk.op('pool', lambda e: e.memset(SF[:], 0.0), writes=[('SF', h) for h in range(RET_H)])
            tk.op('pool', lambda e: e.memset(HF[:], 0.0), writes=[('HF', g) for g in range(2)])
            bchain = list(range(NCT - 1, -1, -1)) + list(range(NT - 1, NCT - 1, -1))
            load_chunk(CH[0], bchain[0], False)
            for ci, c in enumerate(bchain):
                C = CH[ci % 2]
                if ci + 1 < len(bchain):
                    load_chunk(CH[(ci + 1) % 2], bchain[ci + 1], False)
                tk.dma('pool', SBst[c], SBs[:].rearrange("p h d -> p (h d)"), reads=[('SBs', h) for h in range(RET_H)], writes=['SBst'])
                tk.dma('pool', HBst[c], HBs[:].rearrange("p g d -> p (g d)"), reads=[('HBs', g) for g in range(2)], writes=['HBst'])
                ada_tick()
                ssd_cols(C, [1])
                interleave([ssd_state_update(C, HBs, 'HBs', g, 1) for g in range(2)], 2)
                interleave([ret_state_update(C, SBs, 'SBs', h, 8 + h, 8 + h) for h in range(RET_H)], 2)
            if stop == 'e2':
                end_scope(es)
                return

            def ret_head(C, h):
                par = h % 2
                pa, pb = (0, 1) if par == 0 else (4, 5)
                PT, QF, QB = PTs[par], QFs[par], QBs[par]
                ptn, qfn, qbn = f'PT{par}', f'QF{par}', f'QB{par}'
                KTc, QTc, Vc, SBin = C['KTc'], C['QTc'], C['Vc'], C['SBi']
                tk.op('pe', lambda e: e.matmul(PS[pa][:, 0:128], lhsT=KTc[:, h, :], rhs=QTc[:, h, :], start=True, stop=True),
                      reads=[k_(C, 'KTc'), k_(C, 'QTc')], writes=[f'ps{pa}'])
                tk.op('pool', lambda e: e.tensor_tensor(out=QF[:], in0=QTc[:, h, :], in1=DQF[:, h, :], op=ALU.mult), reads=[k_(C, 'QTc'), 'DQF'], writes=[qfn])
                yield
                tk.op('dve', lambda e: e.tensor_tensor(out=PT[:], in0=PS[pa][:, 0:128], in1=DC[:, h, :], op=ALU.mult), reads=[f'ps{pa}', 'DC'], writes=[ptn])
                tk.op('pool', lambda e: e.tensor_tensor(out=QB[:], in0=QTc[:, h, :], in1=DQB[:, h, :], op=ALU.mult), reads=[k_(C, 'QTc'), 'DQB'], writes=[qbn])
                yield
                tk.op('pe', lambda e: e.matmul(PS[pb][:, 0:128], lhsT=PT[:], rhs=Vc[:, h * 128:(h + 1) * 128], start=True, stop=False),
                      reads=[ptn, k_(C, 'Vc')], writes=[f'ps{pb}'])
                tk.op('pe', lambda e: e.matmul(PS[pb][:, 0:128], lhsT=QF[:], rhs=SF[:, h, :], start=False, stop=False),
                      reads=[qfn, ('SF', h)], writes=[f'ps{pb}'])
                tk.op('pe', lambda e: e.matmul(PS[pb][:, 0:128], lhsT=QB[:], rhs=SBin[:, h, :], start=False, stop=True),
                      reads=[qbn, (k_(C, 'SBi'), h)], writes=[f'ps{pb}'])
                yield
                tk.op('act', lambda e: e.copy(out=O8[:, h, :], in_=PS[pb][:, 0:128]), reads=[f'ps{pb}'], writes=[('O8', h)])
                yield
                yield from ret_state_update(C, SF, 'SF', h, h, h)

            def ssd_head(C, g, e_):
                idx = g * 8 + e_
                par = idx % 2
                o = par * 256
                EF, EB, WT = EFs[par], EBs[par], WTs[par]
                efn, ebn, wtn = f'EF{par}', f'EB{par}', f'WT{par}'
                DTc, XSc = C['DTc'], C['XSc']
                if par == 0:
                    q = idx // 2
                    zq = ZH[q // 4][:, 4 * (q % 4):4 * (q % 4) + 4, :]
                    tk.op('pe', lambda e: e.matmul(PS[4][:, :], lhsT=ONES[:], rhs=zq.rearrange("p a t -> p (a t)"),
                                                   start=True, stop=True), reads=['ONES', ZK[q // 4]], writes=['ps4'])
                yield
                tk.op('dve', lambda e: e.scalar_tensor_tensor(out=EF[:], in0=PS[4][:, o:o + 128], scalar=CUM[:, idx:idx + 1], in1=MF[:],
                                                             op0=ALU.subtract, op1=ALU.add), reads=['ps4', 'CUM', 'MF'], writes=[efn])
                tk.op('dve', lambda e: e.scalar_tensor_tensor(out=EB[:], in0=PS[4][:, o + 128:o + 256], scalar=CUM[:, 16 + idx:17 + idx], in1=MB[:],
                                                             op0=ALU.subtract, op1=ALU.add), reads=['ps4', 'CUM', 'MB'], writes=[ebn])
                yield
                tk.op('act', lambda e: e.activation(out=EF[:], in_=EF[:], func=AF.Exp), reads=[efn], writes=[efn])
                tk.op('act', lambda e: e.activation(out=EB[:], in_=EB[:], func=AF.Exp), reads=[ebn], writes=[ebn])
                yield
                tk.op('pool', lambda e: e.tensor_scalar(out=EF[:], in0=EF[:], scalar1=DTc[:, idx:idx + 1], scalar2=None, op0=ALU.mult),
                      reads=[efn, k_(C, 'DTc')], writes=[efn])
                yield
                tk.op('dve', lambda e: e.scalar_tensor_tensor(out=EB[:], in0=EB[:], scalar=DTc[:, 16 + idx:17 + idx], in1=EF[:],
                                                             op0=ALU.mult, op1=ALU.add), reads=[ebn, k_(C, 'DTc'), efn], writes=[ebn])
                yield
                tk.op('pool', lambda e: e.tensor_tensor(out=WT[:], in0=EB[:], in1=CBs[:], op=ALU.mult), reads=[ebn, 'CBs'], writes=[wtn])
                yield
                tk.op('pe', lambda e: e.matmul(PS[6][:, e_ * 64:(e_ + 1) * 64], lhsT=WT[:], rhs=XSc[:, idx * 64:(idx + 1) * 64], start=True, stop=True),
                      reads=[wtn, k_(C, 'XSc')], writes=['ps6'])
                yield

            load_chunk(CH[0], 0, True)
            for c in range(NT):
                r0 = c * 128
                C = CH[c % 2]
                if c + 1 < NT:
                    load_chunk(CH[(c + 1) % 2], c + 1, True)
                ada_tick()
                KTc, QTc, Vc, XSc, SGc, SZc, BTc, CTc, DTc, DTAc, HBin = (C[n] for n in ('KTc', 'QTc', 'Vc', 'XSc', 'SGc', 'SZc', 'BTc', 'CTc', 'DTc', 'DTAc', 'HBi'))
                ssd_cols(C, [0, 1])
                for zh in range(2):
                    z4 = ZH[zh].rearrange("p (i d) t -> p i d t", d=2)
                    for d in range(2):
                        tri = TRIF if d == 0 else TRIB
                        c0 = d * 16 + zh * 8
                        tk.op('dve', lambda e, d=d, tri=tri, z4=z4, c0=c0: e.tensor_tensor(
                            out=z4[:, :, d, :], in0=tri[:].unsqueeze(1).to_broadcast([128, 8, 128]),
                            in1=DTAc[:, c0:c0 + 8].unsqueeze(2).to_broadcast([128, 8, 128]), op=ALU.mult),
                            reads=['TRIF', 'TRIB', k_(C, 'DTAc')], writes=[ZK[zh]])
                interleave([ret_head(C, h) for h in range(RET_H)], 2)
                o8k = [('O8', h) for h in range(RET_H)]
                tk.op('dve', lambda e: e.reduce_sum(out=S8[:, 0:8], in_=O8[:], axis=AX.X), reads=o8k, writes=['S8'])
                tk.op('dve', lambda e: e.tensor_scalar(out=S8[:, 0:8], in0=S8[:, 0:8], scalar1=1.0 / 128, scalar2=None, op0=ALU.mult), reads=['S8'], writes=['S8'])
                tk.op('dve', lambda e: e.tensor_tensor(out=D8[:], in0=O8[:], in1=S8[:, 0:8].unsqueeze(2).to_broadcast([128, 8, 128]), op=ALU.subtract),
                      reads=o8k + ['S8'], writes=['D8'])
                tk.op('pool', lambda e: e.tensor_tensor(out=O8[:], in0=D8[:], in1=D8[:], op=ALU.mult), reads=['D8'], writes=o8k)
                tk.op('dve', lambda e: e.reduce_sum(out=S8[:, 8:16], in_=O8[:], axis=AX.X), reads=o8k, writes=['S8'])
                tk.op('dve', lambda e: e.tensor_scalar(out=S8[:, 8:16], in0=S8[:, 8:16], scalar1=1.0 / 128, scalar2=EPS, op0=ALU.mult, op1=ALU.add), reads=['S8'], writes=['S8'])
                tk.op('act', lambda e: e.activation(out=S8[:, 16:24], in_=S8[:, 8:16], func=AF.Sqrt), reads=['S8'], writes=['S8'])
                tk.op('dve', lambda e: e.reciprocal(out=S8[:, 24:32], in_=S8[:, 16:24]), reads=['S8'], writes=['S8'])
                tk.op('dve', lambda e: e.tensor_tensor(out=D8[:], in0=D8[:], in1=S8[:, 24:32].unsqueeze(2).to_broadcast([128, 8, 128]), op=ALU.mult),
                      reads=['D8', 'S8'], writes=['D8'])
                d8f = D8[:].rearrange("p h d -> p (h d)")
                tk.op('pool', lambda e: e.tensor_tensor(out=d8f, in0=d8f, in1=GNG[:], op=ALU.mult), reads=['D8', 'GNG'], writes=['D8'])
                tk.op('dve', lambda e: e.tensor_tensor(out=MB16[:, 0:1024], in0=d8f, in1=SGc[:], op=ALU.mult), reads=['D8', k_(C, 'SGc')], writes=[('MB16', 0)])
                if stop == 'e3ret':
                    continue
                for g in range(2):
                    tk.op('pe', lambda e, g=g: e.matmul(PS[5][:, 0:128], lhsT=BTc[:, g, :], rhs=CTc[:, g, :], start=True, stop=True),
                          reads=[k_(C, 'BTc'), k_(C, 'CTc')], writes=['ps5'])
                    tk.op('act', lambda e: e.copy(out=CBs[:], in_=PS[5][:, 0:128]), reads=['ps5'], writes=['CBs'])
                    interleave([ssd_head(C, g, e_) for e_ in range(8)], 2)
                    yg = YG[:, g * 512:(g + 1) * 512]
                    yt3 = YT[:].rearrange("p (e q) -> p e q", q=64)
                    tk.op('act', lambda e, yg=yg: e.copy(out=yg, in_=PS[6][:, :]), reads=['ps6'], writes=[('YG', g)])
                    for d, H, hn in ((0, HF, ('HF', g)), (1, HBin, k_(C, 'HBi'))):
                        col = d * 16 + g * 8
                        tk.op('pe', lambda e, g=g, H=H: e.matmul(PS[7][:, :], lhsT=CTc[:, g, :], rhs=H[:, g, :], start=True, stop=True),
                              reads=[k_(C, 'CTc'), hn], writes=['ps7'])
                        tk.op('dve', lambda e, col=col: e.tensor_tensor(out=yt3, in0=PS[7][:, :].rearrange("p (e q) -> p e q", q=64), in1=bc8(ECUM[:, col:col + 8]), op=ALU.mult),
                              reads=['ps7', 'ECUM'], writes=['YT'])
                        tk.op('pool', lambda e, yg=yg: e.tensor_tensor(out=yg, in0=yg, in1=YT[:], op=ALU.add), reads=[('YG', g), 'YT'], writes=[('YG', g)])
                    for _ in ssd_state_update(C, HF, 'HF', g, 0):
                        pass
                ygk = [('YG', 0), ('YG', 1)]
                xs3a = XSc[:].rearrange("p (i q) -> p i q", q=64)
                tk.op('dve', lambda e: e.tensor_tensor(out=xs3a, in0=xs3a, in1=DSK[:].unsqueeze(2).to_broadcast([128, 16, 64]), op=ALU.mult),
                      reads=[k_(C, 'XSc'), 'DSK'], writes=[k_(C, 'XSc')])
                tk.op('pool', lambda e: e.tensor_tensor(out=YG[:], in0=YG[:], in1=XSc[:], op=ALU.add), reads=ygk + [k_(C, 'XSc')], writes=ygk)
                tk.op('dve', lambda e: e.tensor_tensor(out=YG[:], in0=YG[:], in1=SZc[:], op=ALU.mult), reads=ygk + [k_(C, 'SZc')], writes=ygk)
                tk.op('pool', lambda e: e.memset(ST[:, 0:1], 0.0), writes=['ST'])
                tk.op('act', lambda e: e.activation(out=XSc[:], in_=YG[:], func=AF.Square, accum_out=ST[:, 0:1]), reads=ygk + ['ST'], writes=[k_(C, 'XSc'), 'ST'])
                rstd_from_sumsq(1024)
                tk.op('dve', lambda e: e.scalar_tensor_tensor(out=MB16[:, 1024:2048], in0=YG[:], scalar=ST[:, 2:3], in1=SNG[:], op0=ALU.mult, op1=ALU.mult),
                      reads=ygk + ['ST', 'SNG'], writes=[('MB16', 1)])
                if debug:
                    tk.op('dve', lambda e: e.tensor_copy(out=SCR[:], in_=MB16[:]), reads=[('MB16', 0), ('MB16', 1)], writes=['SCR'])
                    tk.dma('pool', MIXd[r0:r0 + 128, :], SCR[:], reads=['SCR'], writes=['MIXd'])
                for k4 in range(KC // 4):
                    pi = k4 % 2
                    pname = f'ps{6 + pi}'
                    for k in range(4):
                        kc = k4 * 4 + k
                        tk.op('pe', lambda e, kc=kc, k=k, pi=pi: e.transpose(
                            out=PST[pi][:, k * 128:(k + 1) * 128], in_=MB16[:, kc * 128:(kc + 1) * 128], identity=ident[:]),
                            reads=[('MB16', kc // 8), 'ident'], writes=[pname])
                    tk.op('act', lambda e, k4=k4, pi=pi: e.copy(
                        out=MTc[:, k4 * 4:(k4 + 1) * 4, :], in_=PST[pi][:, 0:512].rearrange("p (k t) -> p k t", k=4)),
                        reads=[pname], writes=['MTc'])
                tk.dma('pool', MTd[:, :, r0:r0 + 128].rearrange("k p t -> p k t"), MTc[:], reads=['MTc'], writes=['MTd'])
            for _ in ada_it:
                pass
            end_scope(es)

        def e4_phase(l, with_ctx):
            m = l // 2
            currow = None
            for (t0, nt, row) in blocks(with_ctx):
                ntok = nt * 128
                if row != currow:
                    load_consts(l, 1, 1, row, 1.0)
                    currow = row
                for j in range(nt):
                    load_x(j, t0 + j * 128)
                tk.dma('sp', B['AT'][:, 0:KC, 0:ntok], MTd[:, :, t0:t0 + ntok].rearrange("k p t -> p k t"), reads=['MTd'], writes=['AT'])
                down_proj(lambda db: mwo_b[m, db], KC, lambda db: need([('mw', m, 'o', db)]), nt)
                store_block(t0, nt)

        stop = cfg.get('stop')
        if do_mix is True:
            rope_tables()
        last_even = DEPTH - 1 - (DEPTH - 1) % 2
        for l in range(DEPTH if stop != 'rope' else 0):
            ctx_in = l <= last_even
            ctx_out = l < last_even
            esA = scope_a()
            ffn_phase(l, 0, ctx_in)
            if l % 2 == 1 and do_mix:
                sconv_phase(l, ctx_out)
            elif l % 2 == 0 and do_mix is True:
                esC, sbc = mk_scope()
                mixer_consts(l // 2, sbc)
                if stop != 'ffn':
                    e1_phase(l)
                end_scope(esC)
                end_scope(esA)
                if stop in ('e1', 'ffn'):
                    return nc
                scan_phases(l)
                if stop in ('scan', 'sc_const', 'e2', 'e3ret', 'e3z', 'e3g', 'e3post'):
                    return nc
                esA = scope_a()
                e4_phase(l, ctx_out)
            ffn_phase(l, 1, ctx_out)
            end_scope(esA)

        esA = scope_a()
        tk.dma('act', GS[:], final_g.partition_broadcast(128), writes=['GS'])
        for t in range(NCT, NT):
            j = t % TB
            norm_tile(j, t * 128, modulate=False)
            tk.dma('pool', out[(t - NCT) * 128:(t - NCT + 1) * 128, :], SCR[:], reads=['SCR'], writes=['out'])
        tk.barrier()
        esA.close()
    return nc


def prep_inputs(cfg, b, x, c, ctx, c_ctx, ada_w, ada_b, norm_g, final_g, ffn_up, ffn_down, **kw):
    DEPTH, DFF, SEQ, CTX = cfg['depth'], cfg['dff'], cfg['seq'], cfg['ctx']
    NH = DFF // 128
    T = SEQ + CTX
    f32 = np.float32
    xin = np.concatenate([ctx[b], x[b]], axis=0)
    cc = np.stack([c[b], c_ctx], axis=0)
    cT = np.ascontiguousarray(cc.reshape(2, KC, 128).transpose(2, 1, 0))
    up = ffn_up[:DEPTH].reshape(DEPTH, 2, KC, 128, 2, NH, 128)
    wup = np.ascontiguousarray(up.transpose(0, 1, 5, 3, 2, 4, 6)).reshape(DEPTH, 2, NH, 128, KC * 256)
    dn = ffn_down[:DEPTH].reshape(DEPTH, 2, NH, 128, D // 256, 256)
    wdn = np.ascontiguousarray(dn.transpose(0, 1, 4, 3, 2, 5)).reshape(DEPTH, 2, D // 256, 128, NH * 256)
    N_ODD = max(DEPTH // 2, 1)
    N_EVEN = (DEPTH + 1) // 2
    sc_w_in, sc_conv_w, sc_w_out = kw['sc_w_in'], kw['sc_conv_w'], kw['sc_w_out']
    wi = sc_w_in[:N_ODD].reshape(N_ODD, KC, 128, 3, KC, 128)
    scw_cu = np.ascontiguousarray(wi[:, :, :, 1:3].transpose(0, 4, 2, 1, 3, 5)).reshape(N_ODD, KC, 128, KC * 256)
    scw_b = np.ascontiguousarray(wi[:, :, :, 0].transpose(0, 3, 2, 1, 4)).reshape(N_ODD, KC, 128, KC * 128)
    wo = sc_w_out[:N_ODD].reshape(N_ODD, KC, 128, D // 256, 256)
    scwo = np.ascontiguousarray(wo.transpose(0, 3, 2, 1, 4)).reshape(N_ODD, D // 256, 128, KC * 256)
    sccw = np.ascontiguousarray(sc_conv_w[:N_ODD].reshape(N_ODD, 3, KC, 128).transpose(0, 3, 2, 1))
    mw = kw['mix_w_in'][:N_EVEN]

    def fm_slot(cols):
        w = mw[:, :, cols].reshape(N_EVEN, KC, 128, 256)
        return w.transpose(0, 2, 1, 3).reshape(N_EVEN, 128, KC * 256)
    slots = []
    for base in (0, 1024):
        for pr in range(4):
            h0, h1 = 2 * pr, 2 * pr + 1
            cols = np.concatenate([base + h0 * 128 + np.arange(64), base + h1 * 128 + np.arange(64),
                                   base + h0 * 128 + 64 + np.arange(64), base + h1 * 128 + 64 + np.arange(64)])
            slots.append(fm_slot(cols))
    for s in range(6):
        slots.append(fm_slot(5120 + s * 256 + np.arange(256)))
    wfm = np.ascontiguousarray(np.stack(slots, axis=1))
    tms = []
    for blk in range(6):
        w = mw[:, :, 2048 + blk * 512:2048 + (blk + 1) * 512].reshape(N_EVEN, KC, 128, 512)
        tms.append(w.transpose(0, 2, 1, 3).reshape(N_EVEN, 128, KC * 512))
    wtm = np.ascontiguousarray(np.stack(tms, axis=1))
    wdt = np.ascontiguousarray(mw[:, :, 6656:6688].reshape(N_EVEN, KC, 128, 32).transpose(0, 2, 1, 3)).reshape(N_EVEN, 128, KC * 32)
    mo = kw['mix_w_out'][:N_EVEN].reshape(N_EVEN, KC, 128, D // 256, 256)
    mwo = np.ascontiguousarray(mo.transpose(0, 3, 2, 1, 4)).reshape(N_EVEN, D // 256, 128, KC * 256)
    mcw = np.ascontiguousarray(kw['mix_conv_w'][:N_EVEN].reshape(N_EVEN, 3, 12, 128).transpose(0, 3, 2, 1))
    mcb = np.ascontiguousarray(kw['mix_conv_b'][:N_EVEN].reshape(N_EVEN, 12, 128).transpose(0, 2, 1))
    tl = np.arange(SEQ)
    pos = np.zeros((T, 3), f32)
    pos[:CTX, 0] = np.arange(CTX)
    pos[CTX:, 0] = CTX
    pos[CTX:, 1] = tl // 64
    pos[CTX:, 2] = tl % 64
    axis = np.concatenate([np.zeros(16, int), np.ones(24, int), 2 * np.ones(24, int)])
    inv = np.concatenate([10000.0 ** (-np.arange(n, dtype=f32) / n) for n in (16, 24, 24)]).astype(f32)
    pax64 = pos[:, axis].T
    pax = np.ascontiguousarray(np.concatenate([pax64, pax64], axis=0)).astype(f32)
    inv128 = np.concatenate([inv, inv]).reshape(128, 1).astype(f32)
    iota = (np.arange(128)[None, :] - np.arange(128)[:, None]).astype(f32)
    return {
        "scw_cu": scw_cu, "scw_b": scw_b, "scwo": scwo, "sccw": sccw,
        "wfm": wfm, "wtm": wtm, "wdt": wdt, "mwo": mwo, "mcw": mcw, "mcb": mcb,
        "rld": np.ascontiguousarray(kw['ret_log_decay'][:N_EVEN].reshape(N_EVEN, 1, 16)),
        "gng": np.ascontiguousarray(kw['ret_gn_g'][:N_EVEN].reshape(N_EVEN, 1, 1024)),
        "alog": np.ascontiguousarray(kw['ssd_a_log'][:N_EVEN].reshape(N_EVEN, 1, 32)),
        "dtb": np.ascontiguousarray(kw['ssd_dt_bias'][:N_EVEN].reshape(N_EVEN, 1, 32)),
        "ssdd": np.ascontiguousarray(kw['ssd_d'][:N_EVEN].reshape(N_EVEN, 1, 16)),
        "sng": np.ascontiguousarray(kw['ssd_norm_g'][:N_EVEN].reshape(N_EVEN, 1, 1024)),
        "pax": pax, "inv": inv128, "iota_in": iota,
        "xin": np.ascontiguousarray(xin), "cT": cT,
        "ada_w": np.ascontiguousarray(ada_w[:DEPTH]), "ada_b": np.ascontiguousarray(ada_b[:DEPTH]),
        "norm_g": np.ascontiguousarray(norm_g[:DEPTH]), "final_g": np.ascontiguousarray(final_g.reshape(1, D)),
        "wup": wup, "wdn": wdn, "ident_in": np.eye(128, dtype=np.float32),
    }


def run(cfg, inputs, full=False):
    nc = build(cfg)
    nb = inputs['x'].shape[0]
    in_maps = [prep_inputs(cfg, b, **inputs) for b in range(nb)]
    res = run_bass_kernel_spmd(nc, in_maps, core_ids=list(range(nb)))
    if full:
        return res.results
    return np.stack([r["out"] for r in res.results], axis=0)


def kernel(**inputs):
    inputs = {k: np.asarray(v) for k, v in inputs.items()}
    cfg = dict(depth=4, seq=4096, ctx=256, dff=5632)
    return run(cfg, inputs).astype(np.float32)
```
